# Optimizing a Trainium2 kernel written in Bass

```python
import math
import jax, jax.numpy as jnp
from jax import lax
import numpy as np

D_MODEL = 1024
BATCH = 8
SEQ = 2048
DEPTH = 2

N_MEM = 256
D_MIX = D_MODEL
D_ATTN = D_MIX // 2
D_CONV = D_MIX - D_ATTN
HEAD_DIM = 64
N_ATTN_HEADS = D_ATTN // HEAD_DIM
CONV_WIDTH = 31
Q_BLOCK = 128
N_XATTN_HEADS = 4
XATTN_HEAD_DIM = D_MODEL // N_XATTN_HEADS
D_FF = 2816
D_IN = 3 * D_ATTN + N_ATTN_HEADS + 2 * D_CONV
EPS = 1e-6
NEG_INF = -1e30

kernel_name = "fox_conformer_macaron_hybrid"


def rmsnorm(x, g):
    xf = x.astype(jnp.float32)
    y = xf * lax.rsqrt(jnp.mean(xf * xf, axis=-1, keepdims=True) + EPS)
    return (y * g.astype(jnp.float32)).astype(x.dtype)


def layernorm(x, g, b):
    xf = x.astype(jnp.float32)
    mu = jnp.mean(xf, axis=-1, keepdims=True)
    xc = xf - mu
    y = xc * lax.rsqrt(jnp.mean(xc * xc, axis=-1, keepdims=True) + EPS)
    return (y * g.astype(jnp.float32) + b.astype(jnp.float32)).astype(x.dtype)


def swiglu(h, w_gate, w_up, w_down):
    return (jax.nn.silu(h @ w_gate) * (h @ w_up)) @ w_down


def fox_attention(q, k, v, logf):
    B, S, H, Dh = q.shape
    scale = 1.0 / math.sqrt(Dh)
    c = jnp.transpose(jnp.cumsum(logf, axis=1), (0, 2, 1))
    outs = []
    for i in range(S // Q_BLOCK):
        q0, q1 = i * Q_BLOCK, (i + 1) * Q_BLOCK
        qb = q[:, q0:q1]
        kb = k[:, :q1]
        vb = v[:, :q1]
        s = jnp.einsum('bqhd,bkhd->bhqk', qb, kb).astype(jnp.float32) * scale
        s = s + c[:, :, q0:q1, None] - c[:, :, None, :q1]
        mask = (q0 + jnp.arange(Q_BLOCK))[:, None] >= jnp.arange(q1)[None, :]
        s = jnp.where(mask[None, None], s, NEG_INF)
        p = jax.nn.softmax(s, axis=-1).astype(v.dtype)
        outs.append(jnp.einsum('bhqk,bkhd->bqhd', p, vb))
    return jnp.concatenate(outs, axis=1)


def causal_depthwise_conv(u, w, b):
    C = u.shape[-1]
    kern = w.astype(u.dtype)[:, None, :]
    y = lax.conv_general_dilated(
        u, kern, window_strides=(1,), padding=[(CONV_WIDTH - 1, 0)],
        dimension_numbers=('NWC', 'WIO', 'NWC'), feature_group_count=C)
    return y + b.astype(u.dtype)


def hybrid_mix(h, w_in, b_f, conv_w, conv_b, ln_g, ln_b, attn_g, conv_g, w_out):
    B, S, _ = h.shape
    proj = h @ w_in
    splits = [D_ATTN, 2 * D_ATTN, 3 * D_ATTN, 3 * D_ATTN + N_ATTN_HEADS,
              3 * D_ATTN + N_ATTN_HEADS + D_CONV]
    q, k, v, f_logit, a, g = jnp.split(proj, splits, axis=-1)
    q = q.reshape(B, S, N_ATTN_HEADS, HEAD_DIM)
    k = k.reshape(B, S, N_ATTN_HEADS, HEAD_DIM)
    v = v.reshape(B, S, N_ATTN_HEADS, HEAD_DIM)
    logf = jax.nn.log_sigmoid((f_logit + b_f).astype(jnp.float32))
    attn = fox_attention(q, k, v, logf).reshape(B, S, D_ATTN)
    u = a * jax.nn.sigmoid(g)
    u = causal_depthwise_conv(u, conv_w, conv_b)
    u = jax.nn.silu(layernorm(u, ln_g, ln_b))
    y = jnp.concatenate([rmsnorm(attn, attn_g), rmsnorm(u, conv_g)], axis=-1)
    return y @ w_out


def memory_cross_attention(h, mem_n, w_q, w_kv, w_o):
    B, S, _ = h.shape
    q = (h @ w_q).reshape(B, S, N_XATTN_HEADS, XATTN_HEAD_DIM)
    kv = mem_n @ w_kv
    k, v = jnp.split(kv, 2, axis=-1)
    k = k.reshape(B, -1, N_XATTN_HEADS, XATTN_HEAD_DIM)
    v = v.reshape(B, -1, N_XATTN_HEADS, XATTN_HEAD_DIM)
    s = jnp.einsum('bqhd,bmhd->bhqm', q, k).astype(jnp.float32) / math.sqrt(XATTN_HEAD_DIM)
    p = jax.nn.softmax(s, axis=-1).astype(v.dtype)
    o = jnp.einsum('bhqm,bmhd->bqhd', p, v).reshape(B, S, D_MODEL)
    return o @ w_o


def setup_inputs(seed: int = 0) -> dict:
    key = jax.random.key(seed)
    ks = iter(jax.random.split(key, 32))
    L = DEPTH

    def w(shape, fan_in):
        return jax.random.normal(next(ks), shape, jnp.float32) * (fan_in ** -0.5)

    def gain(shape):
        return 1.0 + 0.02 * jax.random.normal(next(ks), shape, jnp.float32)

    def bias(shape):
        return 0.02 * jax.random.normal(next(ks), shape, jnp.float32)

    return {
        "x": jax.random.normal(next(ks), (BATCH, SEQ, D_MODEL), jnp.float32),
        "mem": jax.random.normal(next(ks), (BATCH, N_MEM, D_MODEL), jnp.float32),
        "ffn1_norm_g": gain((L, D_MODEL)),
        "ffn1_w_gate": w((L, D_MODEL, D_FF), D_MODEL),
        "ffn1_w_up": w((L, D_MODEL, D_FF), D_MODEL),
        "ffn1_w_down": w((L, D_FF, D_MODEL), D_FF),
        "mix_norm_g": gain((L, D_MODEL)),
        "w_in": w((L, D_MODEL, D_IN), D_MODEL),
        "b_f": jax.random.uniform(next(ks), (L, N_ATTN_HEADS), jnp.float32, 1.0, 6.0),
        "conv_w": w((L, CONV_WIDTH, D_CONV), CONV_WIDTH),
        "conv_b": bias((L, D_CONV)),
        "conv_ln_g": gain((L, D_CONV)),
        "conv_ln_b": bias((L, D_CONV)),
        "attn_out_g": gain((L, D_ATTN)),
        "conv_out_g": gain((L, D_CONV)),
        "w_out": w((L, D_MIX, D_MODEL), D_MIX),
        "xattn_norm_g": gain((L, D_MODEL)),
        "mem_norm_g": gain((L, D_MODEL)),
        "xattn_w_q": w((L, D_MODEL, D_MODEL), D_MODEL),
        "xattn_w_kv": w((L, D_MODEL, 2 * D_MODEL), D_MODEL),
        "xattn_w_o": w((L, D_MODEL, D_MODEL), D_MODEL),
        "ffn2_norm_g": gain((L, D_MODEL)),
        "ffn2_w_gate": w((L, D_MODEL, D_FF), D_MODEL),
        "ffn2_w_up": w((L, D_MODEL, D_FF), D_MODEL),
        "ffn2_w_down": w((L, D_FF, D_MODEL), D_FF),
        "final_norm_g": gain((D_MODEL,)),
    }


def reference(x, mem, ffn1_norm_g, ffn1_w_gate, ffn1_w_up, ffn1_w_down, mix_norm_g, w_in, b_f,
              conv_w, conv_b, conv_ln_g, conv_ln_b, attn_out_g, conv_out_g, w_out,
              xattn_norm_g, mem_norm_g, xattn_w_q, xattn_w_kv, xattn_w_o,
              ffn2_norm_g, ffn2_w_gate, ffn2_w_up, ffn2_w_down, final_norm_g):
    for l in range(DEPTH):
        x = x + 0.5 * swiglu(rmsnorm(x, ffn1_norm_g[l]), ffn1_w_gate[l], ffn1_w_up[l], ffn1_w_down[l])
        x = x + hybrid_mix(rmsnorm(x, mix_norm_g[l]), w_in[l], b_f[l], conv_w[l], conv_b[l],
                           conv_ln_g[l], conv_ln_b[l], attn_out_g[l], conv_out_g[l], w_out[l])
        x = x + memory_cross_attention(rmsnorm(x, xattn_norm_g[l]), rmsnorm(mem, mem_norm_g[l]),
                                       xattn_w_q[l], xattn_w_kv[l], xattn_w_o[l])
        x = x + 0.5 * swiglu(rmsnorm(x, ffn2_norm_g[l]), ffn2_w_gate[l], ffn2_w_up[l], ffn2_w_down[l])
    return rmsnorm(x, final_norm_g)
```

```python
import os
from contextlib import ExitStack
import numpy as np
import concourse.bass as bass
import concourse.mybir as mybir
from concourse.bass_utils import run_bass_kernel_spmd

F32 = mybir.dt.float32
BF16 = mybir.dt.bfloat16
AF = mybir.ActivationFunctionType
ALU = mybir.AluOpType

S = 2048
D = 1024
DFF = 2816
NMEM = 256
L = 2
EPS = 1e-6
NSLOT = 8
SLOT = 2048
NT = 4
TT = 512
NCST = 128 + 128 + 64
MASKNEG = -30000.0

_PVL = [("ffn1_g", 8), ("mix_g", 8), ("xattn_g", 8), ("mem_g", 8), ("ffn2_g", 8), ("conv_w2", 128),
        ("conv_b", 4), ("ln_g", 4), ("ln_b", 4), ("conv_out_g", 4), ("attn_out_g", 4), ("b_f", 1)]
_PVO = {}
_o = 0
for _n, _w in _PVL:
    _PVO[_n] = _o
    _o += _w
PL = _o
NPV = L * PL + 8


def pvcol(l, name):
    return l * PL + _PVO[name]


FINAL_G = L * PL


FFN_NAMES = {1: ("ffn1_w_gate", "ffn1_w_up", "ffn1_w_down"), 2: ("ffn2_w_gate", "ffn2_w_up", "ffn2_w_down")}


class Buf:
    __slots__ = ("w", "r", "excl")

    def __init__(self, excl=False):
        self.w = None
        self.r = {}
        self.excl = excl


class DSem:
    def __init__(self, sem):
        self.sem = sem
        self.n = 0


class Eng:
    def __init__(self, name, h, sem):
        self.name = name
        self.h = h
        self.sem = sem
        self.n = 0
        self.known = {}


class PEProxy:
    def __init__(self, h, wait):
        self.h = h
        self.wait = wait

    def _first(self, ins):
        if self.wait is not None:
            ins._wait_ge(self.wait[0], self.wait[1])
            self.wait = None
        return ins

    def matmul(self, *a, **k):
        return self._first(self.h.matmul(*a, **k))

    def transpose(self, *a, **k):
        return self._first(self.h.transpose(*a, **k))


class Sched:
    def __init__(self, nc, stack):
        self.nc = nc
        self.stack = stack
        self.eng = {}
        for name, h in (("pe", nc.tensor), ("act", nc.scalar), ("dve", nc.vector),
                        ("pool", nc.gpsimd), ("sp", nc.sync)):
            self.eng[name] = Eng(name, h, stack.enter_context(nc.semaphore("s_" + name)))
        self.nsem = 0
        self.snaps = {}

    def dsem(self):
        self.nsem += 1
        return DSem(self.stack.enter_context(self.nc.semaphore("d%d" % self.nsem)))

    def _waits(self, en, reads, writes, attach_ok=False):
        e = self.eng[en]
        deps = {}

        def need(tok):
            if tok is None:
                return
            k = id(tok[0])
            if k not in deps or deps[k][1] < tok[1]:
                deps[k] = tok

        for b in reads:
            need(b.w)
            if b.excl:
                for t in b.r.values():
                    need(t)
        for b in writes:
            need(b.w)
            for t in b.r.values():
                need(t)
        todo = []
        for k, (sem, val) in deps.items():
            if en == "pe" and sem is e.sem:
                continue
            if e.known.get(k, 0) >= val:
                continue
            e.known[k] = val
            todo.append((sem, val))
            snap = self.snaps.get((k, val))
            if snap is not None:
                for k2, v2 in snap.items():
                    if e.known.get(k2, 0) < v2:
                        e.known[k2] = v2
        self.attach = None
        if en in ("pe", "act", "dve") and attach_ok and todo:
            self.attach = todo.pop()
        for sem, val in todo:
            e.h.wait_ge(sem, val)
        return e

    def _mark(self, tok, reads, writes):
        for b in reads:
            if b.excl:
                b.w = tok
                b.r = {}
            else:
                b.r[id(tok[0])] = tok
        for b in writes:
            b.w = tok
            b.r = {}

    def op(self, en, fn, reads=(), writes=()):
        e = self._waits(en, reads, writes, attach_ok=True)
        if en == "pe":
            ins = fn(PEProxy(e.h, self.attach))
        else:
            ins = fn(e.h)
            if self.attach is not None:
                ins._wait_ge(self.attach[0], self.attach[1])
        e.n += 1
        ins.then_inc(e.sem, 1)
        self.snaps[(id(e.sem), e.n)] = dict(e.known)
        self._mark((e.sem, e.n), reads, writes)

    def dma(self, qn, ds, out_ap, in_ap, reads=(), writes=()):
        e = self._waits(qn, reads, writes)
        e.h.dma_start(out=out_ap, in_=in_ap).then_inc(ds.sem, 16)
        ds.n += 16
        self._mark((ds.sem, ds.n), reads, writes)


class BankPool:
    def __init__(self, items):
        self.items = items
        self.i = 0

    def next(self):
        it = self.items[self.i % len(self.items)]
        self.i += 1
        return it


def n_units():
    per_layer = 33 * 2 + 8 + 1 + 4 + 4 + 16
    return per_layer * L


NU = n_units()


def build(stages=None):
    plan = []
    nc = bass.Bass("TRN2", target_bir_lowering=False)
    xT_d = nc.dram_tensor("xT", [D, S], F32, kind="ExternalInput").ap()
    memT_d = nc.dram_tensor("memT", [D, NMEM], F32, kind="ExternalInput").ap()
    wst_d = nc.dram_tensor("wst", [NU, 128, SLOT], F32, kind="ExternalInput").ap()
    pv_d = nc.dram_tensor("pvec", [128, NPV], F32, kind="ExternalInput").ap()
    cst_d = nc.dram_tensor("cst", [128, NCST], F32, kind="ExternalInput").ap()
    oT_d = nc.dram_tensor("oT", [D, S], F32, kind="ExternalOutput").ap()
    xT_v = xT_d.rearrange("(c p) t -> p c t", p=128)
    memT_v = memT_d.rearrange("(c p) t -> p c t", p=128)
    oT_v = oT_d.rearrange("(c p) t -> p c t", p=128)

    with ExitStack() as st:
        sc = Sched(nc, st)
        _uid = [0]

        def uniq(name):
            _uid[0] += 1
            return "%s_%d" % (name, _uid[0])
        T = lambda name, shape, dt: st.enter_context(nc.sbuf_tensor(name, shape, dt))
        X = T("X", [128, 8, S], F32)
        H = T("H", [128, 8, S], BF16)
        RING = T("RING", [128, NSLOT, SLOT], BF16)
        PV = T("PV", [128, NPV], F32)
        CSTB = T("CSTB", [128, NCST], BF16)
        ID8 = T("ID8", [8, 8], F32)
        ONESM = T("ONESM", [128, 128], BF16)
        ONESM5 = T("ONESM5", [128, 128], BF16)
        ONES = T("ONES", [128, 128], BF16)
        ONESF5 = T("ONESF5", [128, 128], F32)
        EPSC = T("EPSC", [128, 1], F32)
        SQ = T("SQ", [128, 8, TT], BF16)
        RS = [T("RS%d" % i, [128, TT], F32) for i in range(4)]
        MN = T("MN", [128, 8, NMEM], BF16)
        PS = [st.enter_context(nc.psum_tensor("ps%d" % i, [128, TT], F32)) for i in range(8)]
        PSb = [Buf(excl=True) for _ in range(8)]
        bank = lambda i: (PS[i], PSb[i])

        Xb = [[Buf() for _ in range(NT)] for _ in range(8)]
        Hb = [[Buf() for _ in range(NT)] for _ in range(8)]
        SQb = Buf()
        RSb = [Buf() for _ in range(4)]
        mnb = [Buf() for _ in range(8)]
        cb = Buf()
        rs_rr = BankPool([0, 1, 2, 3])
        SG, SGb, sg_rr = RS, RSb, rs_rr
        ringb = [Buf() for _ in range(NSLOT)]
        ringd = [sc.dsem() for _ in range(NSLOT)]
        ring_i = [0]

        def wload(desc, nelem):
            u = len(plan)
            plan.append(desc)
            k = ring_i[0] % NSLOT
            ring_i[0] += 1
            if u == 3:
                nc.gpsimd.wait_ge(d_xs[NT - 1].sem, d_xs[NT - 1].n)
            if u == 6:
                sc.dma("pool", d_cst, CSTB[:, :], cst_d[:, :], writes=[cb2])
            sc.dma("pool", ringd[k], RING[:, k, 0:nelem], wst_d[u, :, 0:nelem], writes=[ringb[k]])
            return RING, k, ringb[k]

        def ring_take():
            k = ring_i[0] % NSLOT
            ring_i[0] += 1
            return k, ringb[k]

        def tsl(tt):
            return slice(tt * TT, (tt + 1) * TT)

        def fresh(*groups):
            toks = {}
            for fn_ in ("pe", "act", "dve", "pool"):
                f = sc.eng[fn_]
                if f.n > 0:
                    toks[id(f.sem)] = (f.sem, f.n)

            def walk(x):
                if isinstance(x, Buf):
                    x.r = dict(toks)
                else:
                    for y in x:
                        walk(y)
            walk(groups)

        def barrier():
            for en in ("pe", "act", "dve", "sp"):
                e = sc.eng[en]
                for fn_ in ("pe", "act", "dve"):
                    f = sc.eng[fn_]
                    if f is e or f.n == 0 or e.known.get(id(f.sem), 0) >= f.n:
                        continue
                    e.known[id(f.sem)] = f.n
                    e.h.wait_ge(f.sem, f.n)

        d_in = sc.dsem()
        sc.dma("sp", d_in, PV[:, :], pv_d[:, :], writes=[cb])
        sc.dma("sp", d_in, ID8[:, :], cst_d[0:8, 0:8], writes=[cb])
        cb2 = Buf()
        d_cst = sc.dsem()
        d_xs = [sc.dsem() for _ in range(NT)]
        for tt in range(NT):
            sc.dma("sp", d_xs[tt], X[:, :, tt * TT:(tt + 1) * TT], xT_v[:, :, tt * TT:(tt + 1) * TT], writes=[Xb[c][tt] for c in range(8)])
        sc.op("dve", lambda v: v.memset(ONESM[:, :], 1.0 / 1024.0), writes=[cb])
        sc.op("dve", lambda v: v.memset(ONESM5[:, :], 1.0 / 512.0), writes=[cb])
        sc.op("dve", lambda v: v.memset(ONES[:, :], 1.0), writes=[cb])
        sc.op("dve", lambda v: v.memset(ONESF5[:, :], 1.0 / 512.0), writes=[cb])
        sc.op("dve", lambda v: v.memset(EPSC[:, :], EPS), writes=[cb])
        for en in ("pe", "act", "dve", "sp"):
            e = sc.eng[en]
            e.h.wait_ge(d_in.sem, d_in.n)
            e.h.wait_ge(sc.eng["dve"].sem, sc.eng["dve"].n)
            e.known[id(d_in.sem)] = d_in.n
            e.known[id(sc.eng["dve"].sem)] = sc.eng["dve"].n
        cb.w = None
        for c in range(8):
            for tt in range(NT):
                Xb[c][tt].w = (d_xs[tt].sem, d_xs[tt].n)
        IDB = CSTB[:, 0:128]
        maskneg = lambda d: CSTB[:, 128:256]
        EID = CSTB[:, 256:320]

        pStat = BankPool([7])

        def rms_parts(src, srcb, dst, dstb, nch, ncol, gcol, onesm):
            st_ = {}

            def p1():
                sc.op("act", lambda a: a.activation(out=SQ[:, 0:nch, 0:ncol], in_=src(None), func=AF.Square),
                      reads=srcb + [cb], writes=[SQb])

            def p2():
                ps, psb = bank(pStat.next())

                def mm(t):
                    for c in range(nch):
                        ins = t.matmul(ps[:, 0:ncol], onesm[:, :], SQ[:, c, 0:ncol], start=(c == 0), stop=(c == nch - 1))
                    return ins
                sc.op("pe", mm, reads=[SQb, cb], writes=[psb])
                i = rs_rr.next()
                st_["i"] = i
                sc.op("act", lambda a: a.activation(out=RS[i][:, 0:ncol], in_=ps[:, 0:ncol], func=AF.Ln, bias=EPSC[:, 0:1]),
                      reads=[psb, cb], writes=[RSb[i]])
                sc.op("act", lambda a: a.activation(out=RS[i][:, 0:ncol], in_=RS[i][:, 0:ncol], func=AF.Exp, scale=-0.5),
                      reads=[RSb[i]], writes=[RSb[i]])

            def p3():
                i = st_["i"]
                for c in range(nch):
                    sc.op("dve", lambda v, c=c: v.scalar_tensor_tensor(out=dst(c), in0=src(c), scalar=PV[:, gcol + c:gcol + c + 1],
                                                                        in1=RS[i][:, 0:ncol], op0=ALU.mult, op1=ALU.mult),
                          reads=[srcb[c], RSb[i], cb], writes=[dstb[c]])
            return [p1, p2, p3]

        def rms_tile(*a):
            for p in rms_parts(*a):
                p()

        def rms_X_tile(tt, gcol):
            rms_tile(lambda c: X[:, :, tsl(tt)] if c is None else X[:, c, tsl(tt)],
                     [Xb[c][tt] for c in range(8)],
                     lambda c: H[:, c, tsl(tt)], [Hb[c][tt] for c in range(8)], 8, TT, gcol, ONESM)

        def rmsnorm_XH(gcol):
            for tt in range(NT):
                rms_X_tile(tt, gcol)

        def ffn(l, which, post, mem_prologue=False):
            n_gate, n_up, n_down = FFN_NAMES[which]
            pG = BankPool([0, 1, 2, 3])
            pD = BankPool([4, 5, 6])
            with ExitStack() as sf:
                ACTT = sf.enter_context(nc.sbuf_tensor(uniq("ACTT"), [128, 6, S], BF16))
                actb = [[Buf() for _ in range(NT)] for _ in range(6)]
                fresh(actb)
                if mem_prologue:
                    MEMT = sf.enter_context(nc.sbuf_tensor(uniq("MEMT"), [128, 8, NMEM], F32))
                    memb = [Buf() for _ in range(8)]
                    fresh(memb)
                    dm = sc.dsem()
                    sc.dma("sp", dm, MEMT[:, :, :], memT_v[:, :, :], writes=memb)
                groups = [(0, 1), (2, 3, 4), (5, 6, 7), (8, 9, 10)]
                for grp in groups:
                    if mem_prologue and grp is groups[1]:
                        rms_tile(lambda c: MEMT[:, :, :] if c is None else MEMT[:, c, :], memb,
                                 lambda c: MN[:, c, :], mnb, 8, NMEM, pvcol(l, "mem_g"), ONESM)
                    sl = {}
                    for u in grp:
                        sl["g", u] = wload(("kxn", n_gate, l, u * 256, 256), 2048)
                        sl["u", u] = wload(("kxn", n_up, l, u * 256, 256), 2048)
                    for ui, u in enumerate(grp):
                        for j in range(2):
                            hl = ui * 2 + j
                            for tt in range(NT):
                                gps, gpb = bank(pG.next())
                                ups, upb = bank(pG.next())
                                for (ps, pb, key) in ((gps, gpb, "g"), (ups, upb, "u")):
                                    _, k_, rb = sl[key, u]

                                    def mm(t, ps=ps, k_=k_, j=j, tt=tt):
                                        for k in range(8):
                                            ins = t.matmul(ps[:, :], RING[:, k_, k * 256 + j * 128:k * 256 + j * 128 + 128],
                                                           H[:, k, tsl(tt)], start=(k == 0), stop=(k == 7))
                                        return ins
                                    sc.op("pe", mm, reads=[rb] + [Hb[k][tt] for k in range(8)], writes=[pb])
                                i = sg_rr.next()
                                sc.op("act", lambda a, i=i, gps=gps: a.activation(out=SG[i][:, :], in_=gps[:, :], func=AF.Silu),
                                      reads=[gpb], writes=[SGb[i]])
                                sc.op("dve", lambda v, i=i, ups=ups, hl=hl, tt=tt: v.tensor_tensor(
                                    out=ACTT[:, hl, tsl(tt)], in0=SG[i][:, :], in1=ups[:, :], op=ALU.mult),
                                    reads=[SGb[i], upb], writes=[actb[hl][tt]])
                    for u in grp:
                        sl["d", u] = wload(("rows", n_down, l, u * 256, 256), 2048)
                    nh = 2 * len(grp)
                    for tt in range(NT):
                        for oc in range(8):
                            dps, dpb = bank(pD.next())

                            def mm(t, dps=dps, tt=tt, oc=oc):
                                for hl in range(nh):
                                    k_ = sl["d", grp[hl // 2]][1]
                                    j = hl % 2
                                    ins = t.matmul(dps[:, :], RING[:, k_, j * 1024 + oc * 128:j * 1024 + oc * 128 + 128],
                                                   ACTT[:, hl, tsl(tt)], start=(hl == 0), stop=(hl == nh - 1))
                                return ins
                            sc.op("pe", mm, reads=[sl["d", u][2] for u in grp] + [actb[hl][tt] for hl in range(nh)], writes=[dpb])
                            sc.op("dve", lambda v, dps=dps, oc=oc, tt=tt: v.scalar_tensor_tensor(
                                out=X[:, oc, tsl(tt)], in0=dps[:, :], scalar=0.5, in1=X[:, oc, tsl(tt)], op0=ALU.mult, op1=ALU.add),
                                reads=[dpb], writes=[Xb[oc][tt]])
                        if grp is groups[-1]:
                            if tt > 0:
                                post(tt - 1)
                            if tt == NT - 1:
                                post(tt)

        def xattn(l, post):
            pA = BankPool([0, 1, 2])
            pB = BankPool([3, 4, 5, 6])
            pC = BankPool([7])
            with ExitStack() as s2:
                T2 = lambda name, shape, dt: s2.enter_context(nc.sbuf_tensor(uniq(name), shape, dt))
                KT = T2("KT", [128, 8, NMEM], BF16)
                VM = T2("VM", [128, 2, D], BF16)
                ktb = [Buf() for _ in range(8)]
                vmb = Buf()
                QT = [T2("QT%d" % i, [128, 8, TT], BF16) for i in range(2)]
                OT = [T2("OT%d" % i, [128, 8, TT], BF16) for i in range(2)]
                PT = [T2("XPT%d" % i, [128, TT], BF16) for i in range(4)]
                REC = [T2("XREC%d" % i, [128, TT], F32) for i in range(2)]
                qtb = [[Buf() for _ in range(8)] for _ in range(2)]
                otb = [[Buf() for _ in range(8)] for _ in range(2)]
                ptb = [Buf() for _ in range(4)]
                recb = [Buf() for _ in range(2)]
                fresh(ktb, vmb, qtb, otb, ptb, recb)
                for oc2 in range(4):
                    _, k_, rb = wload(("kxn", "xattn_w_kv", l, oc2 * 256, 256), 2048)
                    for jj in range(2):
                        oc = oc2 * 2 + jj
                        ps, pb = bank(pA.next())

                        def mm(t, ps=ps, k_=k_, jj=jj):
                            for k in range(8):
                                ins = t.matmul(ps[:, 0:NMEM], RING[:, k_, k * 256 + jj * 128:k * 256 + jj * 128 + 128],
                                               MN[:, k, :], start=(k == 0), stop=(k == 7))
                            return ins
                        sc.op("pe", mm, reads=[rb] + mnb, writes=[pb])
                        sc.op("act", lambda a, ps=ps, oc=oc: a.activation(out=KT[:, oc, :], in_=ps[:, 0:NMEM], func=AF.Copy),
                              reads=[pb], writes=[ktb[oc]])
                for u in range(4):
                    _, k_, rb = wload(("kxn", "xattn_w_kv", l, 1024 + u * 256, 256), 2048)
                    for mt in range(2):
                        ps, pb = bank(pA.next())

                        def mm(t, ps=ps, k_=k_, mt=mt):
                            for k in range(8):
                                ins = t.matmul(ps[:, 0:256], MN[:, k, mt * 128:(mt + 1) * 128],
                                               RING[:, k_, k * 256:(k + 1) * 256], start=(k == 0), stop=(k == 7))
                            return ins
                        sc.op("pe", mm, reads=[rb] + mnb, writes=[pb])
                        sc.op("dve", lambda v, ps=ps, mt=mt, u=u: v.tensor_copy(out=VM[:, mt, u * 256:(u + 1) * 256], in_=ps[:, 0:256]),
                              reads=[pb], writes=[vmb])
                wq = [wload(("kxn", "xattn_w_q", l, oc2 * 256, 256), 2048) for oc2 in range(4)]
                wo = [wload(("kxn", "xattn_w_o", l, oc2 * 256, 256), 2048) for oc2 in range(4)]

                def stageA(tt):
                    q = tt % 2
                    for oc in range(8):
                        ps, pb = bank(pA.next())
                        _, k_, rb = wq[oc // 2]

                        def mm(t):
                            for k in range(8):
                                ins = t.matmul(ps[:, :], RING[:, k_, k * 256 + (oc % 2) * 128:k * 256 + (oc % 2) * 128 + 128],
                                               H[:, k, tsl(tt)], start=(k == 0), stop=(k == 7))
                            return ins
                        sc.op("pe", mm, reads=[rb] + [Hb[k][tt] for k in range(8)], writes=[pb])
                        if oc % 2 == 0:
                            sc.op("act", lambda a: a.activation(out=QT[q][:, oc, :], in_=ps[:, :], func=AF.Copy),
                                  reads=[pb], writes=[qtb[q][oc]])
                        else:
                            sc.op("dve", lambda v: v.tensor_copy(out=QT[q][:, oc, :], in_=ps[:, :]),
                                  reads=[pb], writes=[qtb[q][oc]])

                def stageB(tt):
                    q = tt % 2

                    def s_part(h):
                        for mt in range(2):
                            ps, pb = bank(pB.next())

                            def mm(t):
                                for kc in range(2):
                                    ins = t.matmul(ps[:, :], KT[:, 2 * h + kc, mt * 128:(mt + 1) * 128], QT[q][:, 2 * h + kc, :],
                                                   start=(kc == 0), stop=(kc == 1))
                                return ins
                            sc.op("pe", mm, reads=[ktb[2 * h], ktb[2 * h + 1], qtb[q][2 * h], qtb[q][2 * h + 1]], writes=[pb])
                            pi = (h % 2) * 2 + mt
                            sc.op("act", lambda a: a.activation(out=PT[pi][:, :], in_=ps[:, :], func=AF.Exp, scale=1.0 / 16.0),
                                  reads=[pb], writes=[ptb[pi]])

                    def v_part(h):
                        pts = [(h % 2) * 2, (h % 2) * 2 + 1]
                        dps, dpb = bank(pC.next())

                        def mm(t):
                            for mt in range(2):
                                ins = t.matmul(dps[:, :], ONES[:, :], PT[pts[mt]][:, :], start=(mt == 0), stop=(mt == 1))
                            return ins
                        sc.op("pe", mm, reads=[ptb[p] for p in pts], writes=[dpb])
                        ri = h % 2
                        sc.op("act", lambda a: a.activation(out=REC[ri][:, :], in_=dps[:, :], func=AF.Ln), reads=[dpb], writes=[recb[ri]])
                        sc.op("act", lambda a: a.activation(out=REC[ri][:, :], in_=REC[ri][:, :], func=AF.Exp, scale=-1.0), reads=[recb[ri]], writes=[recb[ri]])
                        for dc in range(2):
                            ops_, opb = bank(pA.next())

                            def mm2(t):
                                for mt in range(2):
                                    ins = t.matmul(ops_[:, :], VM[:, mt, h * 256 + dc * 128:h * 256 + dc * 128 + 128], PT[pts[mt]][:, :],
                                                   start=(mt == 0), stop=(mt == 1))
                                return ins
                            sc.op("pe", mm2, reads=[vmb] + [ptb[p] for p in pts], writes=[opb])
                            sc.op("dve", lambda v: v.tensor_tensor(out=OT[q][:, 2 * h + dc, :], in0=ops_[:, :], in1=REC[ri][:, :], op=ALU.mult),
                                  reads=[opb, recb[ri]], writes=[otb[q][2 * h + dc]])
                    s_part(0)
                    for h in range(4):
                        if h + 1 < 4:
                            s_part(h + 1)
                        v_part(h)

                def stageC(tt):
                    q = tt % 2
                    for oc in range(8):
                        ps, pb = bank(pA.next())
                        _, k_, rb = wo[oc // 2]

                        def mm(t):
                            for k in range(8):
                                ins = t.matmul(ps[:, :], RING[:, k_, k * 256 + (oc % 2) * 128:k * 256 + (oc % 2) * 128 + 128],
                                               OT[q][:, k, :], start=(k == 0), stop=(k == 7))
                            return ins
                        sc.op("pe", mm, reads=[rb] + otb[q], writes=[pb])
                        sc.op("dve", lambda v: v.tensor_tensor(out=X[:, oc, tsl(tt)], in0=ps[:, :], in1=X[:, oc, tsl(tt)], op=ALU.add),
                              reads=[pb], writes=[Xb[oc][tt]])
                    if tt > 0:
                        post(tt - 1)
                    if tt == NT - 1:
                        post(tt)
                stageA(0)
                stageA(1)
                stageB(0)
                stageA(2)
                stageB(1)
                stageC(0)
                stageA(3)
                stageB(2)
                stageC(1)
                stageB(3)
                stageC(2)
                stageC(3)

        def mixer(l, post):
            with ExitStack() as s2:
                T2 = lambda name, shape, dt: s2.enter_context(nc.sbuf_tensor(uniq(name), shape, dt))
                YA = T2("YA", [128, 4, S], BF16)
                yab = [[Buf() for _ in range(NT)] for _ in range(4)]
                with ExitStack() as s3:
                    T3 = lambda name, shape, dt: s3.enter_context(nc.sbuf_tensor(uniq(name), shape, dt))
                    C8B = T3("C8B", [8, S], BF16)
                    NEGC = T3("NEGC", [128, 16 * 8], F32)
                    c8b_b = Buf()
                    negc_b = Buf()
                    pF = BankPool([0, 1])
                    with ExitStack() as s4:
                        T4 = lambda name, shape, dt: s4.enter_context(nc.sbuf_tensor(uniq(name), shape, dt))
                        FC = T4("FC", [8, S], F32)
                        ONES8 = T4("ONES8", [8, S], F32)
                        fcb = Buf()
                        o8b = Buf()
                        fresh(yab, c8b_b, negc_b, fcb, o8b)
                        sc.op("dve", lambda v: v.memset(ONES8[:, :], 1.0), writes=[o8b])
                        _, k_, rb = wload(("kxn", "w_in", l, 1536, 8), 64)
                        for tt in range(NT):
                            ps, pb = bank(pF.next())

                            def mm(t, ps=ps, tt=tt):
                                for k in range(8):
                                    ins = t.matmul(ps[0:8, :], RING[:, k_, k * 8:(k + 1) * 8], H[:, k, tsl(tt)], start=(k == 0), stop=(k == 7))
                                return ins
                            sc.op("pe", mm, reads=[rb] + [Hb[k][tt] for k in range(8)], writes=[pb])
                            bcol = pvcol(l, "b_f")
                            sc.op("act", lambda a, ps=ps, tt=tt: a.activation(out=FC[:, tsl(tt)], in_=ps[0:8, :], func=AF.Sigmoid,
                                                                             bias=PV[0:8, bcol:bcol + 1]),
                                  reads=[pb, cb], writes=[fcb])
                        sc.op("act", lambda a: a.activation(out=FC[:, :], in_=FC[:, :], func=AF.Ln), reads=[fcb], writes=[fcb])
                        sc.op("dve", lambda v: v.tensor_tensor_scan(out=FC[:, :], data0=ONES8[:, :], data1=FC[:, :], initial=0.0,
                                                                    op0=ALU.mult, op1=ALU.add), reads=[fcb, o8b], writes=[fcb])
                        sc.op("dve", lambda v: v.tensor_scalar(out=C8B[:, :], in0=FC[:, :], scalar1=8.0, scalar2=None, op0=ALU.mult),
                              reads=[fcb], writes=[c8b_b])
                        ps, pb = bank(pF.next())

                        def mmT(t):
                            for kb in range(16):
                                ins = t.transpose(ps[:, kb * 8:(kb + 1) * 8], FC[0:8, kb * 128:(kb + 1) * 128], ID8[0:8, 0:8])
                            return ins
                        sc.op("pe", mmT, reads=[fcb, cb], writes=[pb])
                        sc.op("dve", lambda v: v.tensor_scalar(out=NEGC[:, :], in0=ps[:, 0:128], scalar1=-1.0, scalar2=None, op0=ALU.mult),
                              reads=[pb], writes=[negc_b])
                    QA = T3("QA", [128, 2, S], BF16)
                    KA = T3("KA", [128, 2, S], BF16)
                    VP = T3("VP", [128, 16, 192], BF16)
                    PT = [T3("PT%d" % i, [128, TT], BF16) for i in range(4)]
                    qab = [[Buf() for _ in range(NT)] for _ in range(2)]
                    kab = [[Buf() for _ in range(NT)] for _ in range(2)]
                    qmb = [Buf() for _ in range(2)]
                    k1b = Buf()
                    vpb = [Buf() for _ in range(4)]
                    v1b = Buf()
                    ptb = [Buf() for _ in range(4)]
                    REC = RS
                    recb = RSb
                    dq = [sc.dsem() for _ in range(2)]
                    fresh(qab, kab, qmb, k1b, vpb, v1b, ptb)
                    sc.op("dve", lambda v: v.memset(KA[64:65, :, :], 1.0), writes=[k1b])
                    sc.op("dve", lambda v: v.memset(VP[:, :, 64:128], 1.0), writes=[v1b])
                    pP = BankPool([0, 1])
                    pS = BankPool([2, 3, 4, 5])
                    pACC = BankPool([6, 7])
                    pt_rr = BankPool([0, 1, 2, 3])
                    rec_rr = rs_rr
                    for hp in range(4):
                        _, kqk, rqk = wload(("kxn2", "w_in", l, hp * 128, 512 + hp * 128, 128), 2048)
                        _, kv, rv = wload(("kxn", "w_in", l, 1024 + hp * 128, 128), 1024)
                        for tt in range(NT):
                            for (dst, dstb, off) in ((QA, qab, 0), (KA, kab, 128)):
                                ps, pb = bank(pP.next())

                                def mm(t, ps=ps, off=off, tt=tt):
                                    for k in range(8):
                                        ins = t.matmul(ps[:, :], RING[:, kqk, k * 256 + off:k * 256 + off + 128], H[:, k, tsl(tt)],
                                                       start=(k == 0), stop=(k == 7))
                                    return ins
                                sc.op("pe", mm, reads=[rqk] + [Hb[k][tt] for k in range(8)], writes=[pb])
                                sc.op("dve", lambda v, ps=ps, dst=dst, tt=tt: v.tensor_copy(out=dst[0:64, 0, tsl(tt)], in_=ps[0:64, :]),
                                      reads=[pb], writes=[dstb[0][tt]])
                                sc.op("dve", lambda v, ps=ps, dst=dst, tt=tt: v.tensor_copy(out=dst[0:64, 1, tsl(tt)], in_=ps[64:128, :]),
                                      reads=[pb], writes=[dstb[1][tt]])
                        for hl in range(2):
                            h = 2 * hp + hl
                            sc.dma("sp", dq[hl], QA[64:65, hl, :], C8B[h:h + 1, :], reads=[c8b_b], writes=[qmb[hl]])
                        for tb4 in range(4):
                            ps, pb = bank(pP.next())

                            def mm(t, ps=ps, tb4=tb4):
                                for q4 in range(4):
                                    tb = tb4 * 4 + q4
                                    for k in range(8):
                                        ins = t.matmul(ps[:, q4 * 128:(q4 + 1) * 128], H[:, k, tb * 128:(tb + 1) * 128],
                                                       RING[:, kv, k * 128:(k + 1) * 128], start=(k == 0), stop=(k == 7))
                                return ins
                            sc.op("pe", mm, reads=[rv] + [Hb[k][tb4] for k in range(8)], writes=[pb])
                            psv = ps[:, :].rearrange("p (a b) -> p a b", b=128)
                            sc.op("dve", lambda v, psv=psv, tb4=tb4: v.tensor_copy(out=VP[:, tb4 * 4:(tb4 + 1) * 4, 0:64], in_=psv[:, :, 0:64]),
                                  reads=[pb], writes=[vpb[tb4]])
                            sc.op("dve", lambda v, psv=psv, tb4=tb4: v.tensor_copy(out=VP[:, tb4 * 4:(tb4 + 1) * 4, 128:192], in_=psv[:, :, 64:128]),
                                  reads=[pb], writes=[vpb[tb4]])
                        items = []
                        for hl in range(2):
                            for T_ in range(NT):
                                for kb in range(4 * (T_ + 1)):
                                    items.append((hl, T_, kb))
                        LA = 3
                        state = {}

                        def stageS(it):
                            hl, T_, kb = it
                            h = 2 * hp + hl
                            d = kb - 4 * T_
                            c0 = 128 * d if d > 0 else 0
                            ps, pb = bank(pS.next())

                            def mm(t):
                                if d >= 0:
                                    t.matmul(ps[:, c0:c0 + 128], IDB, maskneg(0)[:, 0:128], start=True, stop=False, skip_group_check=True)
                                    return t.matmul(ps[:, c0:TT], KA[0:65, hl, kb * 128:(kb + 1) * 128],
                                                    QA[0:65, hl, T_ * TT + c0:(T_ + 1) * TT], start=False, stop=True, skip_group_check=True)
                                return t.matmul(ps[:, c0:TT], KA[0:65, hl, kb * 128:(kb + 1) * 128],
                                                QA[0:65, hl, T_ * TT + c0:(T_ + 1) * TT], start=True, stop=True)
                            sc.op("pe", mm, reads=[kab[hl][kb // 4], k1b, qab[hl][T_], qmb[hl], cb, cb2], writes=[pb])
                            pi = pt_rr.next()
                            sc.op("act", lambda a: a.activation(out=PT[pi][:, c0:TT], in_=ps[:, c0:TT], func=AF.Exp, scale=0.125,
                                                                bias=NEGC[:, kb * 8 + h:kb * 8 + h + 1]),
                                  reads=[pb, negc_b], writes=[ptb[pi]])
                            state[it] = (pi, c0)

                        def stageV(it):
                            hl, T_, kb = it
                            nkb = 4 * (T_ + 1)
                            pi, c0 = state.pop(it)
                            if kb == 0:
                                state["acc", hl, T_] = bank(pACC.next())
                            acc, accb = state["acc", hl, T_]
                            sc.op("pe", lambda t: t.matmul(acc[:, c0:TT], VP[:, kb, hl * 64:hl * 64 + 128], PT[pi][:, c0:TT],
                                                           start=(kb == 0), stop=(kb == nkb - 1)),
                                  reads=[vpb[kb // 4], v1b, ptb[pi]], writes=[accb])
                            if kb == nkb - 1:
                                ri = rec_rr.next()
                                num = slice(0, 64) if hl == 0 else slice(64, 128)
                                den = slice(64, 128) if hl == 0 else slice(0, 64)
                                sc.op("dve", lambda v: v.reciprocal(out=REC[ri][num, :], in_=acc[den, :]), reads=[accb], writes=[recb[ri]])
                                sc.op("dve", lambda v: v.tensor_tensor(out=YA[num, hp, tsl(T_)], in0=acc[num, :], in1=REC[ri][num, :], op=ALU.mult),
                                      reads=[accb, recb[ri]], writes=[yab[hp][T_]])
                                del state["acc", hl, T_]

                        for i in range(len(items) + LA):
                            if i < len(items):
                                stageS(items[i])
                            if i >= LA:
                                stageV(items[i - LA])
                def rmsA(tt):
                    rms_tile(lambda c: YA[:, :, tsl(tt)] if c is None else YA[:, c, tsl(tt)],
                             [yab[c][tt] for c in range(4)],
                             lambda c: YA[:, c, tsl(tt)], [yab[c][tt] for c in range(4)], 4, TT, pvcol(l, "attn_out_g"), ONESM5)
                YC = T2("YC", [128, 4, S], BF16)
                UBUF = [T2("UBUF%d" % i, [128, 2, 32 + TT], BF16) for i in range(2)]
                UBN = [T2("UBN%d" % i, [128, TT], BF16) for i in range(2)]
                ubnb = [Buf() for _ in range(2)]
                HALO = T2("HALO", [128, 4, 2, 30], BF16)
                CBS = [T2("CB%d" % i, [128, 4, TT], F32) for i in range(2)]
                ycb = [[Buf() for _ in range(NT)] for _ in range(4)]
                ubb = [Buf() for _ in range(2)]
                halob = [Buf() for _ in range(4)]
                cbbs = [[Buf() for _ in range(4)] for _ in range(2)]
                pA = BankPool([0, 1, 2, 3])
                pCv = BankPool([4, 5])
                pM = BankPool([6])
                fresh(ycb, ubb, ubnb, halob, cbbs)
                sc.op("dve", lambda v: v.memset(HALO[:, :, :, :], 0.0), writes=halob)
                for b_ in range(2):
                    sc.op("dve", lambda v, b_=b_: v.memset(UBUF[b_][:, :, :], 0.0), writes=[ubb[b_]])
                wag = [wload(("kxn2", "w_in", l, 1544 + cc * 128, 2056 + cc * 128, 128), 2048) for cc in range(4)]
                dgs = [ring_take() for _ in range(2)]
                cw2 = pvcol(l, "conv_w2")
                cbias = pvcol(l, "conv_b")
                lg = pvcol(l, "ln_g")
                lb = pvcol(l, "ln_b")
                E3 = EID.unsqueeze(1).to_broadcast([128, 16, 64])

                def projA(it):
                    tt, cc = it // 4, it % 4
                    b_ = it % 2
                    _, k_, rb = wag[cc]
                    aps, apb = bank(pA.next())
                    gps, gpb = bank(pA.next())
                    for (ps, pb, off) in ((aps, apb, 0), (gps, gpb, 128)):
                        def mm(t, ps=ps, off=off):
                            for k in range(8):
                                ins = t.matmul(ps[:, :], RING[:, k_, k * 256 + off:k * 256 + off + 128], H[:, k, tsl(tt)],
                                               start=(k == 0), stop=(k == 7))
                            return ins
                        sc.op("pe", mm, reads=[rb] + [Hb[k][tt] for k in range(8)], writes=[pb])
                    i = sg_rr.next()
                    sc.op("act", lambda a: a.activation(out=SG[i][:, :], in_=gps[:, :], func=AF.Sigmoid), reads=[gpb], writes=[SGb[i]])
                    sc.op("dve", lambda v: v.tensor_tensor(out=UBN[b_][:, :], in0=SG[i][:, :], in1=aps[:, :], op=ALU.mult),
                          reads=[SGb[i], apb], writes=[ubnb[b_]])
                    sc.op("pool", lambda v: v.tensor_copy(out=UBUF[b_][:, :, 0:30], in_=HALO[:, cc, :, :]), reads=[halob[cc]], writes=[ubb[b_]])
                    sc.op("pool", lambda v: v.tensor_copy(out=UBUF[b_][0:64, 0, 30:30 + TT], in_=UBN[b_][0:64, :]), reads=[ubnb[b_]], writes=[ubb[b_]])
                    sc.op("pool", lambda v: v.tensor_copy(out=UBUF[b_][64:128, 1, 29:29 + TT], in_=UBN[b_][64:128, :]), reads=[ubnb[b_]], writes=[ubb[b_]])
                    sc.op("dve", lambda v: v.tensor_copy(out=UBUF[b_][64:128, 0, 29:29 + TT], in_=UBN[b_][0:64, :]), reads=[ubnb[b_]], writes=[ubb[b_]])
                    sc.op("dve", lambda v: v.tensor_copy(out=UBUF[b_][0:64, 1, 30:30 + TT], in_=UBN[b_][64:128, :]), reads=[ubnb[b_]], writes=[ubb[b_]])
                    kd, bd = dgs[b_]
                    for h in range(2):
                        w0 = cw2 + (cc * 2 + h) * 16
                        sc.op("pool", lambda v, h=h, w0=w0: v.tensor_tensor(
                            out=RING[:, kd, h * 1024:(h + 1) * 1024].rearrange("p (r c) -> p r c", c=64), in0=E3,
                            in1=PV[:, w0:w0 + 16].unsqueeze(2).to_broadcast([128, 16, 64]), op=ALU.mult),
                            reads=[cb, cb2], writes=[bd])

                def convB(it):
                    tt, cc = it // 4, it % 4
                    b_ = it % 2
                    kd, bd = dgs[b_]
                    cps, cpb = bank(pCv.next())

                    def mm(t):
                        for r in range(16):
                            for h in range(2):
                                ins = t.matmul(cps[64 * h:64 * h + 64, :], RING[:, kd, (h * 16 + r) * 64:(h * 16 + r) * 64 + 64],
                                               UBUF[b_][:, h, 2 * r:2 * r + TT], start=(r == 0), stop=(r == 15), tile_position=(0, 64 * h))
                        return ins
                    sc.op("pe", mm, reads=[bd, ubb[b_]], writes=[cpb])
                    CB = CBS[tt % 2]
                    cbb = cbbs[tt % 2]
                    sc.op("act", lambda a: a.activation(out=CB[:, cc, :], in_=cps[:, :], func=AF.Identity, bias=PV[:, cbias + cc:cbias + cc + 1]),
                          reads=[cpb, cb], writes=[cbb[cc]])
                    if tt < NT - 1:
                        sc.op("pool", lambda v: v.tensor_copy(out=HALO[:, cc, :, :], in_=UBUF[b_][:, :, TT:TT + 30]), reads=[ubb[b_]], writes=[halob[cc]])

                def lnC(tt):
                    CB = CBS[tt % 2]
                    cbb = cbbs[tt % 2]
                    st_ = {}
                    rp = rms_parts(lambda c: CB[:, :, :] if c is None else CB[:, c, :], cbb,
                                   lambda c: YC[:, c, tsl(tt)], [ycb[c][tt] for c in range(4)], 4, TT, pvcol(l, "conv_out_g"), ONESM5)

                    def s1():
                        mps, mpb = bank(pM.next())
                        st_["m"] = (mps, mpb)

                        def mmm(t):
                            for cc in range(4):
                                ins = t.matmul(mps[:, :], ONESF5[:, :], CB[:, cc, :], start=(cc == 0), stop=(cc == 3))
                            return ins
                        sc.op("pe", mmm, reads=cbb + [cb], writes=[mpb])

                    def s2():
                        mps, mpb = st_["m"]
                        for cc in range(4):
                            sc.op("dve", lambda v, cc=cc: v.tensor_tensor(out=CB[:, cc, :], in0=CB[:, cc, :], in1=mps[:, :], op=ALU.subtract),
                                  reads=[cbb[cc], mpb], writes=[cbb[cc]])
                        sc.op("act", lambda a: a.activation(out=SQ[:, 0:4, :], in_=CB[:, :, :], func=AF.Square), reads=cbb, writes=[SQb])

                    def s3():
                        vps, vpb_ = bank(pM.next())
                        st_["v"] = (vps, vpb_)

                        def mmv(t):
                            for cc in range(4):
                                ins = t.matmul(vps[:, :], ONESM5[:, :], SQ[:, cc, :], start=(cc == 0), stop=(cc == 3))
                            return ins
                        sc.op("pe", mmv, reads=[SQb, cb], writes=[vpb_])
                        sc.op("act", lambda a: a.activation(out=vps[:, :], in_=vps[:, :], func=AF.Ln, bias=EPSC[:, 0:1]),
                              reads=[vpb_, cb], writes=[vpb_])
                        sc.op("act", lambda a: a.activation(out=vps[:, :], in_=vps[:, :], func=AF.Exp, scale=-0.5), reads=[vpb_], writes=[vpb_])

                    def s4():
                        vps, vpb_ = st_["v"]
                        for cc in range(4):
                            sc.op("dve", lambda v, cc=cc: v.scalar_tensor_tensor(out=CB[:, cc, :], in0=CB[:, cc, :], scalar=PV[:, lg + cc:lg + cc + 1],
                                                                                  in1=vps[:, :], op0=ALU.mult, op1=ALU.mult),
                                  reads=[cbb[cc], vpb_, cb], writes=[cbb[cc]])
                            sc.op("act", lambda a, cc=cc: a.activation(out=CB[:, cc, :], in_=CB[:, cc, :], func=AF.Silu, bias=PV[:, lb + cc:lb + cc + 1]),
                                  reads=[cbb[cc], cb], writes=[cbb[cc]])
                        rp[0]()

                    def s5():
                        rp[1]()
                        rp[2]()
                    return [s1, s2, s3, s4, s5]

                NIT = 4 * NT
                pending = []
                wout = []
                projA(0)
                for it in range(NIT):
                    if it + 1 < NIT:
                        projA(it + 1)
                        if it + 1 >= NIT - 4:
                            wout.append(wload(("kxn", "w_out", l, (it + 1 - (NIT - 4)) * 256, 256), 2048))
                    convB(it)
                    if it % 4 == 3:
                        pending.append(lnC(it // 4))
                    for p in pending:
                        if p:
                            p.pop(0)()
                    if it % 4 == 1:
                        rmsA(it // 4)
                assert len(wout) == 4
                pO = BankPool([0, 1, 2, 3])

                def wout_tile(tt, do_post=True):
                    for oc in range(8):
                        ps, pb = bank(pO.next())
                        _, k_, rb = wout[oc // 2]

                        def mm(t):
                            for k in range(8):
                                rhs = YA[:, k, tsl(tt)] if k < 4 else YC[:, k - 4, tsl(tt)]
                                ins = t.matmul(ps[:, :], RING[:, k_, k * 256 + (oc % 2) * 128:k * 256 + (oc % 2) * 128 + 128], rhs,
                                               start=(k == 0), stop=(k == 7))
                            return ins
                        sc.op("pe", mm, reads=[rb] + [yab[c][tt] for c in range(4)] + [ycb[c][tt] for c in range(4)], writes=[pb])
                        sc.op("dve", lambda v: v.tensor_tensor(out=X[:, oc, tsl(tt)], in0=ps[:, :], in1=X[:, oc, tsl(tt)], op=ALU.add),
                              reads=[pb], writes=[Xb[oc][tt]])
                    if do_post:
                        if tt > 0:
                            post(tt - 1)
                        if tt == NT - 1:
                            post(tt)
                tdone = 0
                owed = []
                while any(pending):
                    nleft = max(len(p) for p in pending)
                    for p in pending:
                        if p:
                            p.pop(0)()
                    if tdone < NT - 1:
                        if nleft > 2:
                            wout_tile(tdone)
                        else:
                            wout_tile(tdone, do_post=False)
                            if tdone > 0:
                                owed.append(tdone - 1)
                        tdone += 1
                for tt in owed:
                    post(tt)
                for tt in range(tdone, NT):
                    wout_tile(tt)

        stg = stages if stages is not None else "f1,mix,xa,f2"
        stg = stg.split(",")
        gain = {"f1": "ffn1_g", "mix": "mix_g", "xa": "xattn_g", "f2": "ffn2_g", "f3": "ffn1_g"}
        calls = [(name, l) for l in range(L) for name in ("f1", "mix", "xa", "f2", "f3") if name in stg]
        rmsnorm_XH(pvcol(calls[0][1], gain[calls[0][0]]))

        def run(name, l, post):
            if name == "f1" or name == "f3":
                ffn(l, 1, post, mem_prologue=True)
            elif name == "f2":
                ffn(l, 2, post)
            elif name == "mix":
                mixer(l, post)
            elif name == "xa":
                xattn(l, post)
        for i, (name, l) in enumerate(calls[:-1]):
            nn, nl = calls[i + 1]
            run(name, l, lambda tt, g=pvcol(nl, gain[nn]): rms_X_tile(tt, g))
        with nc.sbuf_tensor("OB", [128, 8, TT], F32) as OB:
            obb = [Buf() for _ in range(8)]
            d_out = sc.dsem()
            fresh(obb)

            def final_tile(tt):
                rms_tile(lambda c: X[:, :, tsl(tt)] if c is None else X[:, c, tsl(tt)],
                         [Xb[c][tt] for c in range(8)],
                         lambda c: OB[:, c, :], obb, 8, TT, FINAL_G, ONESM)
                sc.dma("sp", d_out, oT_v[:, :, tsl(tt)], OB[:, :, :], reads=obb)
            run(calls[-1][0], calls[-1][1], final_tile)
            nc.sync.wait_ge(d_out.sem, d_out.n)
    return nc, plan


def _pack_units(inp, plan):
    out = np.zeros((len(plan), 128, SLOT), np.float32)
    for u, d in enumerate(plan):
        kind = d[0]
        if kind == "kxn":
            _, name, l, c0, n = d
            w = inp[name][l][:, c0:c0 + n]
            out[u, :, :8 * n] = w.reshape(8, 128, n).transpose(1, 0, 2).reshape(128, 8 * n)
        elif kind == "kxn2":
            _, name, l, c0, c1, n = d
            w = np.concatenate([inp[name][l][:, c0:c0 + n], inp[name][l][:, c1:c1 + n]], axis=1)
            out[u, :, :16 * n] = w.reshape(8, 128, 2 * n).transpose(1, 0, 2).reshape(128, 16 * n)
        elif kind == "rows":
            _, name, l, r0, n = d
            w = inp[name][l][r0:r0 + n, :]
            out[u, :, :] = w.reshape(2, 128, 1024).transpose(1, 0, 2).reshape(128, 2048)
    return out


def _pack_pvec(inp):
    pv = np.zeros((128, NPV), np.float32)
    c8 = lambda v: v.reshape(8, 128).T
    c4 = lambda v: v.reshape(4, 128).T
    for l in range(L):
        pv[:, pvcol(l, "ffn1_g"):pvcol(l, "ffn1_g") + 8] = c8(inp["ffn1_norm_g"][l])
        pv[:, pvcol(l, "mix_g"):pvcol(l, "mix_g") + 8] = c8(inp["mix_norm_g"][l])
        pv[:, pvcol(l, "xattn_g"):pvcol(l, "xattn_g") + 8] = c8(inp["xattn_norm_g"][l])
        pv[:, pvcol(l, "mem_g"):pvcol(l, "mem_g") + 8] = c8(inp["mem_norm_g"][l])
        pv[:, pvcol(l, "ffn2_g"):pvcol(l, "ffn2_g") + 8] = c8(inp["ffn2_norm_g"][l])
        cw = inp["conv_w"][l]
        cwp = np.concatenate([cw, np.zeros((1, cw.shape[1]), cw.dtype)], axis=0)
        pp = np.arange(128)
        for cc in range(4):
            for h in range(2):
                for r in range(16):
                    col = pvcol(l, "conv_w2") + (cc * 2 + h) * 16 + r
                    pv[:, col] = cwp[2 * r + pp // 64, cc * 128 + 64 * h + pp % 64]
        pv[:, pvcol(l, "conv_b"):pvcol(l, "conv_b") + 4] = c4(inp["conv_b"][l])
        pv[:, pvcol(l, "ln_g"):pvcol(l, "ln_g") + 4] = c4(inp["conv_ln_g"][l])
        pv[:, pvcol(l, "ln_b"):pvcol(l, "ln_b") + 4] = c4(inp["conv_ln_b"][l])
        pv[:, pvcol(l, "conv_out_g"):pvcol(l, "conv_out_g") + 4] = c4(inp["conv_out_g"][l])
        pv[:, pvcol(l, "attn_out_g"):pvcol(l, "attn_out_g") + 4] = c4(inp["attn_out_g"][l])
        pv[0:8, pvcol(l, "b_f")] = inp["b_f"][l]
    pv[:, FINAL_G:FINAL_G + 8] = c8(inp["final_norm_g"])
    return pv


def _consts():
    c = np.zeros((128, NCST), np.float32)
    c[:, 0:128] = np.eye(128, dtype=np.float32)
    p = np.arange(128)[:, None]
    f = np.arange(512)[None, :]
    c[:, 128:256] = np.where(f[:, 0:128] >= p, 0.0, MASKNEG)
    c[:, 256:320] = (np.arange(128)[:, None] % 64 == np.arange(64)[None, :]).astype(np.float32)
    return c


_CACHE = {}


def kernel(**inputs):
    inp = {k: np.asarray(v) for k, v in inputs.items()}
    stages = os.environ.get("MK_STAGES")
    key = stages
    if key not in _CACHE:
        _CACHE[key] = build(stages)
    nc, plan = _CACHE[key]
    assert len(plan) <= NU, (len(plan), NU)
    wst = np.zeros((NU, 128, SLOT), np.float32)
    wst[:len(plan)] = _pack_units(inp, plan)
    pv = _pack_pvec(inp)
    cst = _consts()
    B = inp["x"].shape[0]
    in_maps = []
    for b in range(B):
        in_maps.append({
            "xT": np.ascontiguousarray(inp["x"][b].T),
            "memT": np.ascontiguousarray(inp["mem"][b].T),
            "wst": wst, "pvec": pv, "cst": cst,
        })
    res = run_bass_kernel_spmd(nc, in_maps, core_ids=list(range(B)))
    out = np.stack([np.ascontiguousarray(r["oT"].T) for r in res.results], axis=0)
    return out.astype(np.float32)
```

```python
import os
from contextlib import ExitStack
import numpy as np
import concourse.bass as bass
import concourse.mybir as mybir
from concourse.bass_utils import run_bass_kernel_spmd

F32 = mybir.dt.float32
BF16 = mybir.dt.bfloat16
AF = mybir.ActivationFunctionType
ALU = mybir.AluOpType

S = 2048
D = 1024
DFF = 2816
NMEM = 256
L = 2
EPS = 1e-6
NSLOT = 8
SLOT = 2048
NT = 4
TT = 512
NCST = 128 + 128 + 64
MASKNEG = -30000.0

_PVL = [("ffn1_g", 8), ("mix_g", 8), ("xattn_g", 8), ("mem_g", 8), ("ffn2_g", 8), ("conv_w2", 128),
        ("conv_b", 4), ("ln_g", 4), ("ln_b", 4), ("conv_out_g", 4), ("attn_out_g", 4), ("b_f", 1)]
_PVO = {}
_o = 0
for _n, _w in _PVL:
    _PVO[_n] = _o
    _o += _w
PL = _o
NPV = L * PL + 8


def pvcol(l, name):
    return l * PL + _PVO[name]


FINAL_G = L * PL


FFN_NAMES = {1: ("ffn1_w_gate", "ffn1_w_up", "ffn1_w_down"), 2: ("ffn2_w_gate", "ffn2_w_up", "ffn2_w_down")}


class Buf:
    __slots__ = ("w", "r", "excl")

    def __init__(self, excl=False):
        self.w = None
        self.r = {}
        self.excl = excl


class DSem:
    def __init__(self, sem):
        self.sem = sem
        self.n = 0


class Eng:
    def __init__(self, name, h, sem):
        self.name = name
        self.h = h
        self.sem = sem
        self.n = 0
        self.known = {}


class PEProxy:
    def __init__(self, h, wait):
        self.h = h
        self.wait = wait

    def _first(self, ins):
        if self.wait is not None:
            ins._wait_ge(self.wait[0], self.wait[1])
            self.wait = None
        return ins

    def matmul(self, *a, **k):
        return self._first(self.h.matmul(*a, **k))

    def transpose(self, *a, **k):
        return self._first(self.h.transpose(*a, **k))


class Sched:
    def __init__(self, nc, stack):
        self.nc = nc
        self.stack = stack
        self.eng = {}
        for name, h in (("pe", nc.tensor), ("act", nc.scalar), ("dve", nc.vector),
                        ("pool", nc.gpsimd), ("sp", nc.sync)):
            self.eng[name] = Eng(name, h, stack.enter_context(nc.semaphore("s_" + name)))
        self.nsem = 0
        self.snaps = {}

    def dsem(self):
        self.nsem += 1
        return DSem(self.stack.enter_context(self.nc.semaphore("d%d" % self.nsem)))

    def _waits(self, en, reads, writes, attach_ok=False):
        e = self.eng[en]
        deps = {}

        def need(tok):
            if tok is None:
                return
            k = id(tok[0])
            if k not in deps or deps[k][1] < tok[1]:
                deps[k] = tok

        for b in reads:
            need(b.w)
            if b.excl:
                for t in b.r.values():
                    need(t)
        for b in writes:
            need(b.w)
            for t in b.r.values():
                need(t)
        todo = []
        for k, (sem, val) in deps.items():
            if en == "pe" and sem is e.sem:
                continue
            if e.known.get(k, 0) >= val:
                continue
            e.known[k] = val
            todo.append((sem, val))
            snap = self.snaps.get((k, val))
            if snap is not None:
                for k2, v2 in snap.items():
                    if e.known.get(k2, 0) < v2:
                        e.known[k2] = v2
        self.attach = None
        if en in ("pe", "act", "dve") and attach_ok and todo:
            self.attach = todo.pop()
        for sem, val in todo:
            e.h.wait_ge(sem, val)
        return e

    def _mark(self, tok, reads, writes):
        for b in reads:
            if b.excl:
                b.w = tok
                b.r = {}
            else:
                b.r[id(tok[0])] = tok
        for b in writes:
            b.w = tok
            b.r = {}

    def op(self, en, fn, reads=(), writes=()):
        e = self._waits(en, reads, writes, attach_ok=True)
        if en == "pe":
            ins = fn(PEProxy(e.h, self.attach))
        else:
            ins = fn(e.h)
            if self.attach is not None:
                ins._wait_ge(self.attach[0], self.attach[1])
        e.n += 1
        ins.then_inc(e.sem, 1)
        self.snaps[(id(e.sem), e.n)] = dict(e.known)
        self._mark((e.sem, e.n), reads, writes)

    def dma(self, qn, ds, out_ap, in_ap, reads=(), writes=()):
        e = self._waits(qn, reads, writes)
        e.h.dma_start(out=out_ap, in_=in_ap).then_inc(ds.sem, 16)
        ds.n += 16
        self._mark((ds.sem, ds.n), reads, writes)


class BankPool:
    def __init__(self, items):
        self.items = items
        self.i = 0

    def next(self):
        it = self.items[self.i % len(self.items)]
        self.i += 1
        return it


def n_units():
    per_layer = 33 * 2 + 8 + 1 + 4 + 4 + 16
    return per_layer * L


NU = n_units()


def build(stages=None):
    plan = []
    nc = bass.Bass("TRN2", target_bir_lowering=False)
    xT_d = nc.dram_tensor("xT", [D, S], F32, kind="ExternalInput").ap()
    memT_d = nc.dram_tensor("memT", [D, NMEM], F32, kind="ExternalInput").ap()
    wst_d = nc.dram_tensor("wst", [NU, 128, SLOT], F32, kind="ExternalInput").ap()
    pv_d = nc.dram_tensor("pvec", [128, NPV], F32, kind="ExternalInput").ap()
    cst_d = nc.dram_tensor("cst", [128, NCST], F32, kind="ExternalInput").ap()
    oT_d = nc.dram_tensor("oT", [D, S], F32, kind="ExternalOutput").ap()
    xT_v = xT_d.rearrange("(c p) t -> p c t", p=128)
    memT_v = memT_d.rearrange("(c p) t -> p c t", p=128)
    oT_v = oT_d.rearrange("(c p) t -> p c t", p=128)

    with ExitStack() as st:
        sc = Sched(nc, st)
        _uid = [0]

        def uniq(name):
            _uid[0] += 1
            return "%s_%d" % (name, _uid[0])
        T = lambda name, shape, dt: st.enter_context(nc.sbuf_tensor(name, shape, dt))
        X = T("X", [128, 8, S], F32)
        H = T("H", [128, 8, S], BF16)
        RING = T("RING", [128, NSLOT, SLOT], BF16)
        PV = T("PV", [128, NPV], F32)
        CSTB = T("CSTB", [128, NCST], BF16)
        ID8 = T("ID8", [8, 8], F32)
        ONESM = T("ONESM", [128, 128], BF16)
        ONESM5 = T("ONESM5", [128, 128], BF16)
        ONES = T("ONES", [128, 128], BF16)
        ONESF5 = T("ONESF5", [128, 128], F32)
        EPSC = T("EPSC", [128, 1], F32)
        SQ = T("SQ", [128, 8, TT], BF16)
        RS = [T("RS%d" % i, [128, TT], F32) for i in range(4)]
        MN = T("MN", [128, 8, NMEM], BF16)
        PS = [st.enter_context(nc.psum_tensor("ps%d" % i, [128, TT], F32)) for i in range(8)]
        PSb = [Buf(excl=True) for _ in range(8)]
        bank = lambda i: (PS[i], PSb[i])

        Xb = [[Buf() for _ in range(NT)] for _ in range(8)]
        Hb = [[Buf() for _ in range(NT)] for _ in range(8)]
        SQb = Buf()
        RSb = [Buf() for _ in range(4)]
        mnb = [Buf() for _ in range(8)]
        cb = Buf()
        rs_rr = BankPool([0, 1, 2, 3])
        SG, SGb, sg_rr = RS, RSb, rs_rr
        ringb = [Buf() for _ in range(NSLOT)]
        ringd = [sc.dsem() for _ in range(NSLOT)]
        ring_i = [0]

        def wload(desc, nelem):
            u = len(plan)
            plan.append(desc)
            k = ring_i[0] % NSLOT
            ring_i[0] += 1
            if u == 3:
                nc.gpsimd.wait_ge(d_xs[NT - 1].sem, d_xs[NT - 1].n)
            if u == 6:
                sc.dma("pool", d_cst, CSTB[:, :], cst_d[:, :], writes=[cb2])
            sc.dma("pool", ringd[k], RING[:, k, 0:nelem], wst_d[u, :, 0:nelem], writes=[ringb[k]])
            return RING, k, ringb[k]

        def ring_take():
            k = ring_i[0] % NSLOT
            ring_i[0] += 1
            return k, ringb[k]

        def tsl(tt):
            return slice(tt * TT, (tt + 1) * TT)

        def fresh(*groups):
            toks = {}
            for fn_ in ("pe", "act", "dve", "pool"):
                f = sc.eng[fn_]
                if f.n > 0:
                    toks[id(f.sem)] = (f.sem, f.n)

            def walk(x):
                if isinstance(x, Buf):
                    x.r = dict(toks)
                else:
                    for y in x:
                        walk(y)
            walk(groups)

        def barrier():
            for en in ("pe", "act", "dve", "sp"):
                e = sc.eng[en]
                for fn_ in ("pe", "act", "dve"):
                    f = sc.eng[fn_]
                    if f is e or f.n == 0 or e.known.get(id(f.sem), 0) >= f.n:
                        continue
                    e.known[id(f.sem)] = f.n
                    e.h.wait_ge(f.sem, f.n)

        d_in = sc.dsem()
        sc.dma("sp", d_in, PV[:, :], pv_d[:, :], writes=[cb])
        sc.dma("sp", d_in, ID8[:, :], cst_d[0:8, 0:8], writes=[cb])
        cb2 = Buf()
        d_cst = sc.dsem()
        d_xs = [sc.dsem() for _ in range(NT)]
        for tt in range(NT):
            sc.dma("sp", d_xs[tt], X[:, :, tt * TT:(tt + 1) * TT], xT_v[:, :, tt * TT:(tt + 1) * TT], writes=[Xb[c][tt] for c in range(8)])
        sc.op("dve", lambda v: v.memset(ONESM[:, :], 1.0 / 1024.0), writes=[cb])
        sc.op("dve", lambda v: v.memset(ONESM5[:, :], 1.0 / 512.0), writes=[cb])
        sc.op("dve", lambda v: v.memset(ONES[:, :], 1.0), writes=[cb])
        sc.op("dve", lambda v: v.memset(ONESF5[:, :], 1.0 / 512.0), writes=[cb])
        sc.op("dve", lambda v: v.memset(EPSC[:, :], EPS), writes=[cb])
        for en in ("pe", "act", "dve", "sp"):
            e = sc.eng[en]
            e.h.wait_ge(d_in.sem, d_in.n)
            e.h.wait_ge(sc.eng["dve"].sem, sc.eng["dve"].n)
            e.known[id(d_in.sem)] = d_in.n
            e.known[id(sc.eng["dve"].sem)] = sc.eng["dve"].n
        cb.w = None
        for c in range(8):
            for tt in range(NT):
                Xb[c][tt].w = (d_xs[tt].sem, d_xs[tt].n)
        IDB = CSTB[:, 0:128]
        maskneg = lambda d: CSTB[:, 128:256]
        EID = CSTB[:, 256:320]

        pStat = BankPool([7])

        def rms_parts(src, srcb, dst, dstb, nch, ncol, gcol, onesm):
            st_ = {}

            def p1():
                sc.op("act", lambda a: a.activation(out=SQ[:, 0:nch, 0:ncol], in_=src(None), func=AF.Square),
                      reads=srcb + [cb], writes=[SQb])

            def p2():
                ps, psb = bank(pStat.next())

                def mm(t):
                    for c in range(nch):
                        ins = t.matmul(ps[:, 0:ncol], onesm[:, :], SQ[:, c, 0:ncol], start=(c == 0), stop=(c == nch - 1))
                    return ins
                sc.op("pe", mm, reads=[SQb, cb], writes=[psb])
                i = rs_rr.next()
                st_["i"] = i
                sc.op("act", lambda a: a.activation(out=RS[i][:, 0:ncol], in_=ps[:, 0:ncol], func=AF.Ln, bias=EPSC[:, 0:1]),
                      reads=[psb, cb], writes=[RSb[i]])
                sc.op("act", lambda a: a.activation(out=RS[i][:, 0:ncol], in_=RS[i][:, 0:ncol], func=AF.Exp, scale=-0.5),
                      reads=[RSb[i]], writes=[RSb[i]])

            def p3():
                i = st_["i"]
                for c in range(nch):
                    sc.op("dve", lambda v, c=c: v.scalar_tensor_tensor(out=dst(c), in0=src(c), scalar=PV[:, gcol + c:gcol + c + 1],
                                                                        in1=RS[i][:, 0:ncol], op0=ALU.mult, op1=ALU.mult),
                          reads=[srcb[c], RSb[i], cb], writes=[dstb[c]])
            return [p1, p2, p3]

        def rms_tile(*a):
            for p in rms_parts(*a):
                p()

        def rms_X_tile(tt, gcol):
            rms_tile(lambda c: X[:, :, tsl(tt)] if c is None else X[:, c, tsl(tt)],
                     [Xb[c][tt] for c in range(8)],
                     lambda c: H[:, c, tsl(tt)], [Hb[c][tt] for c in range(8)], 8, TT, gcol, ONESM)

        def rmsnorm_XH(gcol):
            for tt in range(NT):
                rms_X_tile(tt, gcol)

        def ffn(l, which, post, mem_prologue=False):
            n_gate, n_up, n_down = FFN_NAMES[which]
            pG = BankPool([0, 1, 2, 3])
            pD = BankPool([4, 5, 6])
            with ExitStack() as sf:
                ACTT = sf.enter_context(nc.sbuf_tensor(uniq("ACTT"), [128, 6, S], BF16))
                actb = [[Buf() for _ in range(NT)] for _ in range(6)]
                fresh(actb)
                if mem_prologue:
                    MEMT = sf.enter_context(nc.sbuf_tensor(uniq("MEMT"), [128, 8, NMEM], F32))
                    memb = [Buf() for _ in range(8)]
                    fresh(memb)
                    dm = sc.dsem()
                    sc.dma("sp", dm, MEMT[:, :, :], memT_v[:, :, :], writes=memb)
                groups = [(0, 1), (2, 3, 4), (5, 6, 7), (8, 9, 10)]
                for grp in groups:
                    if mem_prologue and grp is groups[1]:
                        rms_tile(lambda c: MEMT[:, :, :] if c is None else MEMT[:, c, :], memb,
                                 lambda c: MN[:, c, :], mnb, 8, NMEM, pvcol(l, "mem_g"), ONESM)
                    sl = {}
                    for u in grp:
                        sl["g", u] = wload(("kxn", n_gate, l, u * 256, 256), 2048)
                        sl["u", u] = wload(("kxn", n_up, l, u * 256, 256), 2048)
                    for ui, u in enumerate(grp):
                        for j in range(2):
                            hl = ui * 2 + j
                            for tt in range(NT):
                                gps, gpb = bank(pG.next())
                                ups, upb = bank(pG.next())
                                for (ps, pb, key) in ((gps, gpb, "g"), (ups, upb, "u")):
                                    _, k_, rb = sl[key, u]

                                    def mm(t, ps=ps, k_=k_, j=j, tt=tt):
                                        for k in range(8):
                                            ins = t.matmul(ps[:, :], RING[:, k_, k * 256 + j * 128:k * 256 + j * 128 + 128],
                                                           H[:, k, tsl(tt)], start=(k == 0), stop=(k == 7))
                                        return ins
                                    sc.op("pe", mm, reads=[rb] + [Hb[k][tt] for k in range(8)], writes=[pb])
                                i = sg_rr.next()
                                sc.op("act", lambda a, i=i, gps=gps: a.activation(out=SG[i][:, :], in_=gps[:, :], func=AF.Silu),
                                      reads=[gpb], writes=[SGb[i]])
                                sc.op("dve", lambda v, i=i, ups=ups, hl=hl, tt=tt: v.tensor_tensor(
                                    out=ACTT[:, hl, tsl(tt)], in0=SG[i][:, :], in1=ups[:, :], op=ALU.mult),
                                    reads=[SGb[i], upb], writes=[actb[hl][tt]])
                    for u in grp:
                        sl["d", u] = wload(("rows", n_down, l, u * 256, 256), 2048)
                    nh = 2 * len(grp)
                    for tt in range(NT):
                        for oc in range(8):
                            dps, dpb = bank(pD.next())

                            def mm(t, dps=dps, tt=tt, oc=oc):
                                for hl in range(nh):
                                    k_ = sl["d", grp[hl // 2]][1]
                                    j = hl % 2
                                    ins = t.matmul(dps[:, :], RING[:, k_, j * 1024 + oc * 128:j * 1024 + oc * 128 + 128],
                                                   ACTT[:, hl, tsl(tt)], start=(hl == 0), stop=(hl == nh - 1))
                                return ins
                            sc.op("pe", mm, reads=[sl["d", u][2] for u in grp] + [actb[hl][tt] for hl in range(nh)], writes=[dpb])
                            sc.op("dve", lambda v, dps=dps, oc=oc, tt=tt: v.scalar_tensor_tensor(
                                out=X[:, oc, tsl(tt)], in0=dps[:, :], scalar=0.5, in1=X[:, oc, tsl(tt)], op0=ALU.mult, op1=ALU.add),
                                reads=[dpb], writes=[Xb[oc][tt]])
                        if grp is groups[-1]:
                            if tt > 0:
                                post(tt - 1)
                            if tt == NT - 1:
                                post(tt)

        def xattn(l, post):
            pA = BankPool([0, 1, 2])
            pB = BankPool([3, 4, 5, 6])
            pC = BankPool([7])
            with ExitStack() as s2:
                T2 = lambda name, shape, dt: s2.enter_context(nc.sbuf_tensor(uniq(name), shape, dt))
                KT = T2("KT", [128, 8, NMEM], BF16)
                VM = T2("VM", [128, 2, D], BF16)
                ktb = [Buf() for _ in range(8)]
                vmb = Buf()
                QT = [T2("QT%d" % i, [128, 8, TT], BF16) for i in range(2)]
                OT = [T2("OT%d" % i, [128, 8, TT], BF16) for i in range(2)]
                PT = [T2("XPT%d" % i, [128, TT], BF16) for i in range(4)]
                REC = [T2("XREC%d" % i, [128, TT], F32) for i in range(2)]
                qtb = [[Buf() for _ in range(8)] for _ in range(2)]
                otb = [[Buf() for _ in range(8)] for _ in range(2)]
                ptb = [Buf() for _ in range(4)]
                recb = [Buf() for _ in range(2)]
                fresh(ktb, vmb, qtb, otb, ptb, recb)
                for oc2 in range(4):
                    _, k_, rb = wload(("kxn", "xattn_w_kv", l, oc2 * 256, 256), 2048)
                    for jj in range(2):
                        oc = oc2 * 2 + jj
                        ps, pb = bank(pA.next())

                        def mm(t, ps=ps, k_=k_, jj=jj):
                            for k in range(8):
                                ins = t.matmul(ps[:, 0:NMEM], RING[:, k_, k * 256 + jj * 128:k * 256 + jj * 128 + 128],
                                               MN[:, k, :], start=(k == 0), stop=(k == 7))
                            return ins
                        sc.op("pe", mm, reads=[rb] + mnb, writes=[pb])
                        sc.op("act", lambda a, ps=ps, oc=oc: a.activation(out=KT[:, oc, :], in_=ps[:, 0:NMEM], func=AF.Copy),
                              reads=[pb], writes=[ktb[oc]])
                for u in range(4):
                    _, k_, rb = wload(("kxn", "xattn_w_kv", l, 1024 + u * 256, 256), 2048)
                    for mt in range(2):
                        ps, pb = bank(pA.next())

                        def mm(t, ps=ps, k_=k_, mt=mt):
                            for k in range(8):
                                ins = t.matmul(ps[:, 0:256], MN[:, k, mt * 128:(mt + 1) * 128],
                                               RING[:, k_, k * 256:(k + 1) * 256], start=(k == 0), stop=(k == 7))
                            return ins
                        sc.op("pe", mm, reads=[rb] + mnb, writes=[pb])
                        sc.op("dve", lambda v, ps=ps, mt=mt, u=u: v.tensor_copy(out=VM[:, mt, u * 256:(u + 1) * 256], in_=ps[:, 0:256]),
                              reads=[pb], writes=[vmb])
                wq = [wload(("kxn", "xattn_w_q", l, oc2 * 256, 256), 2048) for oc2 in range(4)]
                wo = [wload(("kxn", "xattn_w_o", l, oc2 * 256, 256), 2048) for oc2 in range(4)]

                def stageA(tt):
                    q = tt % 2
                    for oc in range(8):
                        ps, pb = bank(pA.next())
                        _, k_, rb = wq[oc // 2]

                        def mm(t):
                            for k in range(8):
                                ins = t.matmul(ps[:, :], RING[:, k_, k * 256 + (oc % 2) * 128:k * 256 + (oc % 2) * 128 + 128],
                                               H[:, k, tsl(tt)], start=(k == 0), stop=(k == 7))
                            return ins
                        sc.op("pe", mm, reads=[rb] + [Hb[k][tt] for k in range(8)], writes=[pb])
                        if oc % 2 == 0:
                            sc.op("act", lambda a: a.activation(out=QT[q][:, oc, :], in_=ps[:, :], func=AF.Copy),
                                  reads=[pb], writes=[qtb[q][oc]])
                        else:
                            sc.op("dve", lambda v: v.tensor_copy(out=QT[q][:, oc, :], in_=ps[:, :]),
                                  reads=[pb], writes=[qtb[q][oc]])

                def stageB(tt):
                    q = tt % 2

                    def s_part(h):
                        for mt in range(2):
                            ps, pb = bank(pB.next())

                            def mm(t):
                                for kc in range(2):
                                    ins = t.matmul(ps[:, :], KT[:, 2 * h + kc, mt * 128:(mt + 1) * 128], QT[q][:, 2 * h + kc, :],
                                                   start=(kc == 0), stop=(kc == 1))
                                return ins
                            sc.op("pe", mm, reads=[ktb[2 * h], ktb[2 * h + 1], qtb[q][2 * h], qtb[q][2 * h + 1]], writes=[pb])
                            pi = (h % 2) * 2 + mt
                            sc.op("act", lambda a: a.activation(out=PT[pi][:, :], in_=ps[:, :], func=AF.Exp, scale=1.0 / 16.0),
                                  reads=[pb], writes=[ptb[pi]])

                    def v_part(h):
                        pts = [(h % 2) * 2, (h % 2) * 2 + 1]
                        dps, dpb = bank(pC.next())

                        def mm(t):
                            for mt in range(2):
                                ins = t.matmul(dps[:, :], ONES[:, :], PT[pts[mt]][:, :], start=(mt == 0), stop=(mt == 1))
                            return ins
                        sc.op("pe", mm, reads=[ptb[p] for p in pts], writes=[dpb])
                        ri = h % 2
                        sc.op("act", lambda a: a.activation(out=REC[ri][:, :], in_=dps[:, :], func=AF.Ln), reads=[dpb], writes=[recb[ri]])
                        sc.op("act", lambda a: a.activation(out=REC[ri][:, :], in_=REC[ri][:, :], func=AF.Exp, scale=-1.0), reads=[recb[ri]], writes=[recb[ri]])
                        for dc in range(2):
                            ops_, opb = bank(pA.next())

                            def mm2(t):
                                for mt in range(2):
                                    ins = t.matmul(ops_[:, :], VM[:, mt, h * 256 + dc * 128:h * 256 + dc * 128 + 128], PT[pts[mt]][:, :],
                                                   start=(mt == 0), stop=(mt == 1))
                                return ins
                            sc.op("pe", mm2, reads=[vmb] + [ptb[p] for p in pts], writes=[opb])
                            sc.op("dve", lambda v: v.tensor_tensor(out=OT[q][:, 2 * h + dc, :], in0=ops_[:, :], in1=REC[ri][:, :], op=ALU.mult),
                                  reads=[opb, recb[ri]], writes=[otb[q][2 * h + dc]])
                    s_part(0)
                    for h in range(4):
                        if h + 1 < 4:
                            s_part(h + 1)
                        v_part(h)

                def stageC(tt):
                    q = tt % 2
                    for oc in range(8):
                        ps, pb = bank(pA.next())
                        _, k_, rb = wo[oc // 2]

                        def mm(t):
                            for k in range(8):
                                ins = t.matmul(ps[:, :], RING[:, k_, k * 256 + (oc % 2) * 128:k * 256 + (oc % 2) * 128 + 128],
                                               OT[q][:, k, :], start=(k == 0), stop=(k == 7))
                            return ins
                        sc.op("pe", mm, reads=[rb] + otb[q], writes=[pb])
                        sc.op("dve", lambda v: v.tensor_tensor(out=X[:, oc, tsl(tt)], in0=ps[:, :], in1=X[:, oc, tsl(tt)], op=ALU.add),
                              reads=[pb], writes=[Xb[oc][tt]])
                    if tt > 0:
                        post(tt - 1)
                    if tt == NT - 1:
                        post(tt)
                stageA(0)
                stageA(1)
                stageB(0)
                stageA(2)
                stageB(1)
                stageC(0)
                stageA(3)
                stageB(2)
                stageC(1)
                stageB(3)
                stageC(2)
                stageC(3)

        def mixer(l, post):
            with ExitStack() as s2:
                T2 = lambda name, shape, dt: s2.enter_context(nc.sbuf_tensor(uniq(name), shape, dt))
                YA = T2("YA", [128, 4, S], BF16)
                yab = [[Buf() for _ in range(NT)] for _ in range(4)]
                with ExitStack() as s3:
                    T3 = lambda name, shape, dt: s3.enter_context(nc.sbuf_tensor(uniq(name), shape, dt))
                    C8B = T3("C8B", [8, S], BF16)
                    NEGC = T3("NEGC", [128, 16 * 8], F32)
                    c8b_b = Buf()
                    negc_b = Buf()
                    pF = BankPool([0, 1])
                    with ExitStack() as s4:
                        T4 = lambda name, shape, dt: s4.enter_context(nc.sbuf_tensor(uniq(name), shape, dt))
                        FC = T4("FC", [8, S], F32)
                        ONES8 = T4("ONES8", [8, S], F32)
                        fcb = Buf()
                        o8b = Buf()
                        fresh(yab, c8b_b, negc_b, fcb, o8b)
                        sc.op("dve", lambda v: v.memset(ONES8[:, :], 1.0), writes=[o8b])
                        _, k_, rb = wload(("kxn", "w_in", l, 1536, 8), 64)
                        for tt in range(NT):
                            ps, pb = bank(pF.next())

                            def mm(t, ps=ps, tt=tt):
                                for k in range(8):
                                    ins = t.matmul(ps[0:8, :], RING[:, k_, k * 8:(k + 1) * 8], H[:, k, tsl(tt)], start=(k == 0), stop=(k == 7))
                                return ins
                            sc.op("pe", mm, reads=[rb] + [Hb[k][tt] for k in range(8)], writes=[pb])
                            bcol = pvcol(l, "b_f")
                            sc.op("act", lambda a, ps=ps, tt=tt: a.activation(out=FC[:, tsl(tt)], in_=ps[0:8, :], func=AF.Sigmoid,
                                                                             bias=PV[0:8, bcol:bcol + 1]),
                                  reads=[pb, cb], writes=[fcb])
                        sc.op("act", lambda a: a.activation(out=FC[:, :], in_=FC[:, :], func=AF.Ln), reads=[fcb], writes=[fcb])
                        sc.op("dve", lambda v: v.tensor_tensor_scan(out=FC[:, :], data0=ONES8[:, :], data1=FC[:, :], initial=0.0,
                                                                    op0=ALU.mult, op1=ALU.add), reads=[fcb, o8b], writes=[fcb])
                        sc.op("dve", lambda v: v.tensor_scalar(out=C8B[:, :], in0=FC[:, :], scalar1=8.0, scalar2=None, op0=ALU.mult),
                              reads=[fcb], writes=[c8b_b])
                        ps, pb = bank(pF.next())

                        def mmT(t):
                            for kb in range(16):
                                ins = t.transpose(ps[:, kb * 8:(kb + 1) * 8], FC[0:8, kb * 128:(kb + 1) * 128], ID8[0:8, 0:8])
                            return ins
                        sc.op("pe", mmT, reads=[fcb, cb], writes=[pb])
                        sc.op("dve", lambda v: v.tensor_scalar(out=NEGC[:, :], in0=ps[:, 0:128], scalar1=-1.0, scalar2=None, op0=ALU.mult),
                              reads=[pb], writes=[negc_b])
                    QA = T3("QA", [128, 2, S], BF16)
                    KA = T3("KA", [128, 2, S], BF16)
                    VP = T3("VP", [128, 16, 192], BF16)
                    PT = [T3("PT%d" % i, [128, TT], BF16) for i in range(4)]
                    qab = [[Buf() for _ in range(NT)] for _ in range(2)]
                    kab = [[Buf() for _ in range(NT)] for _ in range(2)]
                    qmb = [Buf() for _ in range(2)]
                    k1b = Buf()
                    vpb = [Buf() for _ in range(4)]
                    v1b = Buf()
                    ptb = [Buf() for _ in range(4)]
                    REC = RS
                    recb = RSb
                    dq = [sc.dsem() for _ in range(2)]
                    fresh(qab, kab, qmb, k1b, vpb, v1b, ptb)
                    sc.op("dve", lambda v: v.memset(KA[64:65, :, :], 1.0), writes=[k1b])
                    sc.op("dve", lambda v: v.memset(VP[:, :, 64:128], 1.0), writes=[v1b])
                    pP = BankPool([0, 1])
                    pS = BankPool([2, 3, 4, 5])
                    pACC = BankPool([6, 7])
                    pt_rr = BankPool([0, 1, 2, 3])
                    rec_rr = rs_rr
                    for hp in range(4):
                        _, kqk, rqk = wload(("kxn2", "w_in", l, hp * 128, 512 + hp * 128, 128), 2048)
                        _, kv, rv = wload(("kxn", "w_in", l, 1024 + hp * 128, 128), 1024)
                        for tt in range(NT):
                            for (dst, dstb, off) in ((QA, qab, 0), (KA, kab, 128)):
                                ps, pb = bank(pP.next())

                                def mm(t, ps=ps, off=off, tt=tt):
                                    for k in range(8):
                                        ins = t.matmul(ps[:, :], RING[:, kqk, k * 256 + off:k * 256 + off + 128], H[:, k, tsl(tt)],
                                                       start=(k == 0), stop=(k == 7))
                                    return ins
                                sc.op("pe", mm, reads=[rqk] + [Hb[k][tt] for k in range(8)], writes=[pb])
                                sc.op("dve", lambda v, ps=ps, dst=dst, tt=tt: v.tensor_copy(out=dst[0:64, 0, tsl(tt)], in_=ps[0:64, :]),
                                      reads=[pb], writes=[dstb[0][tt]])
                                sc.op("dve", lambda v, ps=ps, dst=dst, tt=tt: v.tensor_copy(out=dst[0:64, 1, tsl(tt)], in_=ps[64:128, :]),
                                      reads=[pb], writes=[dstb[1][tt]])
                        for hl in range(2):
                            h = 2 * hp + hl
                            sc.dma("sp", dq[hl], QA[64:65, hl, :], C8B[h:h + 1, :], reads=[c8b_b], writes=[qmb[hl]])
                        def vproj(tb4):
                            ps, pb = bank(pP.next())

                            def mm(t, ps=ps, tb4=tb4):
                                for q4 in range(4):
                                    tb = tb4 * 4 + q4
                                    for k in range(8):
                                        ins = t.matmul(ps[:, q4 * 128:(q4 + 1) * 128], H[:, k, tb * 128:(tb + 1) * 128],
                                                       RING[:, kv, k * 128:(k + 1) * 128], start=(k == 0), stop=(k == 7))
                                return ins
                            sc.op("pe", mm, reads=[rv] + [Hb[k][tb4] for k in range(8)], writes=[pb])
                            psv = ps[:, :].rearrange("p (a b) -> p a b", b=128)
                            sc.op("dve", lambda v, psv=psv, tb4=tb4: v.tensor_copy(out=VP[:, tb4 * 4:(tb4 + 1) * 4, 0:64], in_=psv[:, :, 0:64]),
                                  reads=[pb], writes=[vpb[tb4]])
                            sc.op("dve", lambda v, psv=psv, tb4=tb4: v.tensor_copy(out=VP[:, tb4 * 4:(tb4 + 1) * 4, 128:192], in_=psv[:, :, 64:128]),
                                  reads=[pb], writes=[vpb[tb4]])
                        items = []
                        for hl in range(2):
                            for T_ in range(NT):
                                for kb in range(4 * (T_ + 1)):
                                    items.append((hl, T_, kb))
                        LA = 3
                        state = {}

                        def stageS(it):
                            hl, T_, kb = it
                            h = 2 * hp + hl
                            d = kb - 4 * T_
                            c0 = 128 * d if d > 0 else 0
                            ps, pb = bank(pS.next())

                            def mm(t):
                                if d >= 0:
                                    t.matmul(ps[:, c0:c0 + 128], IDB, maskneg(0)[:, 0:128], start=True, stop=False, skip_group_check=True)
                                    return t.matmul(ps[:, c0:TT], KA[0:65, hl, kb * 128:(kb + 1) * 128],
                                                    QA[0:65, hl, T_ * TT + c0:(T_ + 1) * TT], start=False, stop=True, skip_group_check=True)
                                return t.matmul(ps[:, c0:TT], KA[0:65, hl, kb * 128:(kb + 1) * 128],
                                                QA[0:65, hl, T_ * TT + c0:(T_ + 1) * TT], start=True, stop=True)
                            sc.op("pe", mm, reads=[kab[hl][kb // 4], k1b, qab[hl][T_], qmb[hl], cb, cb2], writes=[pb])
                            pi = pt_rr.next()
                            sc.op("act", lambda a: a.activation(out=PT[pi][:, c0:TT], in_=ps[:, c0:TT], func=AF.Exp, scale=0.125,
                                                                bias=NEGC[:, kb * 8 + h:kb * 8 + h + 1]),
                                  reads=[pb, negc_b], writes=[ptb[pi]])
                            state[it] = (pi, c0)

                        def stageV(it):
                            hl, T_, kb = it
                            nkb = 4 * (T_ + 1)
                            pi, c0 = state.pop(it)
                            if kb == 0:
                                state["acc", hl, T_] = bank(pACC.next())
                            acc, accb = state["acc", hl, T_]
                            sc.op("pe", lambda t: t.matmul(acc[:, c0:TT], VP[:, kb, hl * 64:hl * 64 + 128], PT[pi][:, c0:TT],
                                                           start=(kb == 0), stop=(kb == nkb - 1)),
                                  reads=[vpb[kb // 4], v1b, ptb[pi]], writes=[accb])
                            if kb == nkb - 1:
                                ri = rec_rr.next()
                                num = slice(0, 64) if hl == 0 else slice(64, 128)
                                den = slice(64, 128) if hl == 0 else slice(0, 64)
                                sc.op("dve", lambda v: v.reciprocal(out=REC[ri][num, :], in_=acc[den, :]), reads=[accb], writes=[recb[ri]])
                                sc.op("dve", lambda v: v.tensor_tensor(out=YA[num, hp, tsl(T_)], in0=acc[num, :], in1=REC[ri][num, :], op=ALU.mult),
                                      reads=[accb, recb[ri]], writes=[yab[hp][T_]])
                                del state["acc", hl, T_]

                        for i in range(len(items) + LA):
                            if i < len(items):
                                stageS(items[i])
                            if i == LA - 1:
                                for tb4 in range(4):
                                    vproj(tb4)
                            if i >= LA:
                                stageV(items[i - LA])
                def rmsA(tt):
                    rms_tile(lambda c: YA[:, :, tsl(tt)] if c is None else YA[:, c, tsl(tt)],
                             [yab[c][tt] for c in range(4)],
                             lambda c: YA[:, c, tsl(tt)], [yab[c][tt] for c in range(4)], 4, TT, pvcol(l, "attn_out_g"), ONESM5)
                YC = T2("YC", [128, 4, S], BF16)
                UBUF = [T2("UBUF%d" % i, [128, 2, 32 + TT], BF16) for i in range(2)]
                HALO = T2("HALO", [128, 4, 2, 30], BF16)
                CBS = [T2("CB%d" % i, [128, 4, TT], F32) for i in range(2)]
                ycb = [[Buf() for _ in range(NT)] for _ in range(4)]
                ubb = [Buf() for _ in range(2)]
                halob = [Buf() for _ in range(4)]
                cbbs = [[Buf() for _ in range(4)] for _ in range(2)]
                pA = BankPool([0, 1, 2, 3])
                pCv = BankPool([4, 5])
                pM = BankPool([6])
                fresh(ycb, ubb, halob, cbbs)
                sc.op("dve", lambda v: v.memset(HALO[:, :, :, :], 0.0), writes=halob)
                for b_ in range(2):
                    sc.op("dve", lambda v, b_=b_: v.memset(UBUF[b_][:, :, :], 0.0), writes=[ubb[b_]])
                wag = [wload(("kxn2", "w_in", l, 1544 + cc * 128, 2056 + cc * 128, 128), 2048) for cc in range(4)]
                dgs = [ring_take() for _ in range(2)]
                cw2 = pvcol(l, "conv_w2")
                cbias = pvcol(l, "conv_b")
                lg = pvcol(l, "ln_g")
                lb = pvcol(l, "ln_b")
                E3 = EID.unsqueeze(1).to_broadcast([128, 16, 64])

                def projA(it):
                    tt, cc = it // 4, it % 4
                    b_ = it % 2
                    _, k_, rb = wag[cc]
                    aps, apb = bank(pA.next())
                    gps, gpb = bank(pA.next())
                    for (ps, pb, off) in ((aps, apb, 0), (gps, gpb, 128)):
                        def mm(t, ps=ps, off=off):
                            for k in range(8):
                                ins = t.matmul(ps[:, :], RING[:, k_, k * 256 + off:k * 256 + off + 128], H[:, k, tsl(tt)],
                                               start=(k == 0), stop=(k == 7))
                            return ins
                        sc.op("pe", mm, reads=[rb] + [Hb[k][tt] for k in range(8)], writes=[pb])
                    i = sg_rr.next()
                    sc.op("act", lambda a: a.activation(out=SG[i][:, :], in_=gps[:, :], func=AF.Sigmoid), reads=[gpb], writes=[SGb[i]])
                    sc.op("dve", lambda v: v.tensor_copy(out=UBUF[b_][:, :, 0:30], in_=HALO[:, cc, :, :]), reads=[halob[cc]], writes=[ubb[b_]])
                    for h in range(2):
                        hs = slice(64 * h, 64 * h + 64)
                        sc.op("dve", lambda v, h=h, hs=hs: v.tensor_tensor(out=UBUF[b_][0:64, h, 30:30 + TT], in0=SG[i][hs, :], in1=aps[hs, :], op=ALU.mult),
                              reads=[SGb[i], apb], writes=[ubb[b_]])
                        sc.op("dve", lambda v, h=h, hs=hs: v.tensor_tensor(out=UBUF[b_][64:128, h, 29:29 + TT], in0=SG[i][hs, :], in1=aps[hs, :], op=ALU.mult),
                              reads=[SGb[i], apb], writes=[ubb[b_]])
                    kd, bd = dgs[b_]
                    for h in range(2):
                        w0 = cw2 + (cc * 2 + h) * 16
                        sc.op("pool", lambda v, h=h, w0=w0: v.tensor_tensor(
                            out=RING[:, kd, h * 1024:(h + 1) * 1024].rearrange("p (r c) -> p r c", c=64), in0=E3,
                            in1=PV[:, w0:w0 + 16].unsqueeze(2).to_broadcast([128, 16, 64]), op=ALU.mult),
                            reads=[cb, cb2], writes=[bd])

                def convB(it):
                    tt, cc = it // 4, it % 4
                    b_ = it % 2
                    kd, bd = dgs[b_]
                    cps, cpb = bank(pCv.next())

                    def mm(t):
                        for r in range(16):
                            for h in range(2):
                                ins = t.matmul(cps[64 * h:64 * h + 64, :], RING[:, kd, (h * 16 + r) * 64:(h * 16 + r) * 64 + 64],
                                               UBUF[b_][:, h, 2 * r:2 * r + TT], start=(r == 0), stop=(r == 15), tile_position=(0, 64 * h))
                        return ins
                    sc.op("pe", mm, reads=[bd, ubb[b_]], writes=[cpb])
                    CB = CBS[tt % 2]
                    cbb = cbbs[tt % 2]
                    sc.op("act", lambda a: a.activation(out=CB[:, cc, :], in_=cps[:, :], func=AF.Identity, bias=PV[:, cbias + cc:cbias + cc + 1]),
                          reads=[cpb, cb], writes=[cbb[cc]])
                    if tt < NT - 1:
                        sc.op("dve", lambda v: v.tensor_copy(out=HALO[:, cc, :, :], in_=UBUF[b_][:, :, TT:TT + 30]), reads=[ubb[b_]], writes=[halob[cc]])

                def lnC(tt):
                    CB = CBS[tt % 2]
                    cbb = cbbs[tt % 2]
                    st_ = {}
                    rp = rms_parts(lambda c: CB[:, :, :] if c is None else CB[:, c, :], cbb,
                                   lambda c: YC[:, c, tsl(tt)], [ycb[c][tt] for c in range(4)], 4, TT, pvcol(l, "conv_out_g"), ONESM5)

                    def s1():
                        mps, mpb = bank(pM.next())
                        st_["m"] = (mps, mpb)

                        def mmm(t):
                            for cc in range(4):
                                ins = t.matmul(mps[:, :], ONESF5[:, :], CB[:, cc, :], start=(cc == 0), stop=(cc == 3))
                            return ins
                        sc.op("pe", mmm, reads=cbb + [cb], writes=[mpb])

                    def s2():
                        mps, mpb = st_["m"]
                        for cc in range(4):
                            sc.op("dve", lambda v, cc=cc: v.tensor_tensor(out=CB[:, cc, :], in0=CB[:, cc, :], in1=mps[:, :], op=ALU.subtract),
                                  reads=[cbb[cc], mpb], writes=[cbb[cc]])
                        sc.op("act", lambda a: a.activation(out=SQ[:, 0:4, :], in_=CB[:, :, :], func=AF.Square), reads=cbb, writes=[SQb])

                    def s3():
                        vps, vpb_ = bank(pM.next())
                        st_["v"] = (vps, vpb_)

                        def mmv(t):
                            for cc in range(4):
                                ins = t.matmul(vps[:, :], ONESM5[:, :], SQ[:, cc, :], start=(cc == 0), stop=(cc == 3))
                            return ins
                        sc.op("pe", mmv, reads=[SQb, cb], writes=[vpb_])
                        sc.op("act", lambda a: a.activation(out=vps[:, :], in_=vps[:, :], func=AF.Ln, bias=EPSC[:, 0:1]),
                              reads=[vpb_, cb], writes=[vpb_])
                        sc.op("act", lambda a: a.activation(out=vps[:, :], in_=vps[:, :], func=AF.Exp, scale=-0.5), reads=[vpb_], writes=[vpb_])

                    def s4():
                        vps, vpb_ = st_["v"]
                        for cc in range(4):
                            sc.op("dve", lambda v, cc=cc: v.scalar_tensor_tensor(out=CB[:, cc, :], in0=CB[:, cc, :], scalar=PV[:, lg + cc:lg + cc + 1],
                                                                                  in1=vps[:, :], op0=ALU.mult, op1=ALU.mult),
                                  reads=[cbb[cc], vpb_, cb], writes=[cbb[cc]])
                            sc.op("act", lambda a, cc=cc: a.activation(out=CB[:, cc, :], in_=CB[:, cc, :], func=AF.Silu, bias=PV[:, lb + cc:lb + cc + 1]),
                                  reads=[cbb[cc], cb], writes=[cbb[cc]])
                        rp[0]()

                    def s5():
                        rp[1]()
                        rp[2]()
                    return [s1, s2, s3, s4, s5]

                NIT = 4 * NT
                pending = []
                wout = []
                projA(0)
                for it in range(NIT):
                    if it + 1 < NIT:
                        projA(it + 1)
                        if it + 1 >= NIT - 4:
                            wout.append(wload(("kxn", "w_out", l, (it + 1 - (NIT - 4)) * 256, 256), 2048))
                    convB(it)
                    if it % 4 == 3:
                        pending.append(lnC(it // 4))
                    for p in pending:
                        if p:
                            p.pop(0)()
                    if it % 4 == 1:
                        rmsA(it // 4)
                assert len(wout) == 4
                pO = BankPool([0, 1, 2, 3])

                def wout_tile(tt, do_post=True):
                    for oc in range(8):
                        ps, pb = bank(pO.next())
                        _, k_, rb = wout[oc // 2]

                        def mm(t):
                            for k in range(8):
                                rhs = YA[:, k, tsl(tt)] if k < 4 else YC[:, k - 4, tsl(tt)]
                                ins = t.matmul(ps[:, :], RING[:, k_, k * 256 + (oc % 2) * 128:k * 256 + (oc % 2) * 128 + 128], rhs,
                                               start=(k == 0), stop=(k == 7))
                            return ins
                        sc.op("pe", mm, reads=[rb] + [yab[c][tt] for c in range(4)] + [ycb[c][tt] for c in range(4)], writes=[pb])
                        sc.op("dve", lambda v: v.tensor_tensor(out=X[:, oc, tsl(tt)], in0=ps[:, :], in1=X[:, oc, tsl(tt)], op=ALU.add),
                              reads=[pb], writes=[Xb[oc][tt]])
                    if do_post:
                        if tt > 0:
                            post(tt - 1)
                        if tt == NT - 1:
                            post(tt)
                tdone = 0
                owed = []
                while any(pending):
                    nleft = max(len(p) for p in pending)
                    for p in pending:
                        if p:
                            p.pop(0)()
                    if tdone < NT - 1:
                        if nleft > 2:
                            wout_tile(tdone)
                        else:
                            wout_tile(tdone, do_post=False)
                            if tdone > 0:
                                owed.append(tdone - 1)
                        tdone += 1
                for tt in owed:
                    post(tt)
                for tt in range(tdone, NT):
                    wout_tile(tt)

        stg = stages if stages is not None else "f1,mix,xa,f2"
        stg = stg.split(",")
        gain = {"f1": "ffn1_g", "mix": "mix_g", "xa": "xattn_g", "f2": "ffn2_g", "f3": "ffn1_g"}
        calls = [(name, l) for l in range(L) for name in ("f1", "mix", "xa", "f2", "f3") if name in stg]
        rmsnorm_XH(pvcol(calls[0][1], gain[calls[0][0]]))

        def run(name, l, post):
            if name == "f1" or name == "f3":
                ffn(l, 1, post, mem_prologue=True)
            elif name == "f2":
                ffn(l, 2, post)
            elif name == "mix":
                mixer(l, post)
            elif name == "xa":
                xattn(l, post)
        for i, (name, l) in enumerate(calls[:-1]):
            nn, nl = calls[i + 1]
            run(name, l, lambda tt, g=pvcol(nl, gain[nn]): rms_X_tile(tt, g))
        with nc.sbuf_tensor("OB", [128, 8, TT], F32) as OB:
            obb = [Buf() for _ in range(8)]
            d_out = sc.dsem()
            fresh(obb)

            def final_tile(tt):
                rms_tile(lambda c: X[:, :, tsl(tt)] if c is None else X[:, c, tsl(tt)],
                         [Xb[c][tt] for c in range(8)],
                         lambda c: OB[:, c, :], obb, 8, TT, FINAL_G, ONESM)
                sc.dma("sp", d_out, oT_v[:, :, tsl(tt)], OB[:, :, :], reads=obb)
            run(calls[-1][0], calls[-1][1], final_tile)
            nc.sync.wait_ge(d_out.sem, d_out.n)
    return nc, plan


def _pack_units(inp, plan):
    out = np.zeros((len(plan), 128, SLOT), np.float32)
    for u, d in enumerate(plan):
        kind = d[0]
        if kind == "kxn":
            _, name, l, c0, n = d
            w = inp[name][l][:, c0:c0 + n]
            out[u, :, :8 * n] = w.reshape(8, 128, n).transpose(1, 0, 2).reshape(128, 8 * n)
        elif kind == "kxn2":
            _, name, l, c0, c1, n = d
            w = np.concatenate([inp[name][l][:, c0:c0 + n], inp[name][l][:, c1:c1 + n]], axis=1)
            out[u, :, :16 * n] = w.reshape(8, 128, 2 * n).transpose(1, 0, 2).reshape(128, 16 * n)
        elif kind == "rows":
            _, name, l, r0, n = d
            w = inp[name][l][r0:r0 + n, :]
            out[u, :, :] = w.reshape(2, 128, 1024).transpose(1, 0, 2).reshape(128, 2048)
    return out


def _pack_pvec(inp):
    pv = np.zeros((128, NPV), np.float32)
    c8 = lambda v: v.reshape(8, 128).T
    c4 = lambda v: v.reshape(4, 128).T
    for l in range(L):
        pv[:, pvcol(l, "ffn1_g"):pvcol(l, "ffn1_g") + 8] = c8(inp["ffn1_norm_g"][l])
        pv[:, pvcol(l, "mix_g"):pvcol(l, "mix_g") + 8] = c8(inp["mix_norm_g"][l])
        pv[:, pvcol(l, "xattn_g"):pvcol(l, "xattn_g") + 8] = c8(inp["xattn_norm_g"][l])
        pv[:, pvcol(l, "mem_g"):pvcol(l, "mem_g") + 8] = c8(inp["mem_norm_g"][l])
        pv[:, pvcol(l, "ffn2_g"):pvcol(l, "ffn2_g") + 8] = c8(inp["ffn2_norm_g"][l])
        cw = inp["conv_w"][l]
        cwp = np.concatenate([cw, np.zeros((1, cw.shape[1]), cw.dtype)], axis=0)
        pp = np.arange(128)
        for cc in range(4):
            for h in range(2):
                for r in range(16):
                    col = pvcol(l, "conv_w2") + (cc * 2 + h) * 16 + r
                    pv[:, col] = cwp[2 * r + pp // 64, cc * 128 + 64 * h + pp % 64]
        pv[:, pvcol(l, "conv_b"):pvcol(l, "conv_b") + 4] = c4(inp["conv_b"][l])
        pv[:, pvcol(l, "ln_g"):pvcol(l, "ln_g") + 4] = c4(inp["conv_ln_g"][l])
        pv[:, pvcol(l, "ln_b"):pvcol(l, "ln_b") + 4] = c4(inp["conv_ln_b"][l])
        pv[:, pvcol(l, "conv_out_g"):pvcol(l, "conv_out_g") + 4] = c4(inp["conv_out_g"][l])
        pv[:, pvcol(l, "attn_out_g"):pvcol(l, "attn_out_g") + 4] = c4(inp["attn_out_g"][l])
        pv[0:8, pvcol(l, "b_f")] = inp["b_f"][l]
    pv[:, FINAL_G:FINAL_G + 8] = c8(inp["final_norm_g"])
    return pv


def _consts():
    c = np.zeros((128, NCST), np.float32)
    c[:, 0:128] = np.eye(128, dtype=np.float32)
    p = np.arange(128)[:, None]
    f = np.arange(512)[None, :]
    c[:, 128:256] = np.where(f[:, 0:128] >= p, 0.0, MASKNEG)
    c[:, 256:320] = (np.arange(128)[:, None] % 64 == np.arange(64)[None, :]).astype(np.float32)
    return c


_CACHE = {}


def kernel(**inputs):
    inp = {k: np.asarray(v) for k, v in inputs.items()}
    stages = os.environ.get("MK_STAGES")
    key = stages
    if key not in _CACHE:
        _CACHE[key] = build(stages)
    nc, plan = _CACHE[key]
    assert len(plan) <= NU, (len(plan), NU)
    wst = np.zeros((NU, 128, SLOT), np.float32)
    wst[:len(plan)] = _pack_units(inp, plan)
    pv = _pack_pvec(inp)
    cst = _consts()
    B = inp["x"].shape[0]
    in_maps = []
    for b in range(B):
        in_maps.append({
            "xT": np.ascontiguousarray(inp["x"][b].T),
            "memT": np.ascontiguousarray(inp["mem"][b].T),
            "wst": wst, "pvec": pv, "cst": cst,
        })
    res = run_bass_kernel_spmd(nc, in_maps, core_ids=list(range(B)))
    out = np.stack([np.ascontiguousarray(r["oT"].T) for r in res.results], axis=0)
    return out.astype(np.float32)
```

```python
import os
from contextlib import ExitStack
import numpy as np
import concourse.bass as bass
import concourse.mybir as mybir
from concourse.bass_utils import run_bass_kernel_spmd

F32 = mybir.dt.float32
BF16 = mybir.dt.bfloat16
AF = mybir.ActivationFunctionType
ALU = mybir.AluOpType

S = 2048
D = 1024
DFF = 2816
NMEM = 256
L = 2
EPS = 1e-6
NSLOT = 8
SLOT = 2048
NT = 4
TT = 512
NCST = 128 + 128 + 64
MASKNEG = -30000.0

_PVL = [("ffn1_g", 8), ("mix_g", 8), ("xattn_g", 8), ("mem_g", 8), ("ffn2_g", 8), ("conv_w2", 128),
        ("conv_b", 4), ("ln_g", 4), ("ln_b", 4), ("conv_out_g", 4), ("attn_out_g", 4), ("b_f", 1)]
_PVO = {}
_o = 0
for _n, _w in _PVL:
    _PVO[_n] = _o
    _o += _w
PL = _o
NPV = L * PL + 8


def pvcol(l, name):
    return l * PL + _PVO[name]


FINAL_G = L * PL


FFN_NAMES = {1: ("ffn1_w_gate", "ffn1_w_up", "ffn1_w_down"), 2: ("ffn2_w_gate", "ffn2_w_up", "ffn2_w_down")}


class Buf:
    __slots__ = ("w", "r", "excl")

    def __init__(self, excl=False):
        self.w = None
        self.r = {}
        self.excl = excl


class DSem:
    def __init__(self, sem):
        self.sem = sem
        self.n = 0


class Eng:
    def __init__(self, name, h, sem):
        self.name = name
        self.h = h
        self.sem = sem
        self.n = 0
        self.known = {}


class PEProxy:
    def __init__(self, h, wait):
        self.h = h
        self.wait = wait

    def _first(self, ins):
        if self.wait is not None:
            ins._wait_ge(self.wait[0], self.wait[1])
            self.wait = None
        return ins

    def matmul(self, *a, **k):
        return self._first(self.h.matmul(*a, **k))

    def transpose(self, *a, **k):
        return self._first(self.h.transpose(*a, **k))


class Sched:
    def __init__(self, nc, stack):
        self.nc = nc
        self.stack = stack
        self.eng = {}
        for name, h in (("pe", nc.tensor), ("act", nc.scalar), ("dve", nc.vector),
                        ("pool", nc.gpsimd), ("sp", nc.sync)):
            self.eng[name] = Eng(name, h, stack.enter_context(nc.semaphore("s_" + name)))
        self.nsem = 0
        self.snaps = {}

    def dsem(self):
        self.nsem += 1
        return DSem(self.stack.enter_context(self.nc.semaphore("d%d" % self.nsem)))

    def _waits(self, en, reads, writes, attach_ok=False):
        e = self.eng[en]
        deps = {}

        def need(tok):
            if tok is None:
                return
            k = id(tok[0])
            if k not in deps or deps[k][1] < tok[1]:
                deps[k] = tok

        for b in reads:
            need(b.w)
            if b.excl:
                for t in b.r.values():
                    need(t)
        for b in writes:
            need(b.w)
            for t in b.r.values():
                need(t)
        todo = []
        for k, (sem, val) in deps.items():
            if en == "pe" and sem is e.sem:
                continue
            if e.known.get(k, 0) >= val:
                continue
            e.known[k] = val
            todo.append((sem, val))
            snap = self.snaps.get((k, val))
            if snap is not None:
                for k2, v2 in snap.items():
                    if e.known.get(k2, 0) < v2:
                        e.known[k2] = v2
        self.attach = None
        if en in ("pe", "act", "dve") and attach_ok and todo:
            self.attach = todo.pop()
        for sem, val in todo:
            e.h.wait_ge(sem, val)
        return e

    def _mark(self, tok, reads, writes):
        for b in reads:
            if b.excl:
                b.w = tok
                b.r = {}
            else:
                b.r[id(tok[0])] = tok
        for b in writes:
            b.w = tok
            b.r = {}

    def op(self, en, fn, reads=(), writes=()):
        e = self._waits(en, reads, writes, attach_ok=True)
        if en == "pe":
            ins = fn(PEProxy(e.h, self.attach))
        else:
            ins = fn(e.h)
            if self.attach is not None:
                ins._wait_ge(self.attach[0], self.attach[1])
        e.n += 1
        ins.then_inc(e.sem, 1)
        self.snaps[(id(e.sem), e.n)] = dict(e.known)
        self._mark((e.sem, e.n), reads, writes)

    def dma(self, qn, ds, out_ap, in_ap, reads=(), writes=()):
        e = self._waits(qn, reads, writes)
        e.h.dma_start(out=out_ap, in_=in_ap).then_inc(ds.sem, 16)
        ds.n += 16
        self._mark((ds.sem, ds.n), reads, writes)


class BankPool:
    def __init__(self, items):
        self.items = items
        self.i = 0

    def next(self):
        it = self.items[self.i % len(self.items)]
        self.i += 1
        return it


def n_units():
    per_layer = 33 * 2 + 8 + 1 + 4 + 4 + 16
    return per_layer * L


NU = n_units()


def build(stages=None):
    plan = []
    nc = bass.Bass("TRN2", target_bir_lowering=False)
    xT_d = nc.dram_tensor("xT", [D, S], F32, kind="ExternalInput").ap()
    memT_d = nc.dram_tensor("memT", [D, NMEM], F32, kind="ExternalInput").ap()
    wst_d = nc.dram_tensor("wst", [NU, 128, SLOT], F32, kind="ExternalInput").ap()
    pv_d = nc.dram_tensor("pvec", [128, NPV], F32, kind="ExternalInput").ap()
    cst_d = nc.dram_tensor("cst", [128, NCST], F32, kind="ExternalInput").ap()
    oT_d = nc.dram_tensor("oT", [D, S], F32, kind="ExternalOutput").ap()
    xT_v = xT_d.rearrange("(c p) t -> p c t", p=128)
    memT_v = memT_d.rearrange("(c p) t -> p c t", p=128)
    oT_v = oT_d.rearrange("(c p) t -> p c t", p=128)

    with ExitStack() as st:
        sc = Sched(nc, st)
        _uid = [0]

        def uniq(name):
            _uid[0] += 1
            return "%s_%d" % (name, _uid[0])
        T = lambda name, shape, dt: st.enter_context(nc.sbuf_tensor(name, shape, dt))
        X = T("X", [128, 8, S], F32)
        H = T("H", [128, 8, S], BF16)
        RING = T("RING", [128, NSLOT, SLOT], BF16)
        PV = T("PV", [128, NPV], F32)
        CSTB = T("CSTB", [128, NCST], BF16)
        ID8 = T("ID8", [8, 8], F32)
        ONESM = T("ONESM", [128, 128], BF16)
        ONESM5 = T("ONESM5", [128, 128], BF16)
        ONES = T("ONES", [128, 128], BF16)
        ONESF5 = T("ONESF5", [128, 128], F32)
        EPSC = T("EPSC", [128, 1], F32)
        SQ = T("SQ", [128, 8, TT], BF16)
        RS = [T("RS%d" % i, [128, TT], F32) for i in range(4)]
        MN = T("MN", [128, 8, NMEM], BF16)
        PS = [st.enter_context(nc.psum_tensor("ps%d" % i, [128, TT], F32)) for i in range(8)]
        PSb = [Buf(excl=True) for _ in range(8)]
        bank = lambda i: (PS[i], PSb[i])

        Xb = [[Buf() for _ in range(NT)] for _ in range(8)]
        Hb = [[Buf() for _ in range(NT)] for _ in range(8)]
        SQb = Buf()
        RSb = [Buf() for _ in range(4)]
        mnb = [Buf() for _ in range(8)]
        cb = Buf()
        rs_rr = BankPool([0, 1, 2, 3])
        SG, SGb, sg_rr = RS, RSb, rs_rr
        ringb = [Buf() for _ in range(NSLOT)]
        ringd = [sc.dsem() for _ in range(NSLOT)]
        ring_i = [0]

        def wload(desc, nelem):
            u = len(plan)
            plan.append(desc)
            k = ring_i[0] % NSLOT
            ring_i[0] += 1
            if u == 3:
                nc.gpsimd.wait_ge(d_xs[NT - 1].sem, d_xs[NT - 1].n)
            if u == 6:
                sc.dma("pool", d_cst, CSTB[:, :], cst_d[:, :], writes=[cb2])
            sc.dma("pool", ringd[k], RING[:, k, 0:nelem], wst_d[u, :, 0:nelem], writes=[ringb[k]])
            return RING, k, ringb[k]

        def ring_take():
            k = ring_i[0] % NSLOT
            ring_i[0] += 1
            return k, ringb[k]

        def tsl(tt):
            return slice(tt * TT, (tt + 1) * TT)

        def fresh(*groups):
            toks = {}
            for fn_ in ("pe", "act", "dve", "pool"):
                f = sc.eng[fn_]
                if f.n > 0:
                    toks[id(f.sem)] = (f.sem, f.n)

            def walk(x):
                if isinstance(x, Buf):
                    x.r = dict(toks)
                else:
                    for y in x:
                        walk(y)
            walk(groups)

        def barrier():
            for en in ("pe", "act", "dve", "sp"):
                e = sc.eng[en]
                for fn_ in ("pe", "act", "dve"):
                    f = sc.eng[fn_]
                    if f is e or f.n == 0 or e.known.get(id(f.sem), 0) >= f.n:
                        continue
                    e.known[id(f.sem)] = f.n
                    e.h.wait_ge(f.sem, f.n)

        d_in = sc.dsem()
        sc.dma("sp", d_in, PV[:, :], pv_d[:, :], writes=[cb])
        sc.dma("sp", d_in, ID8[:, :], cst_d[0:8, 0:8], writes=[cb])
        cb2 = Buf()
        d_cst = sc.dsem()
        d_xs = [sc.dsem() for _ in range(NT)]
        for tt in range(NT):
            sc.dma("sp", d_xs[tt], X[:, :, tt * TT:(tt + 1) * TT], xT_v[:, :, tt * TT:(tt + 1) * TT], writes=[Xb[c][tt] for c in range(8)])
        sc.op("dve", lambda v: v.memset(ONESM[:, :], 1.0 / 1024.0), writes=[cb])
        sc.op("dve", lambda v: v.memset(ONESM5[:, :], 1.0 / 512.0), writes=[cb])
        sc.op("dve", lambda v: v.memset(ONES[:, :], 1.0), writes=[cb])
        sc.op("dve", lambda v: v.memset(ONESF5[:, :], 1.0 / 512.0), writes=[cb])
        sc.op("dve", lambda v: v.memset(EPSC[:, :], EPS), writes=[cb])
        for en in ("pe", "act", "dve", "sp"):
            e = sc.eng[en]
            e.h.wait_ge(d_in.sem, d_in.n)
            e.h.wait_ge(sc.eng["dve"].sem, sc.eng["dve"].n)
            e.known[id(d_in.sem)] = d_in.n
            e.known[id(sc.eng["dve"].sem)] = sc.eng["dve"].n
        cb.w = None
        for c in range(8):
            for tt in range(NT):
                Xb[c][tt].w = (d_xs[tt].sem, d_xs[tt].n)
        IDB = CSTB[:, 0:128]
        maskneg = lambda d: CSTB[:, 128:256]
        EID = CSTB[:, 256:320]

        pStat = BankPool([7])

        def rms_parts(src, srcb, dst, dstb, nch, ncol, gcol, onesm):
            st_ = {}

            def p1():
                sc.op("act", lambda a: a.activation(out=SQ[:, 0:nch, 0:ncol], in_=src(None), func=AF.Square),
                      reads=srcb + [cb], writes=[SQb])

            def p2():
                ps, psb = bank(pStat.next())

                def mm(t):
                    for c in range(nch):
                        ins = t.matmul(ps[:, 0:ncol], onesm[:, :], SQ[:, c, 0:ncol], start=(c == 0), stop=(c == nch - 1))
                    return ins
                sc.op("pe", mm, reads=[SQb, cb], writes=[psb])
                i = rs_rr.next()
                st_["i"] = i
                sc.op("act", lambda a: a.activation(out=RS[i][:, 0:ncol], in_=ps[:, 0:ncol], func=AF.Ln, bias=EPSC[:, 0:1]),
                      reads=[psb, cb], writes=[RSb[i]])
                sc.op("act", lambda a: a.activation(out=RS[i][:, 0:ncol], in_=RS[i][:, 0:ncol], func=AF.Exp, scale=-0.5),
                      reads=[RSb[i]], writes=[RSb[i]])

            def p3():
                i = st_["i"]
                for c in range(nch):
                    sc.op("dve", lambda v, c=c: v.scalar_tensor_tensor(out=dst(c), in0=src(c), scalar=PV[:, gcol + c:gcol + c + 1],
                                                                        in1=RS[i][:, 0:ncol], op0=ALU.mult, op1=ALU.mult),
                          reads=[srcb[c], RSb[i], cb], writes=[dstb[c]])
            return [p1, p2, p3]

        def rms_tile(*a):
            for p in rms_parts(*a):
                p()

        def rms_X_tile(tt, gcol):
            rms_tile(lambda c: X[:, :, tsl(tt)] if c is None else X[:, c, tsl(tt)],
                     [Xb[c][tt] for c in range(8)],
                     lambda c: H[:, c, tsl(tt)], [Hb[c][tt] for c in range(8)], 8, TT, gcol, ONESM)

        def rmsnorm_XH(gcol):
            for tt in range(NT):
                rms_X_tile(tt, gcol)

        def ffn(l, which, post, mem_prologue=False):
            n_gate, n_up, n_down = FFN_NAMES[which]
            pG = BankPool([0, 1, 2, 3])
            pD = BankPool([4, 5, 6])
            with ExitStack() as sf:
                ACTT = sf.enter_context(nc.sbuf_tensor(uniq("ACTT"), [128, 6, S], BF16))
                actb = [[Buf() for _ in range(NT)] for _ in range(6)]
                fresh(actb)
                if mem_prologue:
                    MEMT = sf.enter_context(nc.sbuf_tensor(uniq("MEMT"), [128, 8, NMEM], F32))
                    memb = [Buf() for _ in range(8)]
                    fresh(memb)
                    dm = sc.dsem()
                    sc.dma("sp", dm, MEMT[:, :, :], memT_v[:, :, :], writes=memb)
                groups = [(0, 1), (2, 3, 4), (5, 6, 7), (8, 9, 10)]
                for grp in groups:
                    if mem_prologue and grp is groups[1]:
                        rms_tile(lambda c: MEMT[:, :, :] if c is None else MEMT[:, c, :], memb,
                                 lambda c: MN[:, c, :], mnb, 8, NMEM, pvcol(l, "mem_g"), ONESM)
                    sl = {}
                    for u in grp:
                        sl["g", u] = wload(("kxn", n_gate, l, u * 256, 256), 2048)
                        sl["u", u] = wload(("kxn", n_up, l, u * 256, 256), 2048)
                    for ui, u in enumerate(grp):
                        for j in range(2):
                            hl = ui * 2 + j
                            for tt in range(NT):
                                gps, gpb = bank(pG.next())
                                ups, upb = bank(pG.next())
                                for (ps, pb, key) in ((gps, gpb, "g"), (ups, upb, "u")):
                                    _, k_, rb = sl[key, u]

                                    def mm(t, ps=ps, k_=k_, j=j, tt=tt):
                                        for k in range(8):
                                            ins = t.matmul(ps[:, :], RING[:, k_, k * 256 + j * 128:k * 256 + j * 128 + 128],
                                                           H[:, k, tsl(tt)], start=(k == 0), stop=(k == 7))
                                        return ins
                                    sc.op("pe", mm, reads=[rb] + [Hb[k][tt] for k in range(8)], writes=[pb])
                                i = sg_rr.next()
                                sc.op("act", lambda a, i=i, gps=gps: a.activation(out=SG[i][:, :], in_=gps[:, :], func=AF.Silu),
                                      reads=[gpb], writes=[SGb[i]])
                                sc.op("dve", lambda v, i=i, ups=ups, hl=hl, tt=tt: v.tensor_tensor(
                                    out=ACTT[:, hl, tsl(tt)], in0=SG[i][:, :], in1=ups[:, :], op=ALU.mult),
                                    reads=[SGb[i], upb], writes=[actb[hl][tt]])
                    for u in grp:
                        sl["d", u] = wload(("rows", n_down, l, u * 256, 256), 2048)
                    nh = 2 * len(grp)
                    for tt in range(NT):
                        for oc in range(8):
                            dps, dpb = bank(pD.next())

                            def mm(t, dps=dps, tt=tt, oc=oc):
                                for hl in range(nh):
                                    k_ = sl["d", grp[hl // 2]][1]
                                    j = hl % 2
                                    ins = t.matmul(dps[:, :], RING[:, k_, j * 1024 + oc * 128:j * 1024 + oc * 128 + 128],
                                                   ACTT[:, hl, tsl(tt)], start=(hl == 0), stop=(hl == nh - 1))
                                return ins
                            sc.op("pe", mm, reads=[sl["d", u][2] for u in grp] + [actb[hl][tt] for hl in range(nh)], writes=[dpb])
                            sc.op("dve", lambda v, dps=dps, oc=oc, tt=tt: v.scalar_tensor_tensor(
                                out=X[:, oc, tsl(tt)], in0=dps[:, :], scalar=0.5, in1=X[:, oc, tsl(tt)], op0=ALU.mult, op1=ALU.add),
                                reads=[dpb], writes=[Xb[oc][tt]])
                        if grp is groups[-1]:
                            if tt > 0:
                                post(tt - 1)
                            if tt == NT - 1:
                                post(tt)

        def xattn(l, post):
            pA = BankPool([0, 1, 2])
            pB = BankPool([3, 4, 5, 6])
            pC = BankPool([7])
            with ExitStack() as s2:
                T2 = lambda name, shape, dt: s2.enter_context(nc.sbuf_tensor(uniq(name), shape, dt))
                KT = T2("KT", [128, 8, NMEM], BF16)
                VM = T2("VM", [128, 2, D], BF16)
                ktb = [Buf() for _ in range(8)]
                vmb = Buf()
                QT = [T2("QT%d" % i, [128, 8, TT], BF16) for i in range(2)]
                OT = [T2("OT%d" % i, [128, 8, TT], BF16) for i in range(2)]
                PT = [T2("XPT%d" % i, [128, TT], BF16) for i in range(4)]
                REC = [T2("XREC%d" % i, [128, TT], F32) for i in range(2)]
                qtb = [[Buf() for _ in range(8)] for _ in range(2)]
                otb = [[Buf() for _ in range(8)] for _ in range(2)]
                ptb = [Buf() for _ in range(4)]
                recb = [Buf() for _ in range(2)]
                fresh(ktb, vmb, qtb, otb, ptb, recb)
                for oc2 in range(4):
                    _, k_, rb = wload(("kxn", "xattn_w_kv", l, oc2 * 256, 256), 2048)
                    for jj in range(2):
                        oc = oc2 * 2 + jj
                        ps, pb = bank(pA.next())

                        def mm(t, ps=ps, k_=k_, jj=jj):
                            for k in range(8):
                                ins = t.matmul(ps[:, 0:NMEM], RING[:, k_, k * 256 + jj * 128:k * 256 + jj * 128 + 128],
                                               MN[:, k, :], start=(k == 0), stop=(k == 7))
                            return ins
                        sc.op("pe", mm, reads=[rb] + mnb, writes=[pb])
                        sc.op("act", lambda a, ps=ps, oc=oc: a.activation(out=KT[:, oc, :], in_=ps[:, 0:NMEM], func=AF.Copy),
                              reads=[pb], writes=[ktb[oc]])
                for u in range(4):
                    _, k_, rb = wload(("kxn", "xattn_w_kv", l, 1024 + u * 256, 256), 2048)
                    for mt in range(2):
                        ps, pb = bank(pA.next())

                        def mm(t, ps=ps, k_=k_, mt=mt):
                            for k in range(8):
                                ins = t.matmul(ps[:, 0:256], MN[:, k, mt * 128:(mt + 1) * 128],
                                               RING[:, k_, k * 256:(k + 1) * 256], start=(k == 0), stop=(k == 7))
                            return ins
                        sc.op("pe", mm, reads=[rb] + mnb, writes=[pb])
                        sc.op("dve", lambda v, ps=ps, mt=mt, u=u: v.tensor_copy(out=VM[:, mt, u * 256:(u + 1) * 256], in_=ps[:, 0:256]),
                              reads=[pb], writes=[vmb])
                wq = [wload(("kxn", "xattn_w_q", l, oc2 * 256, 256), 2048) for oc2 in range(4)]
                wo = [wload(("kxn", "xattn_w_o", l, oc2 * 256, 256), 2048) for oc2 in range(4)]

                def stageA(tt):
                    q = tt % 2
                    for oc in range(8):
                        ps, pb = bank(pA.next())
                        _, k_, rb = wq[oc // 2]

                        def mm(t):
                            for k in range(8):
                                ins = t.matmul(ps[:, :], RING[:, k_, k * 256 + (oc % 2) * 128:k * 256 + (oc % 2) * 128 + 128],
                                               H[:, k, tsl(tt)], start=(k == 0), stop=(k == 7))
                            return ins
                        sc.op("pe", mm, reads=[rb] + [Hb[k][tt] for k in range(8)], writes=[pb])
                        if oc % 2 == 0:
                            sc.op("act", lambda a: a.activation(out=QT[q][:, oc, :], in_=ps[:, :], func=AF.Copy),
                                  reads=[pb], writes=[qtb[q][oc]])
                        else:
                            sc.op("dve", lambda v: v.tensor_copy(out=QT[q][:, oc, :], in_=ps[:, :]),
                                  reads=[pb], writes=[qtb[q][oc]])

                def stageB(tt):
                    q = tt % 2

                    def s_part(h):
                        for mt in range(2):
                            ps, pb = bank(pB.next())

                            def mm(t):
                                for kc in range(2):
                                    ins = t.matmul(ps[:, :], KT[:, 2 * h + kc, mt * 128:(mt + 1) * 128], QT[q][:, 2 * h + kc, :],
                                                   start=(kc == 0), stop=(kc == 1))
                                return ins
                            sc.op("pe", mm, reads=[ktb[2 * h], ktb[2 * h + 1], qtb[q][2 * h], qtb[q][2 * h + 1]], writes=[pb])
                            pi = (h % 2) * 2 + mt
                            sc.op("act", lambda a: a.activation(out=PT[pi][:, :], in_=ps[:, :], func=AF.Exp, scale=1.0 / 16.0),
                                  reads=[pb], writes=[ptb[pi]])

                    def v_part(h):
                        pts = [(h % 2) * 2, (h % 2) * 2 + 1]
                        dps, dpb = bank(pC.next())

                        def mm(t):
                            for mt in range(2):
                                ins = t.matmul(dps[:, :], ONES[:, :], PT[pts[mt]][:, :], start=(mt == 0), stop=(mt == 1))
                            return ins
                        sc.op("pe", mm, reads=[ptb[p] for p in pts], writes=[dpb])
                        ri = h % 2
                        sc.op("act", lambda a: a.activation(out=REC[ri][:, :], in_=dps[:, :], func=AF.Ln), reads=[dpb], writes=[recb[ri]])
                        sc.op("act", lambda a: a.activation(out=REC[ri][:, :], in_=REC[ri][:, :], func=AF.Exp, scale=-1.0), reads=[recb[ri]], writes=[recb[ri]])
                        for dc in range(2):
                            ops_, opb = bank(pA.next())

                            def mm2(t):
                                for mt in range(2):
                                    ins = t.matmul(ops_[:, :], VM[:, mt, h * 256 + dc * 128:h * 256 + dc * 128 + 128], PT[pts[mt]][:, :],
                                                   start=(mt == 0), stop=(mt == 1))
                                return ins
                            sc.op("pe", mm2, reads=[vmb] + [ptb[p] for p in pts], writes=[opb])
                            sc.op("dve", lambda v: v.tensor_tensor(out=OT[q][:, 2 * h + dc, :], in0=ops_[:, :], in1=REC[ri][:, :], op=ALU.mult),
                                  reads=[opb, recb[ri]], writes=[otb[q][2 * h + dc]])
                    s_part(0)
                    for h in range(4):
                        if h + 1 < 4:
                            s_part(h + 1)
                        v_part(h)

                def stageC(tt):
                    q = tt % 2
                    for oc in range(8):
                        ps, pb = bank(pA.next())
                        _, k_, rb = wo[oc // 2]

                        def mm(t):
                            for k in range(8):
                                ins = t.matmul(ps[:, :], RING[:, k_, k * 256 + (oc % 2) * 128:k * 256 + (oc % 2) * 128 + 128],
                                               OT[q][:, k, :], start=(k == 0), stop=(k == 7))
                            return ins
                        sc.op("pe", mm, reads=[rb] + otb[q], writes=[pb])
                        sc.op("dve", lambda v: v.tensor_tensor(out=X[:, oc, tsl(tt)], in0=ps[:, :], in1=X[:, oc, tsl(tt)], op=ALU.add),
                              reads=[pb], writes=[Xb[oc][tt]])
                    if tt > 0:
                        post(tt - 1)
                    if tt == NT - 1:
                        post(tt)
                stageA(0)
                stageA(1)
                stageB(0)
                stageA(2)
                stageB(1)
                stageC(0)
                stageA(3)
                stageB(2)
                stageC(1)
                stageB(3)
                stageC(2)
                stageC(3)

        def mixer(l, post):
            with ExitStack() as s2:
                T2 = lambda name, shape, dt: s2.enter_context(nc.sbuf_tensor(uniq(name), shape, dt))
                YA = T2("YA", [128, 4, S], BF16)
                yab = [[Buf() for _ in range(NT)] for _ in range(4)]
                with ExitStack() as s3:
                    T3 = lambda name, shape, dt: s3.enter_context(nc.sbuf_tensor(uniq(name), shape, dt))
                    C8B = T3("C8B", [8, S], BF16)
                    NEGC = T3("NEGC", [128, 16 * 8], F32)
                    c8b_b = Buf()
                    negc_b = Buf()
                    pF = BankPool([0, 1])
                    with ExitStack() as s4:
                        T4 = lambda name, shape, dt: s4.enter_context(nc.sbuf_tensor(uniq(name), shape, dt))
                        FC = T4("FC", [8, S], F32)
                        ONES8 = T4("ONES8", [8, S], F32)
                        fcb = Buf()
                        o8b = Buf()
                        fresh(yab, c8b_b, negc_b, fcb, o8b)
                        sc.op("dve", lambda v: v.memset(ONES8[:, :], 1.0), writes=[o8b])
                        _, k_, rb = wload(("kxn", "w_in", l, 1536, 8), 64)
                        for tt in range(NT):
                            ps, pb = bank(pF.next())

                            def mm(t, ps=ps, tt=tt):
                                for k in range(8):
                                    ins = t.matmul(ps[0:8, :], RING[:, k_, k * 8:(k + 1) * 8], H[:, k, tsl(tt)], start=(k == 0), stop=(k == 7))
                                return ins
                            sc.op("pe", mm, reads=[rb] + [Hb[k][tt] for k in range(8)], writes=[pb])
                            bcol = pvcol(l, "b_f")
                            sc.op("act", lambda a, ps=ps, tt=tt: a.activation(out=FC[:, tsl(tt)], in_=ps[0:8, :], func=AF.Sigmoid,
                                                                             bias=PV[0:8, bcol:bcol + 1]),
                                  reads=[pb, cb], writes=[fcb])
                        sc.op("act", lambda a: a.activation(out=FC[:, :], in_=FC[:, :], func=AF.Ln), reads=[fcb], writes=[fcb])
                        sc.op("dve", lambda v: v.tensor_tensor_scan(out=FC[:, :], data0=ONES8[:, :], data1=FC[:, :], initial=0.0,
                                                                    op0=ALU.mult, op1=ALU.add), reads=[fcb, o8b], writes=[fcb])
                        sc.op("dve", lambda v: v.tensor_scalar(out=C8B[:, :], in0=FC[:, :], scalar1=8.0, scalar2=None, op0=ALU.mult),
                              reads=[fcb], writes=[c8b_b])
                        ps, pb = bank(pF.next())

                        def mmT(t):
                            for kb in range(16):
                                ins = t.transpose(ps[:, kb * 8:(kb + 1) * 8], FC[0:8, kb * 128:(kb + 1) * 128], ID8[0:8, 0:8])
                            return ins
                        sc.op("pe", mmT, reads=[fcb, cb], writes=[pb])
                        sc.op("dve", lambda v: v.tensor_scalar(out=NEGC[:, :], in0=ps[:, 0:128], scalar1=-1.0, scalar2=None, op0=ALU.mult),
                              reads=[pb], writes=[negc_b])
                    QA = T3("QA", [128, 2, S], BF16)
                    KA = T3("KA", [128, 2, S], BF16)
                    VP = T3("VP", [128, 16, 192], BF16)
                    PT = [T3("PT%d" % i, [128, TT], BF16) for i in range(4)]
                    qab = [[Buf() for _ in range(NT)] for _ in range(2)]
                    kab = [[Buf() for _ in range(NT)] for _ in range(2)]
                    qmb = [Buf() for _ in range(2)]
                    k1b = Buf()
                    vpb = [Buf() for _ in range(4)]
                    v1b = Buf()
                    ptb = [Buf() for _ in range(4)]
                    REC = RS
                    recb = RSb
                    dq = [sc.dsem() for _ in range(2)]
                    fresh(qab, kab, qmb, k1b, vpb, v1b, ptb)
                    sc.op("dve", lambda v: v.memset(KA[64:65, :, :], 1.0), writes=[k1b])
                    sc.op("dve", lambda v: v.memset(VP[:, :, 64:128], 1.0), writes=[v1b])
                    pP = BankPool([0, 1])
                    pS = BankPool([2, 3, 4, 5])
                    pACC = BankPool([6, 7])
                    pt_rr = BankPool([0, 1, 2, 3])
                    rec_rr = rs_rr
                    for hp in range(4):
                        _, kqk, rqk = wload(("kxn2", "w_in", l, hp * 128, 512 + hp * 128, 128), 2048)
                        _, kv, rv = wload(("kxn", "w_in", l, 1024 + hp * 128, 128), 1024)
                        def qkproj(tt):
                            for (dst, dstb, off) in ((QA, qab, 0), (KA, kab, 128)):
                                ps, pb = bank(pP.next())

                                def mm(t, ps=ps, off=off, tt=tt):
                                    for k in range(8):
                                        ins = t.matmul(ps[:, :], RING[:, kqk, k * 256 + off:k * 256 + off + 128], H[:, k, tsl(tt)],
                                                       start=(k == 0), stop=(k == 7))
                                    return ins
                                sc.op("pe", mm, reads=[rqk] + [Hb[k][tt] for k in range(8)], writes=[pb])
                                sc.op("dve", lambda v, ps=ps, dst=dst, tt=tt: v.tensor_copy(out=dst[0:64, 0, tsl(tt)], in_=ps[0:64, :]),
                                      reads=[pb], writes=[dstb[0][tt]])
                                sc.op("dve", lambda v, ps=ps, dst=dst, tt=tt: v.tensor_copy(out=dst[0:64, 1, tsl(tt)], in_=ps[64:128, :]),
                                      reads=[pb], writes=[dstb[1][tt]])
                        qkproj(0)
                        for hl in range(2):
                            h = 2 * hp + hl
                            sc.dma("sp", dq[hl], QA[64:65, hl, :], C8B[h:h + 1, :], reads=[c8b_b], writes=[qmb[hl]])

                        def vproj(tb4):
                            ps, pb = bank(pP.next())

                            def mm(t, ps=ps, tb4=tb4):
                                for q4 in range(4):
                                    tb = tb4 * 4 + q4
                                    for k in range(8):
                                        ins = t.matmul(ps[:, q4 * 128:(q4 + 1) * 128], H[:, k, tb * 128:(tb + 1) * 128],
                                                       RING[:, kv, k * 128:(k + 1) * 128], start=(k == 0), stop=(k == 7))
                                return ins
                            sc.op("pe", mm, reads=[rv] + [Hb[k][tb4] for k in range(8)], writes=[pb])
                            psv = ps[:, :].rearrange("p (a b) -> p a b", b=128)
                            sc.op("dve", lambda v, psv=psv, tb4=tb4: v.tensor_copy(out=VP[:, tb4 * 4:(tb4 + 1) * 4, 0:64], in_=psv[:, :, 0:64]),
                                  reads=[pb], writes=[vpb[tb4]])
                            sc.op("dve", lambda v, psv=psv, tb4=tb4: v.tensor_copy(out=VP[:, tb4 * 4:(tb4 + 1) * 4, 128:192], in_=psv[:, :, 64:128]),
                                  reads=[pb], writes=[vpb[tb4]])
                        items = []
                        for hl in range(2):
                            for T_ in range(NT):
                                for kb in range(4 * (T_ + 1)):
                                    items.append((hl, T_, kb))
                        LA = 3
                        state = {}

                        def stageS(it):
                            hl, T_, kb = it
                            h = 2 * hp + hl
                            d = kb - 4 * T_
                            c0 = 128 * d if d > 0 else 0
                            ps, pb = bank(pS.next())

                            def mm(t):
                                if d >= 0:
                                    t.matmul(ps[:, c0:c0 + 128], IDB, maskneg(0)[:, 0:128], start=True, stop=False, skip_group_check=True)
                                    return t.matmul(ps[:, c0:TT], KA[0:65, hl, kb * 128:(kb + 1) * 128],
                                                    QA[0:65, hl, T_ * TT + c0:(T_ + 1) * TT], start=False, stop=True, skip_group_check=True)
                                return t.matmul(ps[:, c0:TT], KA[0:65, hl, kb * 128:(kb + 1) * 128],
                                                QA[0:65, hl, T_ * TT + c0:(T_ + 1) * TT], start=True, stop=True)
                            sc.op("pe", mm, reads=[kab[hl][kb // 4], k1b, qab[hl][T_], qmb[hl], cb, cb2], writes=[pb])
                            pi = pt_rr.next()
                            sc.op("act", lambda a: a.activation(out=PT[pi][:, c0:TT], in_=ps[:, c0:TT], func=AF.Exp, scale=0.125,
                                                                bias=NEGC[:, kb * 8 + h:kb * 8 + h + 1]),
                                  reads=[pb, negc_b], writes=[ptb[pi]])
                            state[it] = (pi, c0)

                        def stageV(it):
                            hl, T_, kb = it
                            nkb = 4 * (T_ + 1)
                            pi, c0 = state.pop(it)
                            if kb == 0:
                                state["acc", hl, T_] = bank(pACC.next())
                            acc, accb = state["acc", hl, T_]
                            sc.op("pe", lambda t: t.matmul(acc[:, c0:TT], VP[:, kb, hl * 64:hl * 64 + 128], PT[pi][:, c0:TT],
                                                           start=(kb == 0), stop=(kb == nkb - 1)),
                                  reads=[vpb[kb // 4], v1b, ptb[pi]], writes=[accb])
                            if kb == nkb - 1:
                                ri = rec_rr.next()
                                num = slice(0, 64) if hl == 0 else slice(64, 128)
                                den = slice(64, 128) if hl == 0 else slice(0, 64)
                                sc.op("dve", lambda v: v.reciprocal(out=REC[ri][num, :], in_=acc[den, :]), reads=[accb], writes=[recb[ri]])
                                sc.op("dve", lambda v: v.tensor_tensor(out=YA[num, hp, tsl(T_)], in0=acc[num, :], in1=REC[ri][num, :], op=ALU.mult),
                                      reads=[accb, recb[ri]], writes=[yab[hp][T_]])
                                del state["acc", hl, T_]

                        for i in range(len(items) + LA):
                            if i < len(items):
                                stageS(items[i])
                            if i == 0:
                                qkproj(1)
                            if i == 1:
                                qkproj(2)
                            if i == LA - 1:
                                for tb4 in range(4):
                                    vproj(tb4)
                            if i == LA:
                                qkproj(3)
                            if i >= LA:
                                stageV(items[i - LA])
                def rmsA(tt):
                    rms_tile(lambda c: YA[:, :, tsl(tt)] if c is None else YA[:, c, tsl(tt)],
                             [yab[c][tt] for c in range(4)],
                             lambda c: YA[:, c, tsl(tt)], [yab[c][tt] for c in range(4)], 4, TT, pvcol(l, "attn_out_g"), ONESM5)
                YC = T2("YC", [128, 4, S], BF16)
                UBUF = [T2("UBUF%d" % i, [128, 2, 32 + TT], BF16) for i in range(2)]
                HALO = T2("HALO", [128, 4, 2, 30], BF16)
                CBS = [T2("CB%d" % i, [128, 4, TT], F32) for i in range(2)]
                ycb = [[Buf() for _ in range(NT)] for _ in range(4)]
                ubb = [Buf() for _ in range(2)]
                halob = [Buf() for _ in range(4)]
                cbbs = [[Buf() for _ in range(4)] for _ in range(2)]
                pA = BankPool([0, 1, 2, 3])
                pCv = BankPool([4, 5])
                pM = BankPool([6])
                fresh(ycb, ubb, halob, cbbs)
                sc.op("dve", lambda v: v.memset(HALO[:, :, :, :], 0.0), writes=halob)
                for b_ in range(2):
                    sc.op("dve", lambda v, b_=b_: v.memset(UBUF[b_][:, :, :], 0.0), writes=[ubb[b_]])
                wag = [wload(("kxn2", "w_in", l, 1544 + cc * 128, 2056 + cc * 128, 128), 2048) for cc in range(4)]
                dgs = [ring_take() for _ in range(2)]
                cw2 = pvcol(l, "conv_w2")
                cbias = pvcol(l, "conv_b")
                lg = pvcol(l, "ln_g")
                lb = pvcol(l, "ln_b")
                E3 = EID.unsqueeze(1).to_broadcast([128, 16, 64])

                def projA(it):
                    tt, cc = it // 4, it % 4
                    b_ = it % 2
                    _, k_, rb = wag[cc]
                    aps, apb = bank(pA.next())
                    gps, gpb = bank(pA.next())
                    for (ps, pb, off) in ((aps, apb, 0), (gps, gpb, 128)):
                        def mm(t, ps=ps, off=off):
                            for k in range(8):
                                ins = t.matmul(ps[:, :], RING[:, k_, k * 256 + off:k * 256 + off + 128], H[:, k, tsl(tt)],
                                               start=(k == 0), stop=(k == 7))
                            return ins
                        sc.op("pe", mm, reads=[rb] + [Hb[k][tt] for k in range(8)], writes=[pb])
                    i = sg_rr.next()
                    sc.op("act", lambda a: a.activation(out=SG[i][:, :], in_=gps[:, :], func=AF.Sigmoid), reads=[gpb], writes=[SGb[i]])
                    sc.op("dve", lambda v: v.tensor_copy(out=UBUF[b_][:, :, 0:30], in_=HALO[:, cc, :, :]), reads=[halob[cc]], writes=[ubb[b_]])
                    for h in range(2):
                        hs = slice(64 * h, 64 * h + 64)
                        sc.op("dve", lambda v, h=h, hs=hs: v.tensor_tensor(out=UBUF[b_][0:64, h, 30:30 + TT], in0=SG[i][hs, :], in1=aps[hs, :], op=ALU.mult),
                              reads=[SGb[i], apb], writes=[ubb[b_]])
                        sc.op("dve", lambda v, h=h, hs=hs: v.tensor_tensor(out=UBUF[b_][64:128, h, 29:29 + TT], in0=SG[i][hs, :], in1=aps[hs, :], op=ALU.mult),
                              reads=[SGb[i], apb], writes=[ubb[b_]])
                    kd, bd = dgs[b_]
                    for h in range(2):
                        w0 = cw2 + (cc * 2 + h) * 16
                        sc.op("pool", lambda v, h=h, w0=w0: v.tensor_tensor(
                            out=RING[:, kd, h * 1024:(h + 1) * 1024].rearrange("p (r c) -> p r c", c=64), in0=E3,
                            in1=PV[:, w0:w0 + 16].unsqueeze(2).to_broadcast([128, 16, 64]), op=ALU.mult),
                            reads=[cb, cb2], writes=[bd])

                def convB(it):
                    tt, cc = it // 4, it % 4
                    b_ = it % 2
                    kd, bd = dgs[b_]
                    cps, cpb = bank(pCv.next())

                    def mm(t):
                        for r in range(16):
                            for h in range(2):
                                ins = t.matmul(cps[64 * h:64 * h + 64, :], RING[:, kd, (h * 16 + r) * 64:(h * 16 + r) * 64 + 64],
                                               UBUF[b_][:, h, 2 * r:2 * r + TT], start=(r == 0), stop=(r == 15), tile_position=(0, 64 * h))
                        return ins
                    sc.op("pe", mm, reads=[bd, ubb[b_]], writes=[cpb])
                    CB = CBS[tt % 2]
                    cbb = cbbs[tt % 2]
                    sc.op("act", lambda a: a.activation(out=CB[:, cc, :], in_=cps[:, :], func=AF.Identity, bias=PV[:, cbias + cc:cbias + cc + 1]),
                          reads=[cpb, cb], writes=[cbb[cc]])
                    if tt < NT - 1:
                        sc.op("dve", lambda v: v.tensor_copy(out=HALO[:, cc, :, :], in_=UBUF[b_][:, :, TT:TT + 30]), reads=[ubb[b_]], writes=[halob[cc]])

                def lnC(tt):
                    CB = CBS[tt % 2]
                    cbb = cbbs[tt % 2]
                    st_ = {}
                    rp = rms_parts(lambda c: CB[:, :, :] if c is None else CB[:, c, :], cbb,
                                   lambda c: YC[:, c, tsl(tt)], [ycb[c][tt] for c in range(4)], 4, TT, pvcol(l, "conv_out_g"), ONESM5)

                    def s1():
                        mps, mpb = bank(pM.next())
                        st_["m"] = (mps, mpb)

                        def mmm(t):
                            for cc in range(4):
                                ins = t.matmul(mps[:, :], ONESF5[:, :], CB[:, cc, :], start=(cc == 0), stop=(cc == 3))
                            return ins
                        sc.op("pe", mmm, reads=cbb + [cb], writes=[mpb])

                    def s2():
                        mps, mpb = st_["m"]
                        for cc in range(4):
                            sc.op("dve", lambda v, cc=cc: v.tensor_tensor(out=CB[:, cc, :], in0=CB[:, cc, :], in1=mps[:, :], op=ALU.subtract),
                                  reads=[cbb[cc], mpb], writes=[cbb[cc]])
                        sc.op("act", lambda a: a.activation(out=SQ[:, 0:4, :], in_=CB[:, :, :], func=AF.Square), reads=cbb, writes=[SQb])

                    def s3():
                        vps, vpb_ = bank(pM.next())
                        st_["v"] = (vps, vpb_)

                        def mmv(t):
                            for cc in range(4):
                                ins = t.matmul(vps[:, :], ONESM5[:, :], SQ[:, cc, :], start=(cc == 0), stop=(cc == 3))
                            return ins
                        sc.op("pe", mmv, reads=[SQb, cb], writes=[vpb_])
                        sc.op("act", lambda a: a.activation(out=vps[:, :], in_=vps[:, :], func=AF.Ln, bias=EPSC[:, 0:1]),
                              reads=[vpb_, cb], writes=[vpb_])
                        sc.op("act", lambda a: a.activation(out=vps[:, :], in_=vps[:, :], func=AF.Exp, scale=-0.5), reads=[vpb_], writes=[vpb_])

                    def s4():
                        vps, vpb_ = st_["v"]
                        for cc in range(4):
                            sc.op("dve", lambda v, cc=cc: v.scalar_tensor_tensor(out=CB[:, cc, :], in0=CB[:, cc, :], scalar=PV[:, lg + cc:lg + cc + 1],
                                                                                  in1=vps[:, :], op0=ALU.mult, op1=ALU.mult),
                                  reads=[cbb[cc], vpb_, cb], writes=[cbb[cc]])
                            sc.op("act", lambda a, cc=cc: a.activation(out=CB[:, cc, :], in_=CB[:, cc, :], func=AF.Silu, bias=PV[:, lb + cc:lb + cc + 1]),
                                  reads=[cbb[cc], cb], writes=[cbb[cc]])
                        rp[0]()

                    def s5():
                        rp[1]()
                        rp[2]()
                    return [s1, s2, s3, s4, s5]

                NIT = 4 * NT
                pending = []
                wout = []
                projA(0)
                for it in range(NIT):
                    if it + 1 < NIT:
                        projA(it + 1)
                        if it + 1 >= NIT - 4:
                            wout.append(wload(("kxn", "w_out", l, (it + 1 - (NIT - 4)) * 256, 256), 2048))
                    convB(it)
                    if it % 4 == 3:
                        pending.append(lnC(it // 4))
                    for p in pending:
                        if p:
                            p.pop(0)()
                    if it % 4 == 1:
                        rmsA(it // 4)
                assert len(wout) == 4
                pO = BankPool([0, 1, 2, 3])

                def wout_tile(tt, do_post=True):
                    for oc in range(8):
                        ps, pb = bank(pO.next())
                        _, k_, rb = wout[oc // 2]

                        def mm(t):
                            for k in range(8):
                                rhs = YA[:, k, tsl(tt)] if k < 4 else YC[:, k - 4, tsl(tt)]
                                ins = t.matmul(ps[:, :], RING[:, k_, k * 256 + (oc % 2) * 128:k * 256 + (oc % 2) * 128 + 128], rhs,
                                               start=(k == 0), stop=(k == 7))
                            return ins
                        sc.op("pe", mm, reads=[rb] + [yab[c][tt] for c in range(4)] + [ycb[c][tt] for c in range(4)], writes=[pb])
                        sc.op("dve", lambda v: v.tensor_tensor(out=X[:, oc, tsl(tt)], in0=ps[:, :], in1=X[:, oc, tsl(tt)], op=ALU.add),
                              reads=[pb], writes=[Xb[oc][tt]])
                    if do_post:
                        if tt > 0:
                            post(tt - 1)
                        if tt == NT - 1:
                            post(tt)
                tdone = 0
                owed = []
                while any(pending):
                    nleft = max(len(p) for p in pending)
                    for p in pending:
                        if p:
                            p.pop(0)()
                    if tdone < NT - 1:
                        if nleft > 2:
                            wout_tile(tdone)
                        else:
                            wout_tile(tdone, do_post=False)
                            if tdone > 0:
                                owed.append(tdone - 1)
                        tdone += 1
                for tt in owed:
                    post(tt)
                for tt in range(tdone, NT):
                    wout_tile(tt)

        stg = stages if stages is not None else "f1,mix,xa,f2"
        stg = stg.split(",")
        gain = {"f1": "ffn1_g", "mix": "mix_g", "xa": "xattn_g", "f2": "ffn2_g", "f3": "ffn1_g"}
        calls = [(name, l) for l in range(L) for name in ("f1", "mix", "xa", "f2", "f3") if name in stg]
        rmsnorm_XH(pvcol(calls[0][1], gain[calls[0][0]]))

        def run(name, l, post):
            if name == "f1" or name == "f3":
                ffn(l, 1, post, mem_prologue=True)
            elif name == "f2":
                ffn(l, 2, post)
            elif name == "mix":
                mixer(l, post)
            elif name == "xa":
                xattn(l, post)
        for i, (name, l) in enumerate(calls[:-1]):
            nn, nl = calls[i + 1]
            run(name, l, lambda tt, g=pvcol(nl, gain[nn]): rms_X_tile(tt, g))
        with nc.sbuf_tensor("OB", [128, 8, TT], F32) as OB:
            obb = [Buf() for _ in range(8)]
            d_out = sc.dsem()
            fresh(obb)

            def final_tile(tt):
                rms_tile(lambda c: X[:, :, tsl(tt)] if c is None else X[:, c, tsl(tt)],
                         [Xb[c][tt] for c in range(8)],
                         lambda c: OB[:, c, :], obb, 8, TT, FINAL_G, ONESM)
                sc.dma("sp", d_out, oT_v[:, :, tsl(tt)], OB[:, :, :], reads=obb)
            run(calls[-1][0], calls[-1][1], final_tile)
            nc.sync.wait_ge(d_out.sem, d_out.n)
    return nc, plan


def _pack_units(inp, plan):
    out = np.zeros((len(plan), 128, SLOT), np.float32)
    for u, d in enumerate(plan):
        kind = d[0]
        if kind == "kxn":
            _, name, l, c0, n = d
            w = inp[name][l][:, c0:c0 + n]
            out[u, :, :8 * n] = w.reshape(8, 128, n).transpose(1, 0, 2).reshape(128, 8 * n)
        elif kind == "kxn2":
            _, name, l, c0, c1, n = d
            w = np.concatenate([inp[name][l][:, c0:c0 + n], inp[name][l][:, c1:c1 + n]], axis=1)
            out[u, :, :16 * n] = w.reshape(8, 128, 2 * n).transpose(1, 0, 2).reshape(128, 16 * n)
        elif kind == "rows":
            _, name, l, r0, n = d
            w = inp[name][l][r0:r0 + n, :]
            out[u, :, :] = w.reshape(2, 128, 1024).transpose(1, 0, 2).reshape(128, 2048)
    return out


def _pack_pvec(inp):
    pv = np.zeros((128, NPV), np.float32)
    c8 = lambda v: v.reshape(8, 128).T
    c4 = lambda v: v.reshape(4, 128).T
    for l in range(L):
        pv[:, pvcol(l, "ffn1_g"):pvcol(l, "ffn1_g") + 8] = c8(inp["ffn1_norm_g"][l])
        pv[:, pvcol(l, "mix_g"):pvcol(l, "mix_g") + 8] = c8(inp["mix_norm_g"][l])
        pv[:, pvcol(l, "xattn_g"):pvcol(l, "xattn_g") + 8] = c8(inp["xattn_norm_g"][l])
        pv[:, pvcol(l, "mem_g"):pvcol(l, "mem_g") + 8] = c8(inp["mem_norm_g"][l])
        pv[:, pvcol(l, "ffn2_g"):pvcol(l, "ffn2_g") + 8] = c8(inp["ffn2_norm_g"][l])
        cw = inp["conv_w"][l]
        cwp = np.concatenate([cw, np.zeros((1, cw.shape[1]), cw.dtype)], axis=0)
        pp = np.arange(128)
        for cc in range(4):
            for h in range(2):
                for r in range(16):
                    col = pvcol(l, "conv_w2") + (cc * 2 + h) * 16 + r
                    pv[:, col] = cwp[2 * r + pp // 64, cc * 128 + 64 * h + pp % 64]
        pv[:, pvcol(l, "conv_b"):pvcol(l, "conv_b") + 4] = c4(inp["conv_b"][l])
        pv[:, pvcol(l, "ln_g"):pvcol(l, "ln_g") + 4] = c4(inp["conv_ln_g"][l])
        pv[:, pvcol(l, "ln_b"):pvcol(l, "ln_b") + 4] = c4(inp["conv_ln_b"][l])
        pv[:, pvcol(l, "conv_out_g"):pvcol(l, "conv_out_g") + 4] = c4(inp["conv_out_g"][l])
        pv[:, pvcol(l, "attn_out_g"):pvcol(l, "attn_out_g") + 4] = c4(inp["attn_out_g"][l])
        pv[0:8, pvcol(l, "b_f")] = inp["b_f"][l]
    pv[:, FINAL_G:FINAL_G + 8] = c8(inp["final_norm_g"])
    return pv


def _consts():
    c = np.zeros((128, NCST), np.float32)
    c[:, 0:128] = np.eye(128, dtype=np.float32)
    p = np.arange(128)[:, None]
    f = np.arange(512)[None, :]
    c[:, 128:256] = np.where(f[:, 0:128] >= p, 0.0, MASKNEG)
    c[:, 256:320] = (np.arange(128)[:, None] % 64 == np.arange(64)[None, :]).astype(np.float32)
    return c


_CACHE = {}


def kernel(**inputs):
    inp = {k: np.asarray(v) for k, v in inputs.items()}
    stages = os.environ.get("MK_STAGES")
    key = stages
    if key not in _CACHE:
        _CACHE[key] = build(stages)
    nc, plan = _CACHE[key]
    assert len(plan) <= NU, (len(plan), NU)
    wst = np.zeros((NU, 128, SLOT), np.float32)
    wst[:len(plan)] = _pack_units(inp, plan)
    pv = _pack_pvec(inp)
    cst = _consts()
    B = inp["x"].shape[0]
    in_maps = []
    for b in range(B):
        in_maps.append({
            "xT": np.ascontiguousarray(inp["x"][b].T),
            "memT": np.ascontiguousarray(inp["mem"][b].T),
            "wst": wst, "pvec": pv, "cst": cst,
        })
    res = run_bass_kernel_spmd(nc, in_maps, core_ids=list(range(B)))
    out = np.stack([np.ascontiguousarray(r["oT"].T) for r in res.results], axis=0)
    return out.astype(np.float32)
```

```python
import os
from contextlib import ExitStack
import numpy as np
import concourse.bass as bass
import concourse.mybir as mybir
from concourse.bass_utils import run_bass_kernel_spmd

F32 = mybir.dt.float32
BF16 = mybir.dt.bfloat16
AF = mybir.ActivationFunctionType
ALU = mybir.AluOpType

S = 2048
D = 1024
DFF = 2816
NMEM = 256
L = 2
EPS = 1e-6
NSLOT = 8
SLOT = 2048
NT = 4
TT = 512
NCST = 128 + 128 + 64
MASKNEG = -30000.0

_PVL = [("ffn1_g", 8), ("mix_g", 8), ("xattn_g", 8), ("mem_g", 8), ("ffn2_g", 8), ("conv_w2", 128),
        ("conv_b", 4), ("ln_g", 4), ("ln_b", 4), ("conv_out_g", 4), ("attn_out_g", 4), ("b_f", 1)]
_PVO = {}
_o = 0
for _n, _w in _PVL:
    _PVO[_n] = _o
    _o += _w
PL = _o
NPV = L * PL + 8


def pvcol(l, name):
    return l * PL + _PVO[name]


FINAL_G = L * PL


FFN_NAMES = {1: ("ffn1_w_gate", "ffn1_w_up", "ffn1_w_down"), 2: ("ffn2_w_gate", "ffn2_w_up", "ffn2_w_down")}


class Buf:
    __slots__ = ("w", "r", "excl")

    def __init__(self, excl=False):
        self.w = None
        self.r = {}
        self.excl = excl


class DSem:
    def __init__(self, sem):
        self.sem = sem
        self.n = 0


class Eng:
    def __init__(self, name, h, sem):
        self.name = name
        self.h = h
        self.sem = sem
        self.n = 0
        self.known = {}


class PEProxy:
    def __init__(self, h, wait):
        self.h = h
        self.wait = wait

    def _first(self, ins):
        if self.wait is not None:
            ins._wait_ge(self.wait[0], self.wait[1])
            self.wait = None
        return ins

    def matmul(self, *a, **k):
        return self._first(self.h.matmul(*a, **k))

    def transpose(self, *a, **k):
        return self._first(self.h.transpose(*a, **k))


class Sched:
    def __init__(self, nc, stack):
        self.nc = nc
        self.stack = stack
        self.eng = {}
        for name, h in (("pe", nc.tensor), ("act", nc.scalar), ("dve", nc.vector),
                        ("pool", nc.gpsimd), ("sp", nc.sync)):
            self.eng[name] = Eng(name, h, stack.enter_context(nc.semaphore("s_" + name)))
        self.nsem = 0
        self.snaps = {}

    def dsem(self):
        self.nsem += 1
        return DSem(self.stack.enter_context(self.nc.semaphore("d%d" % self.nsem)))

    def _waits(self, en, reads, writes, attach_ok=False):
        e = self.eng[en]
        deps = {}

        def need(tok):
            if tok is None:
                return
            k = id(tok[0])
            if k not in deps or deps[k][1] < tok[1]:
                deps[k] = tok

        for b in reads:
            need(b.w)
            if b.excl:
                for t in b.r.values():
                    need(t)
        for b in writes:
            need(b.w)
            for t in b.r.values():
                need(t)
        todo = []
        for k, (sem, val) in deps.items():
            if en == "pe" and sem is e.sem:
                continue
            if e.known.get(k, 0) >= val:
                continue
            e.known[k] = val
            todo.append((sem, val))
            snap = self.snaps.get((k, val))
            if snap is not None:
                for k2, v2 in snap.items():
                    if e.known.get(k2, 0) < v2:
                        e.known[k2] = v2
        self.attach = None
        if en in ("pe", "act", "dve") and attach_ok and todo:
            self.attach = todo.pop()
        for sem, val in todo:
            e.h.wait_ge(sem, val)
        return e

    def _mark(self, tok, reads, writes):
        for b in reads:
            if b.excl:
                b.w = tok
                b.r = {}
            else:
                b.r[id(tok[0])] = tok
        for b in writes:
            b.w = tok
            b.r = {}

    def op(self, en, fn, reads=(), writes=()):
        e = self._waits(en, reads, writes, attach_ok=True)
        if en == "pe":
            ins = fn(PEProxy(e.h, self.attach))
        else:
            ins = fn(e.h)
            if self.attach is not None:
                ins._wait_ge(self.attach[0], self.attach[1])
        e.n += 1
        ins.then_inc(e.sem, 1)
        self.snaps[(id(e.sem), e.n)] = dict(e.known)
        self._mark((e.sem, e.n), reads, writes)

    def dma(self, qn, ds, out_ap, in_ap, reads=(), writes=()):
        e = self._waits(qn, reads, writes)
        e.h.dma_start(out=out_ap, in_=in_ap).then_inc(ds.sem, 16)
        ds.n += 16
        self._mark((ds.sem, ds.n), reads, writes)


class BankPool:
    def __init__(self, items):
        self.items = items
        self.i = 0

    def next(self):
        it = self.items[self.i % len(self.items)]
        self.i += 1
        return it


def n_units():
    per_layer = 33 * 2 + 8 + 1 + 4 + 4 + 16
    return per_layer * L


NU = n_units()


def build(stages=None):
    plan = []
    nc = bass.Bass("TRN2", target_bir_lowering=False)
    xT_d = nc.dram_tensor("xT", [D, S], F32, kind="ExternalInput").ap()
    memT_d = nc.dram_tensor("memT", [D, NMEM], F32, kind="ExternalInput").ap()
    wst_d = nc.dram_tensor("wst", [NU, 128, SLOT], F32, kind="ExternalInput").ap()
    pv_d = nc.dram_tensor("pvec", [128, NPV], F32, kind="ExternalInput").ap()
    cst_d = nc.dram_tensor("cst", [128, NCST], F32, kind="ExternalInput").ap()
    oT_d = nc.dram_tensor("oT", [D, S], F32, kind="ExternalOutput").ap()
    xT_v = xT_d.rearrange("(c p) t -> p c t", p=128)
    memT_v = memT_d.rearrange("(c p) t -> p c t", p=128)
    oT_v = oT_d.rearrange("(c p) t -> p c t", p=128)

    with ExitStack() as st:
        sc = Sched(nc, st)
        _uid = [0]

        def uniq(name):
            _uid[0] += 1
            return "%s_%d" % (name, _uid[0])
        T = lambda name, shape, dt: st.enter_context(nc.sbuf_tensor(name, shape, dt))
        X = T("X", [128, 8, S], F32)
        H = T("H", [128, 8, S], BF16)
        RING = T("RING", [128, NSLOT, SLOT], BF16)
        PV = T("PV", [128, NPV], F32)
        CSTB = T("CSTB", [128, NCST], BF16)
        ID8 = T("ID8", [8, 8], F32)
        ONESM = T("ONESM", [128, 128], BF16)
        ONESM5 = T("ONESM5", [128, 128], BF16)
        ONES = T("ONES", [128, 128], BF16)
        ONESF5 = T("ONESF5", [128, 128], F32)
        EPSC = T("EPSC", [128, 1], F32)
        SQ = T("SQ", [128, 8, TT], BF16)
        RS = [T("RS%d" % i, [128, TT], F32) for i in range(4)]
        MN = T("MN", [128, 8, NMEM], BF16)
        PS = [st.enter_context(nc.psum_tensor("ps%d" % i, [128, TT], F32)) for i in range(8)]
        PSb = [Buf(excl=True) for _ in range(8)]
        bank = lambda i: (PS[i], PSb[i])

        Xb = [[Buf() for _ in range(NT)] for _ in range(8)]
        Hb = [[Buf() for _ in range(NT)] for _ in range(8)]
        SQb = Buf()
        RSb = [Buf() for _ in range(4)]
        mnb = [Buf() for _ in range(8)]
        cb = Buf()
        rs_rr = BankPool([0, 1, 2, 3])
        SG, SGb, sg_rr = RS, RSb, rs_rr
        ringb = [Buf() for _ in range(NSLOT)]
        ringd = [sc.dsem() for _ in range(NSLOT)]
        ring_i = [0]

        def wload(desc, nelem):
            u = len(plan)
            plan.append(desc)
            k = ring_i[0] % NSLOT
            ring_i[0] += 1
            if u == 3:
                nc.gpsimd.wait_ge(d_xs[NT - 1].sem, d_xs[NT - 1].n)
            if u == 6:
                sc.dma("pool", d_cst, CSTB[:, :], cst_d[:, :], writes=[cb2])
            sc.dma("pool", ringd[k], RING[:, k, 0:nelem], wst_d[u, :, 0:nelem], writes=[ringb[k]])
            return RING, k, ringb[k]

        def ring_take():
            k = ring_i[0] % NSLOT
            ring_i[0] += 1
            return k, ringb[k]

        def tsl(tt):
            return slice(tt * TT, (tt + 1) * TT)

        def fresh(*groups):
            toks = {}
            for fn_ in ("pe", "act", "dve", "pool"):
                f = sc.eng[fn_]
                if f.n > 0:
                    toks[id(f.sem)] = (f.sem, f.n)

            def walk(x):
                if isinstance(x, Buf):
                    x.r = dict(toks)
                else:
                    for y in x:
                        walk(y)
            walk(groups)

        def barrier():
            for en in ("pe", "act", "dve", "sp"):
                e = sc.eng[en]
                for fn_ in ("pe", "act", "dve"):
                    f = sc.eng[fn_]
                    if f is e or f.n == 0 or e.known.get(id(f.sem), 0) >= f.n:
                        continue
                    e.known[id(f.sem)] = f.n
                    e.h.wait_ge(f.sem, f.n)

        d_in = sc.dsem()
        sc.dma("sp", d_in, PV[:, :], pv_d[:, :], writes=[cb])
        sc.dma("sp", d_in, ID8[:, :], cst_d[0:8, 0:8], writes=[cb])
        cb2 = Buf()
        d_cst = sc.dsem()
        d_xs = [sc.dsem() for _ in range(NT)]
        for tt in range(NT):
            sc.dma("sp", d_xs[tt], X[:, :, tt * TT:(tt + 1) * TT], xT_v[:, :, tt * TT:(tt + 1) * TT], writes=[Xb[c][tt] for c in range(8)])
        sc.op("dve", lambda v: v.memset(ONESM[:, :], 1.0 / 1024.0), writes=[cb])
        sc.op("dve", lambda v: v.memset(ONESM5[:, :], 1.0 / 512.0), writes=[cb])
        sc.op("dve", lambda v: v.memset(ONES[:, :], 1.0), writes=[cb])
        sc.op("dve", lambda v: v.memset(ONESF5[:, :], 1.0 / 512.0), writes=[cb])
        sc.op("dve", lambda v: v.memset(EPSC[:, :], EPS), writes=[cb])
        for en in ("pe", "act", "dve", "sp"):
            e = sc.eng[en]
            e.h.wait_ge(d_in.sem, d_in.n)
            e.h.wait_ge(sc.eng["dve"].sem, sc.eng["dve"].n)
            e.known[id(d_in.sem)] = d_in.n
            e.known[id(sc.eng["dve"].sem)] = sc.eng["dve"].n
        cb.w = None
        for c in range(8):
            for tt in range(NT):
                Xb[c][tt].w = (d_xs[tt].sem, d_xs[tt].n)
        IDB = CSTB[:, 0:128]
        maskneg = lambda d: CSTB[:, 128:256]
        EID = CSTB[:, 256:320]

        pStat = BankPool([7])

        def rms_parts(src, srcb, dst, dstb, nch, ncol, gcol, onesm):
            st_ = {}

            def p1():
                sc.op("act", lambda a: a.activation(out=SQ[:, 0:nch, 0:ncol], in_=src(None), func=AF.Square),
                      reads=srcb + [cb], writes=[SQb])

            def p2():
                ps, psb = bank(pStat.next())

                def mm(t):
                    for c in range(nch):
                        ins = t.matmul(ps[:, 0:ncol], onesm[:, :], SQ[:, c, 0:ncol], start=(c == 0), stop=(c == nch - 1))
                    return ins
                sc.op("pe", mm, reads=[SQb, cb], writes=[psb])
                i = rs_rr.next()
                st_["i"] = i
                sc.op("act", lambda a: a.activation(out=RS[i][:, 0:ncol], in_=ps[:, 0:ncol], func=AF.Ln, bias=EPSC[:, 0:1]),
                      reads=[psb, cb], writes=[RSb[i]])
                sc.op("act", lambda a: a.activation(out=RS[i][:, 0:ncol], in_=RS[i][:, 0:ncol], func=AF.Exp, scale=-0.5),
                      reads=[RSb[i]], writes=[RSb[i]])

            def p3():
                i = st_["i"]
                for c in range(nch):
                    sc.op("dve", lambda v, c=c: v.scalar_tensor_tensor(out=dst(c), in0=src(c), scalar=PV[:, gcol + c:gcol + c + 1],
                                                                        in1=RS[i][:, 0:ncol], op0=ALU.mult, op1=ALU.mult),
                          reads=[srcb[c], RSb[i], cb], writes=[dstb[c]])
            return [p1, p2, p3]

        def rms_tile(*a):
            for p in rms_parts(*a):
                p()

        def rms_X_tile(tt, gcol):
            rms_tile(lambda c: X[:, :, tsl(tt)] if c is None else X[:, c, tsl(tt)],
                     [Xb[c][tt] for c in range(8)],
                     lambda c: H[:, c, tsl(tt)], [Hb[c][tt] for c in range(8)], 8, TT, gcol, ONESM)

        def rmsnorm_XH(gcol):
            for tt in range(NT):
                rms_X_tile(tt, gcol)

        def ffn(l, which, post, mem_prologue=False):
            n_gate, n_up, n_down = FFN_NAMES[which]
            pG = BankPool([0, 1, 2, 3])
            pD = BankPool([4, 5, 6])
            with ExitStack() as sf:
                ACTT = sf.enter_context(nc.sbuf_tensor(uniq("ACTT"), [128, 8, S], BF16))
                actb = [[Buf() for _ in range(NT)] for _ in range(8)]
                fresh(actb)
                if mem_prologue:
                    MEMT = sf.enter_context(nc.sbuf_tensor(uniq("MEMT"), [128, 8, NMEM], F32))
                    memb = [Buf() for _ in range(8)]
                    fresh(memb)
                    dm = sc.dsem()
                    sc.dma("sp", dm, MEMT[:, :, :], memT_v[:, :, :], writes=memb)
                groups = [(0, 1, 2), (3, 4, 5, 6), (7, 8, 9, 10)]
                for grp in groups:
                    if mem_prologue and grp is groups[1]:
                        rms_tile(lambda c: MEMT[:, :, :] if c is None else MEMT[:, c, :], memb,
                                 lambda c: MN[:, c, :], mnb, 8, NMEM, pvcol(l, "mem_g"), ONESM)
                    sl = {}
                    for u in grp:
                        sl["g", u] = wload(("kxn", n_gate, l, u * 256, 256), 2048)
                        sl["u", u] = wload(("kxn", n_up, l, u * 256, 256), 2048)
                    for ui, u in enumerate(grp):
                        for j in range(2):
                            hl = ui * 2 + j
                            for tt in range(NT):
                                gps, gpb = bank(pG.next())
                                ups, upb = bank(pG.next())
                                for (ps, pb, key) in ((gps, gpb, "g"), (ups, upb, "u")):
                                    _, k_, rb = sl[key, u]

                                    def mm(t, ps=ps, k_=k_, j=j, tt=tt):
                                        for k in range(8):
                                            ins = t.matmul(ps[:, :], RING[:, k_, k * 256 + j * 128:k * 256 + j * 128 + 128],
                                                           H[:, k, tsl(tt)], start=(k == 0), stop=(k == 7))
                                        return ins
                                    sc.op("pe", mm, reads=[rb] + [Hb[k][tt] for k in range(8)], writes=[pb])
                                i = sg_rr.next()
                                sc.op("act", lambda a, i=i, gps=gps: a.activation(out=SG[i][:, :], in_=gps[:, :], func=AF.Silu),
                                      reads=[gpb], writes=[SGb[i]])
                                sc.op("dve", lambda v, i=i, ups=ups, hl=hl, tt=tt: v.tensor_tensor(
                                    out=ACTT[:, hl, tsl(tt)], in0=SG[i][:, :], in1=ups[:, :], op=ALU.mult),
                                    reads=[SGb[i], upb], writes=[actb[hl][tt]])
                    for u in grp:
                        sl["d", u] = wload(("rows", n_down, l, u * 256, 256), 2048)
                    nh = 2 * len(grp)
                    for tt in range(NT):
                        for oc in range(8):
                            dps, dpb = bank(pD.next())

                            def mm(t, dps=dps, tt=tt, oc=oc):
                                for hl in range(nh):
                                    k_ = sl["d", grp[hl // 2]][1]
                                    j = hl % 2
                                    ins = t.matmul(dps[:, :], RING[:, k_, j * 1024 + oc * 128:j * 1024 + oc * 128 + 128],
                                                   ACTT[:, hl, tsl(tt)], start=(hl == 0), stop=(hl == nh - 1))
                                return ins
                            sc.op("pe", mm, reads=[sl["d", u][2] for u in grp] + [actb[hl][tt] for hl in range(nh)], writes=[dpb])
                            sc.op("dve", lambda v, dps=dps, oc=oc, tt=tt: v.scalar_tensor_tensor(
                                out=X[:, oc, tsl(tt)], in0=dps[:, :], scalar=0.5, in1=X[:, oc, tsl(tt)], op0=ALU.mult, op1=ALU.add),
                                reads=[dpb], writes=[Xb[oc][tt]])
                        if grp is groups[-1]:
                            if tt > 0:
                                post(tt - 1)
                            if tt == NT - 1:
                                post(tt)

        def xattn(l, post):
            pA = BankPool([0, 1, 2])
            pB = BankPool([3, 4, 5, 6])
            pC = BankPool([7])
            with ExitStack() as s2:
                T2 = lambda name, shape, dt: s2.enter_context(nc.sbuf_tensor(uniq(name), shape, dt))
                KT = T2("KT", [128, 8, NMEM], BF16)
                VM = T2("VM", [128, 2, D], BF16)
                ktb = [Buf() for _ in range(8)]
                vmb = Buf()
                QT = [T2("QT%d" % i, [128, 8, TT], BF16) for i in range(2)]
                OT = [T2("OT%d" % i, [128, 8, TT], BF16) for i in range(2)]
                PT = [T2("XPT%d" % i, [128, TT], BF16) for i in range(4)]
                REC = [T2("XREC%d" % i, [128, TT], F32) for i in range(2)]
                qtb = [[Buf() for _ in range(8)] for _ in range(2)]
                otb = [[Buf() for _ in range(8)] for _ in range(2)]
                ptb = [Buf() for _ in range(4)]
                recb = [Buf() for _ in range(2)]
                fresh(ktb, vmb, qtb, otb, ptb, recb)
                for oc2 in range(4):
                    _, k_, rb = wload(("kxn", "xattn_w_kv", l, oc2 * 256, 256), 2048)
                    for jj in range(2):
                        oc = oc2 * 2 + jj
                        ps, pb = bank(pA.next())

                        def mm(t, ps=ps, k_=k_, jj=jj):
                            for k in range(8):
                                ins = t.matmul(ps[:, 0:NMEM], RING[:, k_, k * 256 + jj * 128:k * 256 + jj * 128 + 128],
                                               MN[:, k, :], start=(k == 0), stop=(k == 7))
                            return ins
                        sc.op("pe", mm, reads=[rb] + mnb, writes=[pb])
                        sc.op("act", lambda a, ps=ps, oc=oc: a.activation(out=KT[:, oc, :], in_=ps[:, 0:NMEM], func=AF.Copy),
                              reads=[pb], writes=[ktb[oc]])
                for u in range(4):
                    _, k_, rb = wload(("kxn", "xattn_w_kv", l, 1024 + u * 256, 256), 2048)
                    for mt in range(2):
                        ps, pb = bank(pA.next())

                        def mm(t, ps=ps, k_=k_, mt=mt):
                            for k in range(8):
                                ins = t.matmul(ps[:, 0:256], MN[:, k, mt * 128:(mt + 1) * 128],
                                               RING[:, k_, k * 256:(k + 1) * 256], start=(k == 0), stop=(k == 7))
                            return ins
                        sc.op("pe", mm, reads=[rb] + mnb, writes=[pb])
                        sc.op("dve", lambda v, ps=ps, mt=mt, u=u: v.tensor_copy(out=VM[:, mt, u * 256:(u + 1) * 256], in_=ps[:, 0:256]),
                              reads=[pb], writes=[vmb])
                wq = [wload(("kxn", "xattn_w_q", l, oc2 * 256, 256), 2048) for oc2 in range(4)]
                wo = [wload(("kxn", "xattn_w_o", l, oc2 * 256, 256), 2048) for oc2 in range(4)]

                def stageA(tt):
                    q = tt % 2
                    for oc in range(8):
                        ps, pb = bank(pA.next())
                        _, k_, rb = wq[oc // 2]

                        def mm(t):
                            for k in range(8):
                                ins = t.matmul(ps[:, :], RING[:, k_, k * 256 + (oc % 2) * 128:k * 256 + (oc % 2) * 128 + 128],
                                               H[:, k, tsl(tt)], start=(k == 0), stop=(k == 7))
                            return ins
                        sc.op("pe", mm, reads=[rb] + [Hb[k][tt] for k in range(8)], writes=[pb])
                        if oc % 2 == 0:
                            sc.op("act", lambda a: a.activation(out=QT[q][:, oc, :], in_=ps[:, :], func=AF.Copy),
                                  reads=[pb], writes=[qtb[q][oc]])
                        else:
                            sc.op("dve", lambda v: v.tensor_copy(out=QT[q][:, oc, :], in_=ps[:, :]),
                                  reads=[pb], writes=[qtb[q][oc]])

                def stageB(tt):
                    q = tt % 2

                    def s_part(h):
                        for mt in range(2):
                            ps, pb = bank(pB.next())

                            def mm(t):
                                for kc in range(2):
                                    ins = t.matmul(ps[:, :], KT[:, 2 * h + kc, mt * 128:(mt + 1) * 128], QT[q][:, 2 * h + kc, :],
                                                   start=(kc == 0), stop=(kc == 1))
                                return ins
                            sc.op("pe", mm, reads=[ktb[2 * h], ktb[2 * h + 1], qtb[q][2 * h], qtb[q][2 * h + 1]], writes=[pb])
                            pi = (h % 2) * 2 + mt
                            sc.op("act", lambda a: a.activation(out=PT[pi][:, :], in_=ps[:, :], func=AF.Exp, scale=1.0 / 16.0),
                                  reads=[pb], writes=[ptb[pi]])

                    def v_part(h):
                        pts = [(h % 2) * 2, (h % 2) * 2 + 1]
                        dps, dpb = bank(pC.next())

                        def mm(t):
                            for mt in range(2):
                                ins = t.matmul(dps[:, :], ONES[:, :], PT[pts[mt]][:, :], start=(mt == 0), stop=(mt == 1))
                            return ins
                        sc.op("pe", mm, reads=[ptb[p] for p in pts], writes=[dpb])
                        ri = h % 2
                        sc.op("act", lambda a: a.activation(out=REC[ri][:, :], in_=dps[:, :], func=AF.Ln), reads=[dpb], writes=[recb[ri]])
                        sc.op("act", lambda a: a.activation(out=REC[ri][:, :], in_=REC[ri][:, :], func=AF.Exp, scale=-1.0), reads=[recb[ri]], writes=[recb[ri]])
                        for dc in range(2):
                            ops_, opb = bank(pA.next())

                            def mm2(t):
                                for mt in range(2):
                                    ins = t.matmul(ops_[:, :], VM[:, mt, h * 256 + dc * 128:h * 256 + dc * 128 + 128], PT[pts[mt]][:, :],
                                                   start=(mt == 0), stop=(mt == 1))
                                return ins
                            sc.op("pe", mm2, reads=[vmb] + [ptb[p] for p in pts], writes=[opb])
                            sc.op("dve", lambda v: v.tensor_tensor(out=OT[q][:, 2 * h + dc, :], in0=ops_[:, :], in1=REC[ri][:, :], op=ALU.mult),
                                  reads=[opb, recb[ri]], writes=[otb[q][2 * h + dc]])
                    s_part(0)
                    for h in range(4):
                        if h + 1 < 4:
                            s_part(h + 1)
                        v_part(h)

                def stageC(tt):
                    q = tt % 2
                    for oc in range(8):
                        ps, pb = bank(pA.next())
                        _, k_, rb = wo[oc // 2]

                        def mm(t):
                            for k in range(8):
                                ins = t.matmul(ps[:, :], RING[:, k_, k * 256 + (oc % 2) * 128:k * 256 + (oc % 2) * 128 + 128],
                                               OT[q][:, k, :], start=(k == 0), stop=(k == 7))
                            return ins
                        sc.op("pe", mm, reads=[rb] + otb[q], writes=[pb])
                        sc.op("dve", lambda v: v.tensor_tensor(out=X[:, oc, tsl(tt)], in0=ps[:, :], in1=X[:, oc, tsl(tt)], op=ALU.add),
                              reads=[pb], writes=[Xb[oc][tt]])
                    if tt > 0:
                        post(tt - 1)
                    if tt == NT - 1:
                        post(tt)
                stageA(0)
                stageA(1)
                stageB(0)
                stageA(2)
                stageB(1)
                stageC(0)
                stageA(3)
                stageB(2)
                stageC(1)
                stageB(3)
                stageC(2)
                stageC(3)

        def mixer(l, post):
            with ExitStack() as s2:
                T2 = lambda name, shape, dt: s2.enter_context(nc.sbuf_tensor(uniq(name), shape, dt))
                YA = T2("YA", [128, 4, S], BF16)
                yab = [[Buf() for _ in range(NT)] for _ in range(4)]
                with ExitStack() as s3:
                    T3 = lambda name, shape, dt: s3.enter_context(nc.sbuf_tensor(uniq(name), shape, dt))
                    C8B = T3("C8B", [8, S], BF16)
                    NEGC = T3("NEGC", [128, 16 * 8], F32)
                    c8b_b = Buf()
                    negc_b = Buf()
                    pF = BankPool([0, 1])
                    with ExitStack() as s4:
                        T4 = lambda name, shape, dt: s4.enter_context(nc.sbuf_tensor(uniq(name), shape, dt))
                        FC = T4("FC", [8, S], F32)
                        ONES8 = T4("ONES8", [8, S], F32)
                        fcb = Buf()
                        o8b = Buf()
                        fresh(yab, c8b_b, negc_b, fcb, o8b)
                        sc.op("dve", lambda v: v.memset(ONES8[:, :], 1.0), writes=[o8b])
                        _, k_, rb = wload(("kxn", "w_in", l, 1536, 8), 64)
                        for tt in range(NT):
                            ps, pb = bank(pF.next())

                            def mm(t, ps=ps, tt=tt):
                                for k in range(8):
                                    ins = t.matmul(ps[0:8, :], RING[:, k_, k * 8:(k + 1) * 8], H[:, k, tsl(tt)], start=(k == 0), stop=(k == 7))
                                return ins
                            sc.op("pe", mm, reads=[rb] + [Hb[k][tt] for k in range(8)], writes=[pb])
                            bcol = pvcol(l, "b_f")
                            sc.op("act", lambda a, ps=ps, tt=tt: a.activation(out=FC[:, tsl(tt)], in_=ps[0:8, :], func=AF.Sigmoid,
                                                                             bias=PV[0:8, bcol:bcol + 1]),
                                  reads=[pb, cb], writes=[fcb])
                        sc.op("act", lambda a: a.activation(out=FC[:, :], in_=FC[:, :], func=AF.Ln), reads=[fcb], writes=[fcb])
                        sc.op("dve", lambda v: v.tensor_tensor_scan(out=FC[:, :], data0=ONES8[:, :], data1=FC[:, :], initial=0.0,
                                                                    op0=ALU.mult, op1=ALU.add), reads=[fcb, o8b], writes=[fcb])
                        sc.op("dve", lambda v: v.tensor_scalar(out=C8B[:, :], in0=FC[:, :], scalar1=8.0, scalar2=None, op0=ALU.mult),
                              reads=[fcb], writes=[c8b_b])
                        ps, pb = bank(pF.next())

                        def mmT(t):
                            for kb in range(16):
                                ins = t.transpose(ps[:, kb * 8:(kb + 1) * 8], FC[0:8, kb * 128:(kb + 1) * 128], ID8[0:8, 0:8])
                            return ins
                        sc.op("pe", mmT, reads=[fcb, cb], writes=[pb])
                        sc.op("dve", lambda v: v.tensor_scalar(out=NEGC[:, :], in0=ps[:, 0:128], scalar1=-1.0, scalar2=None, op0=ALU.mult),
                              reads=[pb], writes=[negc_b])
                    QA = T3("QA", [128, 2, S], BF16)
                    KA = T3("KA", [128, 2, S], BF16)
                    VP = T3("VP", [128, 16, 192], BF16)
                    PT = [T3("PT%d" % i, [128, TT], BF16) for i in range(4)]
                    qab = [[Buf() for _ in range(NT)] for _ in range(2)]
                    kab = [[Buf() for _ in range(NT)] for _ in range(2)]
                    qmb = [Buf() for _ in range(2)]
                    k1b = Buf()
                    vpb = [Buf() for _ in range(4)]
                    v1b = Buf()
                    ptb = [Buf() for _ in range(4)]
                    REC = RS
                    recb = RSb
                    dq = [sc.dsem() for _ in range(2)]
                    fresh(qab, kab, qmb, k1b, vpb, v1b, ptb)
                    sc.op("dve", lambda v: v.memset(KA[64:65, :, :], 1.0), writes=[k1b])
                    sc.op("dve", lambda v: v.memset(VP[:, :, 64:128], 1.0), writes=[v1b])
                    pP = BankPool([0, 1])
                    pS = BankPool([2, 3, 4, 5])
                    pACC = BankPool([6, 7])
                    pt_rr = BankPool([0, 1, 2, 3])
                    rec_rr = rs_rr
                    for hp in range(4):
                        _, kqk, rqk = wload(("kxn2", "w_in", l, hp * 128, 512 + hp * 128, 128), 2048)
                        _, kv, rv = wload(("kxn", "w_in", l, 1024 + hp * 128, 128), 1024)
                        for tt in range(NT):
                            for (dst, dstb, off) in ((QA, qab, 0), (KA, kab, 128)):
                                ps, pb = bank(pP.next())

                                def mm(t, ps=ps, off=off, tt=tt):
                                    for k in range(8):
                                        ins = t.matmul(ps[:, :], RING[:, kqk, k * 256 + off:k * 256 + off + 128], H[:, k, tsl(tt)],
                                                       start=(k == 0), stop=(k == 7))
                                    return ins
                                sc.op("pe", mm, reads=[rqk] + [Hb[k][tt] for k in range(8)], writes=[pb])
                                sc.op("dve", lambda v, ps=ps, dst=dst, tt=tt: v.tensor_copy(out=dst[0:64, 0, tsl(tt)], in_=ps[0:64, :]),
                                      reads=[pb], writes=[dstb[0][tt]])
                                sc.op("dve", lambda v, ps=ps, dst=dst, tt=tt: v.tensor_copy(out=dst[0:64, 1, tsl(tt)], in_=ps[64:128, :]),
                                      reads=[pb], writes=[dstb[1][tt]])
                        for hl in range(2):
                            h = 2 * hp + hl
                            sc.dma("sp", dq[hl], QA[64:65, hl, :], C8B[h:h + 1, :], reads=[c8b_b], writes=[qmb[hl]])
                        def vproj(tb4):
                            ps, pb = bank(pP.next())

                            def mm(t, ps=ps, tb4=tb4):
                                for q4 in range(4):
                                    tb = tb4 * 4 + q4
                                    for k in range(8):
                                        ins = t.matmul(ps[:, q4 * 128:(q4 + 1) * 128], H[:, k, tb * 128:(tb + 1) * 128],
                                                       RING[:, kv, k * 128:(k + 1) * 128], start=(k == 0), stop=(k == 7))
                                return ins
                            sc.op("pe", mm, reads=[rv] + [Hb[k][tb4] for k in range(8)], writes=[pb])
                            psv = ps[:, :].rearrange("p (a b) -> p a b", b=128)
                            sc.op("dve", lambda v, psv=psv, tb4=tb4: v.tensor_copy(out=VP[:, tb4 * 4:(tb4 + 1) * 4, 0:64], in_=psv[:, :, 0:64]),
                                  reads=[pb], writes=[vpb[tb4]])
                            sc.op("dve", lambda v, psv=psv, tb4=tb4: v.tensor_copy(out=VP[:, tb4 * 4:(tb4 + 1) * 4, 128:192], in_=psv[:, :, 64:128]),
                                  reads=[pb], writes=[vpb[tb4]])
                        items = []
                        for hl in range(2):
                            for T_ in range(NT):
                                for kb in range(4 * (T_ + 1)):
                                    items.append((hl, T_, kb))
                        LA = 3
                        state = {}

                        def stageS(it):
                            hl, T_, kb = it
                            h = 2 * hp + hl
                            d = kb - 4 * T_
                            c0 = 128 * d if d > 0 else 0
                            ps, pb = bank(pS.next())

                            def mm(t):
                                if d >= 0:
                                    t.matmul(ps[:, c0:c0 + 128], IDB, maskneg(0)[:, 0:128], start=True, stop=False, skip_group_check=True)
                                    return t.matmul(ps[:, c0:TT], KA[0:65, hl, kb * 128:(kb + 1) * 128],
                                                    QA[0:65, hl, T_ * TT + c0:(T_ + 1) * TT], start=False, stop=True, skip_group_check=True)
                                return t.matmul(ps[:, c0:TT], KA[0:65, hl, kb * 128:(kb + 1) * 128],
                                                QA[0:65, hl, T_ * TT + c0:(T_ + 1) * TT], start=True, stop=True)
                            sc.op("pe", mm, reads=[kab[hl][kb // 4], k1b, qab[hl][T_], qmb[hl], cb, cb2], writes=[pb])
                            pi = pt_rr.next()
                            sc.op("act", lambda a: a.activation(out=PT[pi][:, c0:TT], in_=ps[:, c0:TT], func=AF.Exp, scale=0.125,
                                                                bias=NEGC[:, kb * 8 + h:kb * 8 + h + 1]),
                                  reads=[pb, negc_b], writes=[ptb[pi]])
                            state[it] = (pi, c0)

                        def stageV(it):
                            hl, T_, kb = it
                            nkb = 4 * (T_ + 1)
                            pi, c0 = state.pop(it)
                            if kb == 0:
                                state["acc", hl, T_] = bank(pACC.next())
                            acc, accb = state["acc", hl, T_]
                            sc.op("pe", lambda t: t.matmul(acc[:, c0:TT], VP[:, kb, hl * 64:hl * 64 + 128], PT[pi][:, c0:TT],
                                                           start=(kb == 0), stop=(kb == nkb - 1)),
                                  reads=[vpb[kb // 4], v1b, ptb[pi]], writes=[accb])
                            if kb == nkb - 1:
                                ri = rec_rr.next()
                                num = slice(0, 64) if hl == 0 else slice(64, 128)
                                den = slice(64, 128) if hl == 0 else slice(0, 64)
                                sc.op("dve", lambda v: v.reciprocal(out=REC[ri][num, :], in_=acc[den, :]), reads=[accb], writes=[recb[ri]])
                                sc.op("dve", lambda v: v.tensor_tensor(out=YA[num, hp, tsl(T_)], in0=acc[num, :], in1=REC[ri][num, :], op=ALU.mult),
                                      reads=[accb, recb[ri]], writes=[yab[hp][T_]])
                                del state["acc", hl, T_]

                        for i in range(len(items) + LA):
                            if i < len(items):
                                stageS(items[i])
                            if i == LA - 1:
                                for tb4 in range(4):
                                    vproj(tb4)
                            if i >= LA:
                                stageV(items[i - LA])
                def rmsA(tt):
                    rms_tile(lambda c: YA[:, :, tsl(tt)] if c is None else YA[:, c, tsl(tt)],
                             [yab[c][tt] for c in range(4)],
                             lambda c: YA[:, c, tsl(tt)], [yab[c][tt] for c in range(4)], 4, TT, pvcol(l, "attn_out_g"), ONESM5)
                YC = T2("YC", [128, 4, S], BF16)
                UBUF = [T2("UBUF%d" % i, [128, 2, 32 + TT], BF16) for i in range(2)]
                HALO = T2("HALO", [128, 4, 2, 30], BF16)
                CBS = [T2("CB%d" % i, [128, 4, TT], F32) for i in range(2)]
                ycb = [[Buf() for _ in range(NT)] for _ in range(4)]
                ubb = [Buf() for _ in range(2)]
                halob = [Buf() for _ in range(4)]
                cbbs = [[Buf() for _ in range(4)] for _ in range(2)]
                pA = BankPool([0, 1, 2, 3])
                pCv = BankPool([4, 5])
                pM = BankPool([6])
                fresh(ycb, ubb, halob, cbbs)
                sc.op("dve", lambda v: v.memset(HALO[:, :, :, :], 0.0), writes=halob)
                for b_ in range(2):
                    sc.op("dve", lambda v, b_=b_: v.memset(UBUF[b_][:, :, :], 0.0), writes=[ubb[b_]])
                wag = [wload(("kxn2", "w_in", l, 1544 + cc * 128, 2056 + cc * 128, 128), 2048) for cc in range(4)]
                dgs = [ring_take() for _ in range(2)]
                cw2 = pvcol(l, "conv_w2")
                cbias = pvcol(l, "conv_b")
                lg = pvcol(l, "ln_g")
                lb = pvcol(l, "ln_b")
                E3 = EID.unsqueeze(1).to_broadcast([128, 16, 64])

                def projA(it):
                    tt, cc = it // 4, it % 4
                    b_ = it % 2
                    _, k_, rb = wag[cc]
                    aps, apb = bank(pA.next())
                    gps, gpb = bank(pA.next())
                    for (ps, pb, off) in ((aps, apb, 0), (gps, gpb, 128)):
                        def mm(t, ps=ps, off=off):
                            for k in range(8):
                                ins = t.matmul(ps[:, :], RING[:, k_, k * 256 + off:k * 256 + off + 128], H[:, k, tsl(tt)],
                                               start=(k == 0), stop=(k == 7))
                            return ins
                        sc.op("pe", mm, reads=[rb] + [Hb[k][tt] for k in range(8)], writes=[pb])
                    i = sg_rr.next()
                    sc.op("act", lambda a: a.activation(out=SG[i][:, :], in_=gps[:, :], func=AF.Sigmoid), reads=[gpb], writes=[SGb[i]])
                    sc.op("dve", lambda v: v.tensor_copy(out=UBUF[b_][:, :, 0:30], in_=HALO[:, cc, :, :]), reads=[halob[cc]], writes=[ubb[b_]])
                    for h in range(2):
                        hs = slice(64 * h, 64 * h + 64)
                        sc.op("dve", lambda v, h=h, hs=hs: v.tensor_tensor(out=UBUF[b_][0:64, h, 30:30 + TT], in0=SG[i][hs, :], in1=aps[hs, :], op=ALU.mult),
                              reads=[SGb[i], apb], writes=[ubb[b_]])
                        sc.op("dve", lambda v, h=h, hs=hs: v.tensor_tensor(out=UBUF[b_][64:128, h, 29:29 + TT], in0=SG[i][hs, :], in1=aps[hs, :], op=ALU.mult),
                              reads=[SGb[i], apb], writes=[ubb[b_]])
                    kd, bd = dgs[b_]
                    for h in range(2):
                        w0 = cw2 + (cc * 2 + h) * 16
                        sc.op("pool", lambda v, h=h, w0=w0: v.tensor_tensor(
                            out=RING[:, kd, h * 1024:(h + 1) * 1024].rearrange("p (r c) -> p r c", c=64), in0=E3,
                            in1=PV[:, w0:w0 + 16].unsqueeze(2).to_broadcast([128, 16, 64]), op=ALU.mult),
                            reads=[cb, cb2], writes=[bd])

                def convB(it):
                    tt, cc = it // 4, it % 4
                    b_ = it % 2
                    kd, bd = dgs[b_]
                    cps, cpb = bank(pCv.next())

                    def mm(t):
                        for r in range(16):
                            for h in range(2):
                                ins = t.matmul(cps[64 * h:64 * h + 64, :], RING[:, kd, (h * 16 + r) * 64:(h * 16 + r) * 64 + 64],
                                               UBUF[b_][:, h, 2 * r:2 * r + TT], start=(r == 0), stop=(r == 15), tile_position=(0, 64 * h))
                        return ins
                    sc.op("pe", mm, reads=[bd, ubb[b_]], writes=[cpb])
                    CB = CBS[tt % 2]
                    cbb = cbbs[tt % 2]
                    sc.op("act", lambda a: a.activation(out=CB[:, cc, :], in_=cps[:, :], func=AF.Identity, bias=PV[:, cbias + cc:cbias + cc + 1]),
                          reads=[cpb, cb], writes=[cbb[cc]])
                    if tt < NT - 1:
                        sc.op("dve", lambda v: v.tensor_copy(out=HALO[:, cc, :, :], in_=UBUF[b_][:, :, TT:TT + 30]), reads=[ubb[b_]], writes=[halob[cc]])

                def lnC(tt):
                    CB = CBS[tt % 2]
                    cbb = cbbs[tt % 2]
                    st_ = {}
                    rp = rms_parts(lambda c: CB[:, :, :] if c is None else CB[:, c, :], cbb,
                                   lambda c: YC[:, c, tsl(tt)], [ycb[c][tt] for c in range(4)], 4, TT, pvcol(l, "conv_out_g"), ONESM5)

                    def s1():
                        mps, mpb = bank(pM.next())
                        st_["m"] = (mps, mpb)

                        def mmm(t):
                            for cc in range(4):
                                ins = t.matmul(mps[:, :], ONESF5[:, :], CB[:, cc, :], start=(cc == 0), stop=(cc == 3))
                            return ins
                        sc.op("pe", mmm, reads=cbb + [cb], writes=[mpb])

                    def s2():
                        mps, mpb = st_["m"]
                        for cc in range(4):
                            sc.op("dve", lambda v, cc=cc: v.tensor_tensor(out=CB[:, cc, :], in0=CB[:, cc, :], in1=mps[:, :], op=ALU.subtract),
                                  reads=[cbb[cc], mpb], writes=[cbb[cc]])
                        sc.op("act", lambda a: a.activation(out=SQ[:, 0:4, :], in_=CB[:, :, :], func=AF.Square), reads=cbb, writes=[SQb])

                    def s3():
                        vps, vpb_ = bank(pM.next())
                        st_["v"] = (vps, vpb_)

                        def mmv(t):
                            for cc in range(4):
                                ins = t.matmul(vps[:, :], ONESM5[:, :], SQ[:, cc, :], start=(cc == 0), stop=(cc == 3))
                            return ins
                        sc.op("pe", mmv, reads=[SQb, cb], writes=[vpb_])
                        sc.op("act", lambda a: a.activation(out=vps[:, :], in_=vps[:, :], func=AF.Ln, bias=EPSC[:, 0:1]),
                              reads=[vpb_, cb], writes=[vpb_])
                        sc.op("act", lambda a: a.activation(out=vps[:, :], in_=vps[:, :], func=AF.Exp, scale=-0.5), reads=[vpb_], writes=[vpb_])

                    def s4():
                        vps, vpb_ = st_["v"]
                        for cc in range(4):
                            sc.op("dve", lambda v, cc=cc: v.scalar_tensor_tensor(out=CB[:, cc, :], in0=CB[:, cc, :], scalar=PV[:, lg + cc:lg + cc + 1],
                                                                                  in1=vps[:, :], op0=ALU.mult, op1=ALU.mult),
                                  reads=[cbb[cc], vpb_, cb], writes=[cbb[cc]])
                            sc.op("act", lambda a, cc=cc: a.activation(out=CB[:, cc, :], in_=CB[:, cc, :], func=AF.Silu, bias=PV[:, lb + cc:lb + cc + 1]),
                                  reads=[cbb[cc], cb], writes=[cbb[cc]])
                        rp[0]()

                    def s5():
                        rp[1]()
                        rp[2]()
                    return [s1, s2, s3, s4, s5]

                NIT = 4 * NT
                pending = []
                wout = []
                projA(0)
                for it in range(NIT):
                    if it + 1 < NIT:
                        projA(it + 1)
                        if it + 1 >= NIT - 4:
                            wout.append(wload(("kxn", "w_out", l, (it + 1 - (NIT - 4)) * 256, 256), 2048))
                    convB(it)
                    if it % 4 == 3:
                        pending.append(lnC(it // 4))
                    for p in pending:
                        if p:
                            p.pop(0)()
                    if it % 4 == 1:
                        rmsA(it // 4)
                assert len(wout) == 4
                pO = BankPool([0, 1, 2, 3])

                def wout_tile(tt, do_post=True):
                    for oc in range(8):
                        ps, pb = bank(pO.next())
                        _, k_, rb = wout[oc // 2]

                        def mm(t):
                            for k in range(8):
                                rhs = YA[:, k, tsl(tt)] if k < 4 else YC[:, k - 4, tsl(tt)]
                                ins = t.matmul(ps[:, :], RING[:, k_, k * 256 + (oc % 2) * 128:k * 256 + (oc % 2) * 128 + 128], rhs,
                                               start=(k == 0), stop=(k == 7))
                            return ins
                        sc.op("pe", mm, reads=[rb] + [yab[c][tt] for c in range(4)] + [ycb[c][tt] for c in range(4)], writes=[pb])
                        sc.op("dve", lambda v: v.tensor_tensor(out=X[:, oc, tsl(tt)], in0=ps[:, :], in1=X[:, oc, tsl(tt)], op=ALU.add),
                              reads=[pb], writes=[Xb[oc][tt]])
                    if do_post:
                        if tt > 0:
                            post(tt - 1)
                        if tt == NT - 1:
                            post(tt)
                tdone = 0
                owed = []
                while any(pending):
                    nleft = max(len(p) for p in pending)
                    for p in pending:
                        if p:
                            p.pop(0)()
                    if tdone < NT - 1:
                        if nleft > 2:
                            wout_tile(tdone)
                        else:
                            wout_tile(tdone, do_post=False)
                            if tdone > 0:
                                owed.append(tdone - 1)
                        tdone += 1
                for tt in owed:
                    post(tt)
                for tt in range(tdone, NT):
                    wout_tile(tt)

        stg = stages if stages is not None else "f1,mix,xa,f2"
        stg = stg.split(",")
        gain = {"f1": "ffn1_g", "mix": "mix_g", "xa": "xattn_g", "f2": "ffn2_g", "f3": "ffn1_g"}
        calls = [(name, l) for l in range(L) for name in ("f1", "mix", "xa", "f2", "f3") if name in stg]
        rmsnorm_XH(pvcol(calls[0][1], gain[calls[0][0]]))

        def run(name, l, post):
            if name == "f1" or name == "f3":
                ffn(l, 1, post, mem_prologue=True)
            elif name == "f2":
                ffn(l, 2, post)
            elif name == "mix":
                mixer(l, post)
            elif name == "xa":
                xattn(l, post)
        for i, (name, l) in enumerate(calls[:-1]):
            nn, nl = calls[i + 1]
            run(name, l, lambda tt, g=pvcol(nl, gain[nn]): rms_X_tile(tt, g))
        with nc.sbuf_tensor("OB", [128, 8, TT], F32) as OB:
            obb = [Buf() for _ in range(8)]
            d_out = sc.dsem()
            fresh(obb)

            def final_tile(tt):
                rms_tile(lambda c: X[:, :, tsl(tt)] if c is None else X[:, c, tsl(tt)],
                         [Xb[c][tt] for c in range(8)],
                         lambda c: OB[:, c, :], obb, 8, TT, FINAL_G, ONESM)
                sc.dma("sp", d_out, oT_v[:, :, tsl(tt)], OB[:, :, :], reads=obb)
            run(calls[-1][0], calls[-1][1], final_tile)
            nc.sync.wait_ge(d_out.sem, d_out.n)
    return nc, plan


def _pack_units(inp, plan):
    out = np.zeros((len(plan), 128, SLOT), np.float32)
    for u, d in enumerate(plan):
        kind = d[0]
        if kind == "kxn":
            _, name, l, c0, n = d
            w = inp[name][l][:, c0:c0 + n]
            out[u, :, :8 * n] = w.reshape(8, 128, n).transpose(1, 0, 2).reshape(128, 8 * n)
        elif kind == "kxn2":
            _, name, l, c0, c1, n = d
            w = np.concatenate([inp[name][l][:, c0:c0 + n], inp[name][l][:, c1:c1 + n]], axis=1)
            out[u, :, :16 * n] = w.reshape(8, 128, 2 * n).transpose(1, 0, 2).reshape(128, 16 * n)
        elif kind == "rows":
            _, name, l, r0, n = d
            w = inp[name][l][r0:r0 + n, :]
            out[u, :, :] = w.reshape(2, 128, 1024).transpose(1, 0, 2).reshape(128, 2048)
    return out


def _pack_pvec(inp):
    pv = np.zeros((128, NPV), np.float32)
    c8 = lambda v: v.reshape(8, 128).T
    c4 = lambda v: v.reshape(4, 128).T
    for l in range(L):
        pv[:, pvcol(l, "ffn1_g"):pvcol(l, "ffn1_g") + 8] = c8(inp["ffn1_norm_g"][l])
        pv[:, pvcol(l, "mix_g"):pvcol(l, "mix_g") + 8] = c8(inp["mix_norm_g"][l])
        pv[:, pvcol(l, "xattn_g"):pvcol(l, "xattn_g") + 8] = c8(inp["xattn_norm_g"][l])
        pv[:, pvcol(l, "mem_g"):pvcol(l, "mem_g") + 8] = c8(inp["mem_norm_g"][l])
        pv[:, pvcol(l, "ffn2_g"):pvcol(l, "ffn2_g") + 8] = c8(inp["ffn2_norm_g"][l])
        cw = inp["conv_w"][l]
        cwp = np.concatenate([cw, np.zeros((1, cw.shape[1]), cw.dtype)], axis=0)
        pp = np.arange(128)
        for cc in range(4):
            for h in range(2):
                for r in range(16):
                    col = pvcol(l, "conv_w2") + (cc * 2 + h) * 16 + r
                    pv[:, col] = cwp[2 * r + pp // 64, cc * 128 + 64 * h + pp % 64]
        pv[:, pvcol(l, "conv_b"):pvcol(l, "conv_b") + 4] = c4(inp["conv_b"][l])
        pv[:, pvcol(l, "ln_g"):pvcol(l, "ln_g") + 4] = c4(inp["conv_ln_g"][l])
        pv[:, pvcol(l, "ln_b"):pvcol(l, "ln_b") + 4] = c4(inp["conv_ln_b"][l])
        pv[:, pvcol(l, "conv_out_g"):pvcol(l, "conv_out_g") + 4] = c4(inp["conv_out_g"][l])
        pv[:, pvcol(l, "attn_out_g"):pvcol(l, "attn_out_g") + 4] = c4(inp["attn_out_g"][l])
        pv[0:8, pvcol(l, "b_f")] = inp["b_f"][l]
    pv[:, FINAL_G:FINAL_G + 8] = c8(inp["final_norm_g"])
    return pv


def _consts():
    c = np.zeros((128, NCST), np.float32)
    c[:, 0:128] = np.eye(128, dtype=np.float32)
    p = np.arange(128)[:, None]
    f = np.arange(512)[None, :]
    c[:, 128:256] = np.where(f[:, 0:128] >= p, 0.0, MASKNEG)
    c[:, 256:320] = (np.arange(128)[:, None] % 64 == np.arange(64)[None, :]).astype(np.float32)
    return c


_CACHE = {}


def kernel(**inputs):
    inp = {k: np.asarray(v) for k, v in inputs.items()}
    stages = os.environ.get("MK_STAGES")
    key = stages
    if key not in _CACHE:
        _CACHE[key] = build(stages)
    nc, plan = _CACHE[key]
    assert len(plan) <= NU, (len(plan), NU)
    wst = np.zeros((NU, 128, SLOT), np.float32)
    wst[:len(plan)] = _pack_units(inp, plan)
    pv = _pack_pvec(inp)
    cst = _consts()
    B = inp["x"].shape[0]
    in_maps = []
    for b in range(B):
        in_maps.append({
            "xT": np.ascontiguousarray(inp["x"][b].T),
            "memT": np.ascontiguousarray(inp["mem"][b].T),
            "wst": wst, "pvec": pv, "cst": cst,
        })
    res = run_bass_kernel_spmd(nc, in_maps, core_ids=list(range(B)))
    out = np.stack([np.ascontiguousarray(r["oT"].T) for r in res.results], axis=0)
    return out.astype(np.float32)
```
